# Optimizing a Trainium2 kernel written in Bass

```python
import math
import jax
import jax.numpy as jnp
from jax import lax
import numpy as np

D_MODEL = 1024
BATCH = 8
SEQ = 4096
DEPTH = 1

GRID_W = 64
CTX_LEN = 256
EPS = 1e-6
N_MOD = 6

RG_WIDTH = 512
RG_BLOCKS = 8
RG_BLOCK_DIM = RG_WIDTH // RG_BLOCKS
RG_CONV = 4
RG_C = 8.0

GDN_HEADS = 4
GDN_DK = 128
GDN_DV = 128
GDN_WIDTH = GDN_HEADS * GDN_DV
GDN_CONV = 4
GDN_CHUNK = 64

MIX_WIDTH = RG_WIDTH + GDN_WIDTH
SPLIT_POINTS = (RG_WIDTH, 2 * RG_WIDTH, 2 * RG_WIDTH + 3 * GDN_WIDTH, 2 * RG_WIDTH + 4 * GDN_WIDTH)
IN_COLS = 2 * RG_WIDTH + 4 * GDN_WIDTH + 2 * 2 * GDN_HEADS

PEER_HEADS = 8
PEER_NKEYS = 128
PEER_EXPERTS = PEER_NKEYS * PEER_NKEYS
PEER_QDIM = 256
PEER_HALF = PEER_QDIM // 2
PEER_TOPK = 16
PEER_BLOCK = 128

kernel_name = 'hybrid_rglru_gdn_peer_dit_block'


def _rmsnorm(x, g):
    xf = x.astype(jnp.float32)
    y = xf * lax.rsqrt(jnp.mean(xf * xf, axis=-1, keepdims=True) + EPS)
    return (y * g.astype(jnp.float32)).astype(x.dtype)


def _modulate(h, shift, scale):
    return h * (1 + scale) + shift


def _l2norm(t):
    return t * lax.rsqrt(jnp.sum(t * t, axis=-1, keepdims=True) + EPS)


def _dwconv_centred(x, w, b=None):
    K, C = w.shape
    left = K // 2
    y = lax.conv_general_dilated(x, w[:, None, :].astype(x.dtype), window_strides=(1,),
                                 padding=[(left, K - 1 - left)],
                                 dimension_numbers=('NWC', 'WIO', 'NWC'),
                                 feature_group_count=C)
    if b is not None:
        y = y + b.astype(x.dtype)
    return y


def _raster_to_colmajor(t, rows):
    B, L, C = t.shape
    return t.reshape(B, rows, GRID_W, C).transpose(0, 2, 1, 3).reshape(B, L, C)


def _colmajor_to_raster(t, rows):
    B, L, C = t.shape
    return t.reshape(B, GRID_W, rows, C).transpose(0, 2, 1, 3).reshape(B, L, C)


def _combine(left, right):
    a_l, b_l = left
    a_r, b_r = right
    return a_l * a_r, a_r * b_l + b_r


def _rglru_direction(xc, gate_w, gate_b, lam, h0):
    B, T, W = xc.shape
    xf = xc.astype(jnp.float32)
    xb = xf.reshape(B, T, RG_BLOCKS, RG_BLOCK_DIM)
    gates = jnp.einsum('btnd,gnde->gbtne', xb, gate_w.astype(jnp.float32)).reshape(2, B, T, W)
    gates = gates + gate_b.astype(jnp.float32)[:, None, None, :]
    r = jax.nn.sigmoid(gates[0])
    i = jax.nn.sigmoid(gates[1])
    log_a = -RG_C * r * jax.nn.softplus(-lam.astype(jnp.float32))
    a = jnp.exp(log_a)
    b = jnp.sqrt(-jnp.expm1(2.0 * log_a)) * (i * xf)
    b = b.at[:, 0].add(a[:, 0] * h0)
    _, h = lax.associative_scan(_combine, (a, b), axis=1)
    return h, h[:, -1]


def _rglru_mixer(u, gate, conv_w, conv_b, gate_w, gate_b, lam, h0_f, h0_b):
    xc = _dwconv_centred(u, conv_w, conv_b)
    h_f, s_f = _rglru_direction(xc, gate_w[0], gate_b[0], lam[0], h0_f)
    h_b, s_b = _rglru_direction(jnp.flip(xc, 1), gate_w[1], gate_b[1], lam[1], h0_b)
    y = (h_f + jnp.flip(h_b, 1)).astype(u.dtype) * jax.nn.gelu(gate)
    return y, s_f, s_b


def _gdn_chunked(q, k, v, g, beta, s0):
    B, H, T, _ = q.shape
    n = T // GDN_CHUNK
    C = GDN_CHUNK
    q, k, v = (t.reshape(B, H, n, C, t.shape[-1]) for t in (q, k, v))
    g = g.reshape(B, H, n, C)
    beta = beta.reshape(B, H, n, C)
    gc = jnp.cumsum(g, axis=-1)
    pos = jnp.arange(C)
    incl = pos[:, None] >= pos[None, :]
    strict = pos[:, None] > pos[None, :]
    diff = gc[..., :, None] - gc[..., None, :]
    decay = jnp.where(incl, jnp.exp(jnp.where(incl, diff, 0.0)), 0.0)
    kb = k * beta[..., None]
    a_strict = jnp.where(strict, jnp.einsum('bhncd,bhnsd->bhncs', kb, k) * decay, 0.0)
    rhs = jnp.concatenate([v * beta[..., None], kb * jnp.exp(gc)[..., None]], axis=-1)
    sol = lax.linalg.triangular_solve(a_strict, rhs, left_side=True, lower=True, unit_diagonal=True)
    w_val = sol[..., :GDN_DV]
    k_cum = sol[..., GDN_DV:]
    attn = jnp.where(incl, jnp.einsum('bhncd,bhnsd->bhncs', q, k) * decay, 0.0)
    xs = tuple(jnp.moveaxis(t, 2, 0) for t in (q, k, w_val, k_cum, attn, gc))

    def step(S, inp):
        qi, ki, wi, kci, ai, gi = inp
        v_new = wi - jnp.einsum('bhcd,bhde->bhce', kci, S)
        o = jnp.einsum('bhcd,bhde->bhce', qi * jnp.exp(gi)[..., None], S) + jnp.einsum('bhcs,bhse->bhce', ai, v_new)
        g_last = gi[..., -1]
        S = S * jnp.exp(g_last)[..., None, None] + jnp.einsum('bhcd,bhce->bhde', ki * jnp.exp(g_last[..., None] - gi)[..., None], v_new)
        return S, o

    s_fin, o = lax.scan(step, s0, xs)
    o = jnp.moveaxis(o, 0, 2).reshape(B, H, T, GDN_DV)
    return o, s_fin


def _gdn_mixer(qkv, z, ab, conv_w, a_log, dt_bias, norm_g, s0_f, s0_b):
    B, T, _ = qkv.shape
    f32 = jnp.float32
    qkv = jax.nn.silu(_dwconv_centred(qkv, conv_w)).astype(f32)
    q, k, v = jnp.split(qkv, 3, axis=-1)
    heads = lambda t: t.reshape(B, T, GDN_HEADS, -1).transpose(0, 2, 1, 3)
    q = _l2norm(heads(q)) * (GDN_DK ** -0.5)
    k = _l2norm(heads(k))
    v = heads(v)
    ab = ab.astype(f32).reshape(B, T, 2, 2, GDN_HEADS).transpose(2, 3, 0, 4, 1)
    g = -jnp.exp(a_log.astype(f32))[:, None, :, None] * jax.nn.softplus(ab[:, 0] + dt_bias.astype(f32)[:, None, :, None])
    beta = jax.nn.sigmoid(ab[:, 1])
    o_f, s_f = _gdn_chunked(q, k, v, g[0], beta[0], s0_f)
    fl = lambda t: jnp.flip(t, 2)
    o_b, s_b = _gdn_chunked(fl(q), fl(k), fl(v), fl(g[1]), fl(beta[1]), s0_b)
    o = (o_f + fl(o_b)).transpose(0, 2, 1, 3)
    o = _rmsnorm(o, norm_g) * jax.nn.silu(z.astype(f32).reshape(B, T, GDN_HEADS, GDN_DV))
    return o.reshape(B, T, GDN_WIDTH).astype(z.dtype), s_f, s_b


def _mix(p, rows, mix_params, states):
    rg_conv_w, rg_conv_b, rg_gate_w, rg_gate_b, rg_lambda, gdn_conv_w, gdn_a_log, gdn_dt_bias, gdn_norm_g = mix_params
    rg_h0_f, rg_h0_b, gdn_s0_f, gdn_s0_b = states
    rg_u, rg_gate, qkv, z, ab = jnp.split(p, list(SPLIT_POINTS), axis=-1)
    if rows is not None:
        qkv = _raster_to_colmajor(qkv, rows)
        z = _raster_to_colmajor(z, rows)
        ab = _raster_to_colmajor(ab, rows)
    y_rg, rg_f, rg_b = _rglru_mixer(rg_u, rg_gate, rg_conv_w, rg_conv_b, rg_gate_w, rg_gate_b, rg_lambda, rg_h0_f, rg_h0_b)
    y_gdn, s_f, s_b = _gdn_mixer(qkv, z, ab, gdn_conv_w, gdn_a_log, gdn_dt_bias, gdn_norm_g, gdn_s0_f, gdn_s0_b)
    if rows is not None:
        y_gdn = _colmajor_to_raster(y_gdn, rows)
    return jnp.concatenate([y_rg, y_gdn], axis=-1), (rg_f, rg_b, s_f, s_b)


def _peer(h, wq, keys, u, v):
    B, T, D = h.shape
    blocks = h.reshape(B * T // PEER_BLOCK, PEER_BLOCK, D)

    def one_block(hb):
        q = (hb @ wq).astype(jnp.float32).reshape(PEER_BLOCK, PEER_HEADS, 2, PEER_HALF)
        s = jnp.einsum('phxd,xkd->phxk', q, keys.astype(jnp.float32))
        s1, i1 = lax.top_k(s[:, :, 0], PEER_TOPK)
        s2, i2 = lax.top_k(s[:, :, 1], PEER_TOPK)
        cand_s = (s1[..., :, None] + s2[..., None, :]).reshape(PEER_BLOCK, PEER_HEADS, PEER_TOPK * PEER_TOPK)
        cand_i = (i1[..., :, None] * PEER_NKEYS + i2[..., None, :]).reshape(PEER_BLOCK, PEER_HEADS, PEER_TOPK * PEER_TOPK)
        top_s, pos = lax.top_k(cand_s, PEER_TOPK)
        idx = jnp.take_along_axis(cand_i, pos, axis=-1)
        gate = jax.nn.softmax(top_s, axis=-1).astype(hb.dtype)
        u_e = jnp.take(u, idx, axis=0)
        v_e = jnp.take(v, idx, axis=0)
        act = jax.nn.gelu(jnp.einsum('pd,phkd->phk', hb, u_e))
        return jnp.einsum('phk,phkd->pd', gate * act, v_e)

    return lax.map(one_block, blocks).reshape(B, T, D)


def setup_inputs(seed: int = 0) -> dict:
    key = jax.random.key(seed)
    ks = jax.random.split(key, 24)
    f32 = jnp.float32
    D = D_MODEL
    nrm = lambda k, shape, s: jax.random.normal(k, shape, f32) * s
    x = nrm(ks[0], (BATCH, SEQ, D), 1.0)
    c = nrm(ks[1], (BATCH, D), 1.0)
    ctx = nrm(ks[2], (BATCH, CTX_LEN, D), 1.0)
    c_ctx = nrm(ks[3], (D,), 1.0)
    w_mod = nrm(ks[4], (DEPTH, D, N_MOD * D), 0.5 * D ** -0.5)
    b_mod = nrm(ks[5], (DEPTH, N_MOD * D), 0.01)
    norm1_g = 1.0 + nrm(ks[6], (DEPTH, D), 0.01)
    norm2_g = 1.0 + nrm(ks[7], (DEPTH, D), 0.01)
    w_in = nrm(ks[8], (DEPTH, D, IN_COLS), D ** -0.5)
    rg_conv_w = nrm(ks[9], (DEPTH, RG_CONV, RG_WIDTH), RG_CONV ** -0.5)
    rg_conv_b = nrm(ks[10], (DEPTH, RG_WIDTH), 0.01)
    rg_gate_w = nrm(ks[11], (DEPTH, 2, 2, RG_BLOCKS, RG_BLOCK_DIM, RG_BLOCK_DIM), RG_BLOCK_DIM ** -0.5)
    rg_gate_b = nrm(ks[12], (DEPTH, 2, 2, RG_WIDTH), 0.01)
    a_pow = jax.random.uniform(ks[13], (DEPTH, 2, RG_WIDTH), f32, 0.9, 0.999)
    a0 = a_pow ** (1.0 / RG_C)
    rg_lambda = jnp.log(a0) - jnp.log1p(-a0)
    gdn_conv_w = nrm(ks[14], (DEPTH, GDN_CONV, 3 * GDN_WIDTH), GDN_CONV ** -0.5)
    gdn_a_log = jnp.log(jax.random.uniform(ks[15], (DEPTH, 2, GDN_HEADS), f32, 1.0, 16.0))
    dt = jnp.exp(jax.random.uniform(ks[16], (DEPTH, 2, GDN_HEADS), f32, math.log(1e-3), math.log(1e-1)))
    gdn_dt_bias = dt + jnp.log(-jnp.expm1(-dt))
    gdn_norm_g = 1.0 + nrm(ks[17], (DEPTH, GDN_DV), 0.01)
    w_out = nrm(ks[18], (DEPTH, MIX_WIDTH, D), MIX_WIDTH ** -0.5)
    peer_wq = nrm(ks[19], (DEPTH, D, PEER_HEADS * PEER_QDIM), D ** -0.5)
    peer_keys = nrm(ks[20], (DEPTH, 2, PEER_NKEYS, PEER_HALF), PEER_HALF ** -0.5)
    peer_u = nrm(ks[21], (DEPTH, PEER_EXPERTS, D), D ** -0.5)
    peer_v = nrm(ks[22], (DEPTH, PEER_EXPERTS, D), PEER_HEADS ** -0.5)
    final_g = 1.0 + nrm(ks[23], (D,), 0.01)
    return {'x': x, 'c': c, 'ctx': ctx, 'c_ctx': c_ctx, 'w_mod': w_mod, 'b_mod': b_mod,
            'norm1_g': norm1_g, 'norm2_g': norm2_g, 'w_in': w_in,
            'rg_conv_w': rg_conv_w, 'rg_conv_b': rg_conv_b, 'rg_gate_w': rg_gate_w,
            'rg_gate_b': rg_gate_b, 'rg_lambda': rg_lambda, 'gdn_conv_w': gdn_conv_w,
            'gdn_a_log': gdn_a_log, 'gdn_dt_bias': gdn_dt_bias, 'gdn_norm_g': gdn_norm_g,
            'w_out': w_out, 'peer_wq': peer_wq, 'peer_keys': peer_keys, 'peer_u': peer_u,
            'peer_v': peer_v, 'final_g': final_g}


def reference(x, c, ctx, c_ctx, w_mod, b_mod, norm1_g, norm2_g, w_in, rg_conv_w, rg_conv_b,
              rg_gate_w, rg_gate_b, rg_lambda, gdn_conv_w, gdn_a_log, gdn_dt_bias, gdn_norm_g,
              w_out, peer_wq, peer_keys, peer_u, peer_v, final_g):
    B, L, _ = x.shape
    rows = L // GRID_W
    zero_states = (jnp.zeros((B, RG_WIDTH), jnp.float32), jnp.zeros((B, RG_WIDTH), jnp.float32),
                   jnp.zeros((B, GDN_HEADS, GDN_DK, GDN_DV), jnp.float32),
                   jnp.zeros((B, GDN_HEADS, GDN_DK, GDN_DV), jnp.float32))
    for l in range(DEPTH):
        mod_lat = (jax.nn.silu(c) @ w_mod[l] + b_mod[l])[:, None, :]
        mod_ctx = (jax.nn.silu(c_ctx) @ w_mod[l] + b_mod[l])[None, None, :]
        sh1, sc1, gt1, sh2, sc2, gt2 = jnp.split(mod_lat, N_MOD, axis=-1)
        csh1, csc1, cgt1, csh2, csc2, cgt2 = jnp.split(mod_ctx, N_MOD, axis=-1)
        mix_params = (rg_conv_w[l], rg_conv_b[l], rg_gate_w[l], rg_gate_b[l], rg_lambda[l],
                      gdn_conv_w[l], gdn_a_log[l], gdn_dt_bias[l], gdn_norm_g[l])
        p_ctx = _modulate(_rmsnorm(ctx, norm1_g[l]), csh1, csc1) @ w_in[l]
        y_ctx, ctx_states = _mix(p_ctx, None, mix_params, zero_states)
        p_lat = _modulate(_rmsnorm(x, norm1_g[l]), sh1, sc1) @ w_in[l]
        y_lat, _ = _mix(p_lat, rows, mix_params, ctx_states)
        x = x + gt1 * (y_lat @ w_out[l])
        x = x + gt2 * _peer(_modulate(_rmsnorm(x, norm2_g[l]), sh2, sc2), peer_wq[l], peer_keys[l], peer_u[l], peer_v[l])
        if l < DEPTH - 1:
            ctx = ctx + cgt1 * (y_ctx @ w_out[l])
            ctx = ctx + cgt2 * _peer(_modulate(_rmsnorm(ctx, norm2_g[l]), csh2, csc2), peer_wq[l], peer_keys[l], peer_u[l], peer_v[l])
    return _rmsnorm(x, final_g)
```

```python
import contextlib
import numpy as np
import concourse.bass as bass
import concourse.mybir as mybir
from concourse.bass_utils import run_bass_kernel_spmd

F32 = mybir.dt.float32
U32 = mybir.dt.uint32
BF16 = mybir.dt.bfloat16
AF = mybir.ActivationFunctionType
ALU = mybir.AluOpType
AX = mybir.AxisListType

D = 1024
KD = 8
EPS = 1e-6
RGW = 512
GDW = 512
INC = 3088
NEXP = 16384
BIG = 30000.0
NEG = -1.0e30


class _Rec:
    def dma_start(self, **kw):
        self.kw = kw
        return self


class Prog:
    def __init__(self, nc, st):
        self.nc = nc
        self.st = st
        self.eng = {'pe': nc.tensor, 'act': nc.scalar, 'dve': nc.vector, 'pool': nc.gpsimd, 'sp': nc.sync}
        self.cnt = {e: 0 for e in self.eng}
        self.sem = {}
        for e in ('pe', 'act', 'dve', 'pool'):
            self.sem[e] = st.enter_context(nc.semaphore('sem_' + e))
        self.dcnt = {}
        self.seen = {e: {} for e in self.eng}
        self.lastw = {}
        self.readers = {}
        self.nins = 0
        self.pending = []

    def flush(self):
        p, self.pending = self.pending, []
        for a in p:
            self.dma(*a)

    def _deps(self, eng, reads, writes):
        if self.pending:
            rs, ws = set(reads), set(writes)
            for (_, _, _, pr, pw) in self.pending:
                if (set(pr) & ws) or (set(pw) & (rs | ws)):
                    self.flush()
                    break
        need = {}

        def add(t):
            if t is None:
                return
            k, v = t
            if need.get(k, 0) < v:
                need[k] = v
        for k in reads:
            add(self.lastw.get(k))
        for k in writes:
            add(self.lastw.get(k))
            for t in self.readers.get(k, {}).items():
                add(t)
        e = self.eng[eng]
        for k, v in need.items():
            if k == 'pe' and eng == 'pe':
                continue
            if self.seen[eng].get(k, 0) >= v:
                continue
            self.seen[eng][k] = v
            e.wait_ge(self.sem[k], v)

    def _commit(self, t, reads, writes):
        for k in reads:
            r = self.readers.setdefault(k, {})
            if r.get(t[0], 0) < t[1]:
                r[t[0]] = t[1]
        for k in writes:
            self.lastw[k] = t
            self.readers[k] = {}

    def op(self, eng, fn, reads=(), writes=()):
        self._deps(eng, reads, writes)
        ins = fn(self.eng[eng])
        self.cnt[eng] += 1
        ins.then_inc(self.sem[eng], 1)
        self._commit((eng, self.cnt[eng]), reads, writes)
        self.nins += 1

    def dma(self, q, slot, fn, reads=(), writes=(), defer=False):
        if defer:
            rec = _Rec()
            fn(rec)
            kw = rec.kw
            self.pending.append((q, slot, (lambda e, kw=kw: e.dma_start(**kw)), tuple(reads), tuple(writes)))
            return
        key = 'd_' + slot
        if key not in self.sem:
            self.sem[key] = self.st.enter_context(self.nc.semaphore(key))
            self.dcnt[key] = 0
        self._deps(q, reads, writes)
        e = self.eng[q]
        if self.dcnt[key] > 0 and self.seen[q].get(key, 0) < self.dcnt[key]:
            self.seen[q][key] = self.dcnt[key]
            e.wait_ge(self.sem[key], self.dcnt[key])
        ins = fn(e)
        self.dcnt[key] += 16
        ins.then_inc(self.sem[key], 16)
        self._commit((key, self.dcnt[key]), reads, writes)
        self.nins += 1

    def barrier(self):
        self.flush()
        for e in ('pe', 'act', 'dve', 'pool', 'sp'):
            eo = self.eng[e]
            for k in ('pe', 'act', 'dve', 'pool'):
                if k != e and self.cnt[k] > self.seen[e].get(k, 0):
                    self.seen[e][k] = self.cnt[k]
                    eo.wait_ge(self.sem[k], self.cnt[k])
            for k, v in self.dcnt.items():
                if v > self.seen[e].get(k, 0):
                    self.seen[e][k] = v
                    eo.wait_ge(self.sem[k], v)

    def finish(self):
        self.flush()
        eo = self.eng['sp']
        for k, v in self.dcnt.items():
            if v > self.seen['sp'].get(k, 0):
                self.seen['sp'][k] = v
                eo.wait_ge(self.sem[k], v)


class Ring:
    def __init__(self, nc, st, name, shape, dtype, n, psum=False):
        self.items = []
        for i in range(n):
            nm = '%s%d' % (name, i)
            t = st.enter_context((nc.psum_tensor if psum else nc.sbuf_tensor)(nm, shape, dtype))
            self.items.append((t, nm))
        self.i = 0

    def next(self):
        it = self.items[self.i % len(self.items)]
        self.i += 1
        return it


def build_nc(cfg):
    ROWS = cfg['rows']
    L = ROWS * 64
    CTX = cfg['ctx']
    GRP = cfg['grp']
    stop_after = cfg.get('stop_after', 99)
    gdn_cut = cfg.get('gdn_cut', 99)
    p3pool = cfg.get('p3pool', 'pool')
    TT = CTX + L
    NGL = L // GRP
    NTG = GRP // 128
    NCC = CTX // 128
    NCL = L // 128
    CPT = 128 // ROWS if ROWS < 128 else 1
    assert ROWS <= 128 and 128 % ROWS == 0 and GRP % 128 == 0 and L % GRP == 0 and CTX % 128 == 0 and CTX <= GRP

    nc = bass.Bass("TRN2", target_bir_lowering=False)

    def din(name, shape):
        return nc.dram_tensor(name, shape, F32, kind="ExternalInput").ap()
    x = din("x", [L, D])
    ctxx = din("ctxx", [CTX, D])
    cc = din("cc", [2, D])
    w_mod = din("w_mod", [D, 6 * D])
    b_mod = din("b_mod", [1, 6 * D])
    norm1_g = din("norm1_g", [1, D])
    norm2_g = din("norm2_g", [1, D])
    w_in = din("w_in", [D, INC])
    rg_conv_w = din("rg_conv_w", [4, RGW])
    rg_conv_b = din("rg_conv_b", [1, RGW])
    rg_gate_w = din("rg_gate_w", [2, 2, 8, 64, 64])
    rg_gate_b = din("rg_gate_b", [4, RGW])
    rg_lambda = din("rg_lambda", [2, RGW])
    gdn_conv_w = din("gdn_conv_w", [4, 3 * GDW])
    gdn_a_log = din("gdn_a_log", [1, 8])
    gdn_dt_bias = din("gdn_dt_bias", [1, 8])
    gdn_norm_g = din("gdn_norm_g", [1, 128])
    w_out = din("w_out", [D, D])
    peer_wq = din("peer_wq", [D, 2048])
    peer_keys = din("peer_keys", [2, 128, 128])
    peer_u = din("peer_u", [NEXP, D])
    peer_v = din("peer_v", [NEXP, D])
    final_g = din("final_g", [1, D])
    out = nc.dram_tensor("out", [L, D], F32, kind="ExternalOutput").ap()

    def dscr(name, shape):
        return nc.dram_tensor(name, shape, F32).ap()
    XC = dscr("s_xc", [RGW, TT])
    GG = dscr("s_gg", [RGW, L])
    HF = dscr("s_hf", [RGW, L])
    YTR = dscr("s_ytr", [RGW, L])
    QKV = dscr("s_qkv", [3 * GDW, TT])
    ZS = dscr("s_zs", [L, GDW])
    OF = dscr("s_of", [L, GDW])
    YG = dscr("s_yg", [L, GDW])
    X1S = dscr("s_x1", [L, D])
    H2S = dscr("s_h2", [L, D])
    QTS = dscr("s_qt", [2048, L])
    UV16 = nc.dram_tensor("s_uv16", [NEXP, 2 * D], BF16).ap()

    x_cm = x.rearrange("(r c) f -> c r f", c=64)
    yg_cm = YG.rearrange("(r c) f -> c r f", c=64)

    with contextlib.ExitStack() as st:
        P = Prog(nc, st)

        def sb(name, shape, dtype=F32, stack=None):
            return (stack or st).enter_context(nc.sbuf_tensor(name, shape, dtype))
        psr = Ring(nc, st, 'psb', [128, 512], F32, 6, psum=True)
        accA = st.enter_context(nc.psum_tensor('accA', [128, 512], F32))
        accB = st.enter_context(nc.psum_tensor('accB', [128, 512], F32))

        ident = sb('ident', [128, 128])
        ones = sb('ones', [128, 128])
        noti = sb('noti', [128, 128])
        Mm = [sb('Mf', [128, 128]), sb('Mb', [128, 128])]
        BGm = [sb('BGf', [128, 128]), sb('BGb', [128, 128])]
        bigs = sb('bigs', [128, 128])
        P.op('pool', lambda e: e.memset(ones[:], 1.0), writes=['ones'])
        P.op('pool', lambda e: e.memset(bigs[:], BIG), writes=['bigs'])
        P.op('pool', lambda e: e.memset(ident[:], 0.0), writes=['ident'])
        P.op('pool', lambda e: e.affine_select(out=ident[:], in_=ident[:], pattern=[[-1, 128]], compare_op=ALU.not_equal,
                                               fill=1.0, base=0, channel_multiplier=1), reads=['ident'], writes=['ident'])
        P.op('pool', lambda e: e.affine_select(out=noti[:], in_=ones[:], pattern=[[-1, 128]], compare_op=ALU.not_equal,
                                               fill=0.0, base=0, channel_multiplier=1), reads=['ones'], writes=['noti'])
        P.op('pool', lambda e: e.affine_select(out=Mm[0][:], in_=ones[:], pattern=[[1, 128]], compare_op=ALU.is_ge,
                                               fill=0.0, base=0, channel_multiplier=-1), reads=['ones'], writes=['Mf'])
        P.op('pool', lambda e: e.affine_select(out=Mm[1][:], in_=ones[:], pattern=[[-1, 128]], compare_op=ALU.is_ge,
                                               fill=0.0, base=0, channel_multiplier=1), reads=['ones'], writes=['Mb'])
        P.op('pool', lambda e: e.affine_select(out=BGm[0][:], in_=bigs[:], pattern=[[1, 128]], compare_op=ALU.is_gt,
                                               fill=0.0, base=0, channel_multiplier=-1), reads=['bigs'], writes=['BGf'])
        P.op('pool', lambda e: e.affine_select(out=BGm[1][:], in_=bigs[:], pattern=[[-1, 128]], compare_op=ALU.is_gt,
                                               fill=0.0, base=0, channel_multiplier=1), reads=['bigs'], writes=['BGb'])
        MK = ['Mf', 'Mb']
        BK = ['BGf', 'BGb']

        junk = sb('junk', [128, D])
        ssr = Ring(nc, st, 'ss', [128, 2], F32, 2)
        hr = Ring(nc, st, 'hh', [128, D], F32, 2)
        MOD = {n: sb('mod_' + n, [128, D]) for n in ['GT2']}
        FG = sb('FG', [128, D])
        st_b = contextlib.ExitStack()
        st_b.__enter__()
        for n in ['GT1', 'SH2', 'G2']:
            MOD[n] = sb('mod_' + n, [128, D], stack=st_b)
        st_a = contextlib.ExitStack()
        st_a.__enter__()
        NXR = 4
        xr = Ring(nc, st_a, 'xt', [128, D], F32, NXR)
        for n in ['SH1', 'G1', 'CSH1', 'CG1']:
            MOD[n] = sb('mod_' + n, [128, D], stack=st_a)
        GBt = sb('GBt', [128, (CTX + L) // 128, 2, 2, 4], stack=st_a)
        P.dma('sp', 'c0', lambda e: e.dma_start(out=FG[:], in_=final_g[0:1, :].partition_broadcast(128)), writes=['FG'])

        with contextlib.ExitStack() as ph:
            cc2 = sb('cc2', [2, D], stack=ph)
            sc2 = sb('sc2', [2, D], stack=ph)
            scT = sb('scT', [128, KD, 2], stack=ph)
            scB = sb('scB', [128, KD, 2, 128], stack=ph)
            bmod = sb('bmod', [1, 6 * D], stack=ph)
            ng = sb('ng', [128, D], stack=ph)
            wmr = Ring(nc, ph, 'wm', [128, KD, 512], F32, 2)
            P.dma('sp', 'c0', lambda e: e.dma_start(out=cc2[:], in_=cc[:, :]), writes=['cc2'])
            P.dma('sp', 'c1', lambda e: e.dma_start(out=bmod[:], in_=b_mod[:, :]), writes=['bmod'])
            P.op('act', lambda e: e.activation(out=sc2[:], in_=cc2[:], func=AF.Silu), reads=['cc2'], writes=['sc2'])
            pt, pk = psr.next()
            for kc in range(KD):
                P.op('pe', lambda e, kc=kc: e.transpose(out=pt[:, kc * 2:kc * 2 + 2], in_=sc2[0:2, kc * 128:(kc + 1) * 128],
                                                         identity=ident[0:2, 0:2]), reads=['sc2', 'ident'], writes=[pk])
            P.op('dve', lambda e: e.tensor_copy(out=scT[:].rearrange("p k r -> p (k r)"), in_=pt[:, 0:2 * KD]), reads=[pk], writes=['scT'])
            for r in range(2):
                P.op('dve', lambda e, r=r: e.tensor_copy(out=scB[:, :, r, :], in_=scT[:, :, r:r + 1].broadcast_to([128, KD, 128])),
                     reads=['scT'], writes=['scB'])
            wmv = w_mod.rearrange("(k p) n -> p k n", p=128)
            order = [('SH1', 'CSH1'), ('G1', 'CG1'), ('GT1', None), ('SH2', None), ('G2', None), ('GT2', None)]
            for n in range(12):
                wt, wk = wmr.next()
                P.dma('sp', 'wm%d' % (n % 2), lambda e, n=n, wt=wt: e.dma_start(out=wt[:], in_=wmv[:, :, n * 512:(n + 1) * 512]), writes=[wk])
                for r in range(2):
                    dst = order[n // 2][r]
                    if dst is None:
                        continue
                    pt, pk = psr.next()
                    for kc in range(KD):
                        P.op('pe', lambda e, kc=kc, r=r, wt=wt, pt=pt: e.matmul(pt[:], lhsT=scB[:, kc, r, :], rhs=wt[:, kc, :], start=(kc == 0), stop=False),
                             reads=['scB', wk], writes=[pk])
                    P.op('pe', lambda e, n=n, pt=pt: e.matmul(pt[:], lhsT=ones[0:1, :], rhs=bmod[0:1, n * 512:(n + 1) * 512], start=False, stop=True),
                         reads=['ones', 'bmod'], writes=[pk])
                    P.op('act', lambda e, dst=dst, n=n, pt=pt: e.copy(out=MOD[dst][:, (n % 2) * 512:(n % 2 + 1) * 512], in_=pt[:]),
                         reads=[pk], writes=['mod_' + dst])
            for gsrc, names in ((norm1_g, ('G1', 'CG1')), (norm2_g, ('G2',))):
                P.dma('sp', 'c0', lambda e, gsrc=gsrc: e.dma_start(out=ng[:], in_=gsrc[0:1, :].partition_broadcast(128)), writes=['ng'])
                for nm in names:
                    P.op('dve', lambda e, nm=nm: e.scalar_tensor_tensor(out=MOD[nm][:], in0=MOD[nm][:], scalar=1.0, in1=ng[:], op0=ALU.add, op1=ALU.mult),
                         reads=['mod_' + nm, 'ng'], writes=['mod_' + nm])
            P.barrier()


        def rms_rstd(src, skey, width, dstcol, dkey, np_=128):
            P.op('act', lambda e: e.activation(out=junk[0:np_, 0:width], in_=src, func=AF.Square, accum_out=dstcol),
                 reads=[skey], writes=['junk', dkey])
            P.op('act', lambda e: e.activation(out=dstcol, in_=dstcol, func=AF.Sqrt, scale=1.0 / width, bias=EPS),
                 reads=[dkey], writes=[dkey])
            P.op('dve', lambda e: e.reciprocal(out=dstcol, in_=dstcol), reads=[dkey], writes=[dkey])

        def norm_mod(xt, xk, gname, shname, np_=128):
            s_, sk = ssr.next()
            rms_rstd(xt[0:np_, :], xk, D, s_[0:np_, 0:1], sk, np_=np_)
            h, hk = hr.next()
            P.op('dve', lambda e: e.scalar_tensor_tensor(out=h[0:np_, :], in0=xt[0:np_, :], scalar=s_[0:np_, 0:1], in1=MOD[gname][0:np_, :],
                                                         op0=ALU.mult, op1=ALU.mult), reads=[xk, sk, 'mod_' + gname], writes=[hk])
            P.op('pool', lambda e: e.tensor_tensor(out=h[0:np_, :], in0=h[0:np_, :], in1=MOD[shname][0:np_, :], op=ALU.add),
                 reads=[hk, 'mod_' + shname], writes=[hk])
            return h, hk

        def transpose_to(h, hk, hT, hTk, tok0, np_=128, flip=0):
            for half in range(2):
                pt, pk = psr.next()
                for j in range(4):
                    kc = half * 4 + j
                    P.op('pe', lambda e, j=j, kc=kc, pt=pt: e.transpose(out=pt[:, j * np_:(j + 1) * np_], in_=h[0:np_, kc * 128:(kc + 1) * 128],
                                                                       identity=ident[0:np_, 0:np_]), reads=[hk, 'ident'], writes=[pk])
                eng = 'act' if (half + flip) % 2 == 0 else 'dve'
                src = pt[:, 0:4 * np_].rearrange("p (k t) -> p k t", k=4)
                dst = hT[:, half * 4:half * 4 + 4, tok0:tok0 + np_]
                if eng == 'act':
                    P.op('act', lambda e, src=src, dst=dst: e.copy(out=dst, in_=src), reads=[pk], writes=[hTk])
                else:
                    P.op('dve', lambda e, src=src, dst=dst: e.tensor_copy(out=dst, in_=src), reads=[pk], writes=[hTk])

        def gelu_inplace_g(buf, key, nchk, width, tmp, tkey, sq_eng='pool'):
            v = buf[:, 0:nchk, 0:width]
            t = tmp[:, 0:nchk, 0:width]
            P.op(sq_eng, lambda e: e.tensor_tensor(out=t, in0=v, in1=v, op=ALU.mult), reads=[key], writes=[tkey])
            P.op('dve', lambda e: e.tensor_scalar(out=t, in0=t, scalar1=0.044715, scalar2=1.0, op0=ALU.mult, op1=ALU.add), reads=[tkey], writes=[tkey])
            P.op('dve', lambda e: e.tensor_tensor(out=t, in0=t, in1=v, op=ALU.mult), reads=[tkey, key], writes=[tkey])
            P.op('act', lambda e: e.activation(out=t, in_=t, func=AF.Sigmoid, scale=1.5957691216), reads=[tkey], writes=[tkey])
            P.op('dve', lambda e: e.tensor_tensor(out=v, in0=v, in1=t, op=ALU.mult), reads=[tkey, key], writes=[key])


        def load_cols(dst, src2d, key, slot='c0'):
            P.dma('sp', slot, lambda e: e.dma_start(out=dst, in_=src2d.rearrange("n p -> p n"), allow_slow_non_contiguous=True), writes=[key])

        def seq_tile_loads(seq, order, i, xt, xk):
            if seq == 'ctx':
                P.dma('sp', 'x%d' % (xr.i % NXR), lambda e: e.dma_start(out=xt[:], in_=ctxx[i * 128:(i + 1) * 128, :]), writes=[xk])
            elif order == 'raster':
                P.dma('sp', 'x%d' % (xr.i % NXR), lambda e: e.dma_start(out=xt[:], in_=x[i * 128:(i + 1) * 128, :]), writes=[xk])
            else:
                ncol = 128 // ROWS
                for ci in range(ncol):
                    col = i * ncol + ci
                    P.dma('sp', 'x%d' % (xr.i % NXR), lambda e, ci=ci, col=col: e.dma_start(out=xt[ci * ROWS:(ci + 1) * ROWS, :], in_=x_cm[col]), writes=[xk])

        with contextlib.ExitStack() as ph:
            WIN = sb('WIN', [128, KD, 2064], stack=ph)
            hT = sb('hT', [128, KD, GRP], stack=ph)
            hTb = sb('hTb', [128, KD, 16], stack=ph)
            PTb = sb('PTb', [128, 12, 16], stack=ph)
            PT = sb('PT', [128, 4, GRP + 3], stack=ph)
            CV = sb('CV', [128, 4, GRP], stack=ph)
            SQ = sb('SQ', [128, 4, GRP], stack=ph)
            RS = sb('RS', [128, GRP], stack=ph)
            GEL = sb('GEL', [128, 4, GRP], stack=ph)
            HALO = sb('HALO', [128, 12, 2], stack=ph)
            cw = sb('cw', [128, 12, 4], stack=ph)
            cb = sb('cb', [128, 4], stack=ph)
            zt = sb('zt', [128, GDW], stack=ph)
            abc = sb('abc', [128, 3, 8], stack=ph)
            abt = sb('abt', [128, 16], stack=ph)
            xb = sb('xb', [16, D], stack=ph)
            w_in_v = w_in.rearrange("(k p) n -> p k n", p=128)

            def inproj_fm(c0col, nch, width, rhsT, rhsk, dst_fn, dkey):
                for c in range(nch):
                    pt, pk = psr.next()
                    for kc in range(KD):
                        P.op('pe', lambda e, c=c, kc=kc, pt=pt: e.matmul(pt[:, 0:width], lhsT=WIN[:, kc, c0col + c * 128:c0col + (c + 1) * 128],
                                                                           rhs=rhsT[:, kc, 0:width], start=(kc == 0), stop=(kc == KD - 1)),
                             reads=['WIN', rhsk], writes=[pk])
                    if c % 2 == 0:
                        P.op('act', lambda e, c=c, pt=pt: e.copy(out=dst_fn(c), in_=pt[:, 0:width]), reads=[pk], writes=[dkey])
                    else:
                        P.op('dve', lambda e, c=c, pt=pt: e.tensor_copy(out=dst_fn(c), in_=pt[:, 0:width]), reads=[pk], writes=[dkey])

            def boundary(tokens, c0col, nch):
                nb = len(tokens)
                if nb == 0:
                    return
                for i_, t in enumerate(tokens):
                    P.dma('sp', 'c0', lambda e, i_=i_, t=t: e.dma_start(out=xb[i_:i_ + 1, :], in_=x[t:t + 1, :]), writes=['xb'])
                h, hk = norm_mod(xb, 'xb', 'G1', 'SH1', np_=nb)
                transpose_to(h, hk, hTb, 'hTb', 0, np_=nb)
                inproj_fm(c0col, nch, nb, hTb, 'hTb', lambda c: PTb[:, c, 0:nb], 'PTb')

            def conv4(nchk, cwoff, bias, width):
                for c in range(nchk):
                    eng = 'dve' if c % 2 == 0 else 'pool'
                    if bias:
                        P.op(eng, lambda e, c=c: e.tensor_scalar(out=CV[:, c, 0:width], in0=PT[:, c, 0:width], scalar1=cw[:, cwoff + c, 0:1], scalar2=cb[:, c:c + 1],
                                                                 op0=ALU.mult, op1=ALU.add), reads=['PT', 'cw', 'cb'], writes=['CV%d' % c])
                    else:
                        P.op(eng, lambda e, c=c: e.tensor_scalar(out=CV[:, c, 0:width], in0=PT[:, c, 0:width], scalar1=cw[:, cwoff + c, 0:1], scalar2=None,
                                                                 op0=ALU.mult), reads=['PT', 'cw'], writes=['CV%d' % c])
                    for k in range(1, 4):
                        P.op('dve', lambda e, c=c, k=k: e.scalar_tensor_tensor(out=CV[:, c, 0:width], in0=PT[:, c, k:k + width], scalar=cw[:, cwoff + c, k:k + 1],
                                                                               in1=CV[:, c, 0:width], op0=ALU.mult, op1=ALU.add),
                             reads=['PT', 'cw', 'CV%d' % c], writes=['CV%d' % c])

            def gelu_inplace(buf, key, nchk, width, tmp, tkey):
                v = buf[:, 0:nchk, 0:width]
                t = tmp[:, 0:nchk, 0:width]
                P.op('pool', lambda e: e.tensor_tensor(out=t, in0=v, in1=v, op=ALU.mult), reads=[key], writes=[tkey])
                P.op('dve', lambda e: e.tensor_scalar(out=t, in0=t, scalar1=0.044715, scalar2=1.0, op0=ALU.mult, op1=ALU.add), reads=[tkey], writes=[tkey])
                P.op('dve', lambda e: e.tensor_tensor(out=t, in0=t, in1=v, op=ALU.mult), reads=[tkey, key], writes=[tkey])
                P.op('act', lambda e: e.activation(out=t, in_=t, func=AF.Sigmoid, scale=1.5957691216), reads=[tkey], writes=[tkey])
                P.op('dve', lambda e: e.tensor_tensor(out=v, in0=v, in1=t, op=ALU.mult), reads=[tkey, key], writes=[key])

            P.dma('sp', 'w0', lambda e: e.dma_start(out=WIN[:, :, 0:1024], in_=w_in_v[:, :, 0:1024]), writes=['WIN'])
            for k in range(4):
                load_cols(cw[:, 0:4, k], rg_conv_w[k:k + 1, :].rearrange("o (c p) -> (o c) p", p=128), 'cw')
            load_cols(cb[:, 0:4], rg_conv_b[0:1, :].rearrange("o (c p) -> (o c) p", p=128), 'cb')
            rg_bt = [g * GRP for g in range(1, NGL)]
            boundary(rg_bt, 0, 4)
            xc_v = XC.rearrange("(c p) t -> p c t", p=128)
            gg_v = GG.rearrange("(c p) t -> p c t", p=128)

            def p1_rg_group(seq, g):
                width = CTX if seq == 'ctx' else GRP
                ntile = width // 128
                seqoff = 0 if seq == 'ctx' else CTX
                t0 = g * GRP
                tl = []
                for i in range(ntile):
                    xt, xk = xr.next()
                    seq_tile_loads(seq, 'raster', g * NTG + i, xt, xk)
                    tl.append((xt, xk))
                P.flush()
                for i in range(ntile):
                    xt, xk = tl[i]
                    h, hk = norm_mod(xt, xk, 'CG1' if seq == 'ctx' else 'G1', 'CSH1' if seq == 'ctx' else 'SH1')
                    transpose_to(h, hk, hT, 'hT', i * 128, flip=i)
                if g == 0:
                    P.op('pool', lambda e: e.memset(PT[:, :, 0:2], 0.0), writes=['PT'])
                else:
                    P.op('pool', lambda e: e.tensor_copy(out=PT[:, :, 0:2], in_=PT[:, :, GRP:GRP + 2]), reads=['PT'], writes=['PT'])
                inproj_fm(0, 4, width, hT, 'hT', lambda c: PT[:, c, 2:2 + width], 'PT')
                if seq == 'ctx' or g == NGL - 1:
                    P.op('pool', lambda e: e.memset(PT[:, :, 2 + width:3 + width], 0.0), writes=['PT'])
                else:
                    P.op('pool', lambda e: e.tensor_copy(out=PT[:, :, 2 + width:3 + width], in_=PTb[:, 0:4, g:g + 1]), reads=['PTb'], writes=['PT'])
                conv4(4, 0, True, width)
                P.dma('sp', 'st0', lambda e: e.dma_start(out=xc_v[:, :, seqoff + t0:seqoff + t0 + width], in_=CV[:, :, 0:width]),
                      reads=['CV0', 'CV1', 'CV2', 'CV3'], writes=['XC'], defer=True)
                if seq == 'lat':
                    inproj_fm(512, 4, width, hT, 'hT', lambda c: SQ[:, c, 0:width], 'SQ')
                    gelu_inplace(SQ, 'SQ', 4, width, GEL, 'GEL')
                    P.dma('sp', 'st1', lambda e: e.dma_start(out=gg_v[:, :, t0:t0 + width], in_=SQ[:, :, 0:width]), reads=['SQ'], writes=['GG'], defer=True)

            def touch_cv():
                pass

            p1_rg_group('ctx', 0)
            for g in range(NGL):
                p1_rg_group('lat', g)

            if stop_after >= 2:
                P.dma('sp', 'w0', lambda e: e.dma_start(out=WIN[:, :, 0:2064], in_=w_in_v[:, :, 1024:3088]), reads=['WIN'], writes=['WIN'])
                for k in range(4):
                    load_cols(cw[:, 0:12, k], gdn_conv_w[k:k + 1, :].rearrange("o (c p) -> (o c) p", p=128), 'cw')
                P.dma('sp', 'c0', lambda e: e.dma_start(out=abc[:, 0, :], in_=gdn_dt_bias[0:1, :].partition_broadcast(128)), writes=['abc'])
                P.dma('sp', 'c0', lambda e: e.dma_start(out=abc[:, 1, :], in_=gdn_a_log[0:1, :].partition_broadcast(128)), writes=['abc'])
                P.op('act', lambda e: e.activation(out=abc[:, 1, :], in_=abc[:, 1, :], func=AF.Exp), reads=['abc'], writes=['abc'])
                P.op('dve', lambda e: e.tensor_scalar(out=abc[:, 1, :], in0=abc[:, 1, :], scalar1=-1.0, scalar2=None, op0=ALU.mult), reads=['abc'], writes=['abc'])
                gd_bt = [(g * GRP) // ROWS for g in range(1, NGL)]
                boundary(gd_bt, 0, 12)
                qkv_v = QKV.rearrange("(c p) t -> p c t", p=128)

                def p1_gdn_group(seq, g):
                    width = CTX if seq == 'ctx' else GRP
                    ntile = width // 128
                    seqoff = 0 if seq == 'ctx' else CTX
                    j0 = g * GRP
                    tl = []
                    for i in range(ntile):
                        xt, xk = xr.next()
                        seq_tile_loads(seq, 'cm', g * NTG + i, xt, xk)
                        tl.append((xt, xk))
                    P.flush()
                    for i in range(ntile):
                        xt, xk = tl[i]
                        h, hk = norm_mod(xt, xk, 'CG1' if seq == 'ctx' else 'G1', 'CSH1' if seq == 'ctx' else 'SH1')
                        transpose_to(h, hk, hT, 'hT', i * 128, flip=i)
                    for part in range(3):
                        if g == 0:
                            P.op('pool', lambda e: e.memset(PT[:, :, 0:2], 0.0), writes=['PT'])
                        else:
                            P.op('pool', lambda e, part=part: e.tensor_copy(out=PT[:, :, 0:2], in_=HALO[:, part * 4:part * 4 + 4, :]), reads=['HALO'], writes=['PT'])
                        inproj_fm(part * 512, 4, width, hT, 'hT', lambda c: PT[:, c, 2:2 + width], 'PT')
                        P.op('pool', lambda e, part=part: e.tensor_copy(out=HALO[:, part * 4:part * 4 + 4, :], in_=PT[:, :, width:width + 2]), reads=['PT'], writes=['HALO'])
                        if seq == 'ctx' or g == NGL - 1:
                            P.op('pool', lambda e: e.memset(PT[:, :, 2 + width:3 + width], 0.0), writes=['PT'])
                        else:
                            P.op('pool', lambda e, part=part: e.tensor_copy(out=PT[:, :, 2 + width:3 + width], in_=PTb[:, part * 4:part * 4 + 4, g:g + 1]), reads=['PTb'], writes=['PT'])
                        conv4(4, part * 4, False, width)
                        cvk = ['CV0', 'CV1', 'CV2', 'CV3']
                        P.op('act', lambda e: e.activation(out=CV[:, :, 0:width], in_=CV[:, :, 0:width], func=AF.Silu), reads=cvk, writes=cvk)
                        if part < 2:
                            P.op('pool', lambda e: e.tensor_tensor(out=SQ[:, :, 0:width], in0=CV[:, :, 0:width], in1=CV[:, :, 0:width], op=ALU.mult), reads=cvk, writes=['SQ'])
                            for c in range(4):
                                pt, pk = psr.next()
                                P.op('pe', lambda e, c=c, pt=pt: e.matmul(pt[:, 0:width], lhsT=ones[:], rhs=SQ[:, c, 0:width], start=True, stop=True), reads=['ones', 'SQ'], writes=[pk])
                                P.op('act', lambda e, pt=pt: e.activation(out=RS[:, 0:width], in_=pt[:, 0:width], func=AF.Sqrt, bias=EPS), reads=[pk], writes=['RS'])
                                P.op('dve', lambda e: e.reciprocal(out=RS[:, 0:width], in_=RS[:, 0:width]), reads=['RS'], writes=['RS'])
                                sc_ = (128.0 ** -0.5) if part == 0 else 1.0
                                P.op('dve', lambda e, c=c, sc_=sc_: e.scalar_tensor_tensor(out=CV[:, c, 0:width], in0=CV[:, c, 0:width], scalar=sc_, in1=RS[:, 0:width], op0=ALU.mult, op1=ALU.mult),
                                     reads=['CV%d' % c, 'RS'], writes=['CV%d' % c])
                        P.dma('sp', 'st0', lambda e, part=part: e.dma_start(out=qkv_v[:, part * 4:part * 4 + 4, seqoff + j0:seqoff + j0 + width], in_=CV[:, :, 0:width]),
                              reads=cvk, writes=['QKV'], defer=True)
                    for i in range(ntile):
                        ci = (g * NTG + i) + (0 if seq == 'ctx' else NCC)
                        if seq == 'lat':
                            pt, pk = psr.next()
                            for kc in range(KD):
                                P.op('pe', lambda e, kc=kc, i=i, pt=pt: e.matmul(pt[:, 0:GDW], lhsT=hT[:, kc, i * 128:(i + 1) * 128], rhs=WIN[:, kc, 1536:2048],
                                                                                 start=(kc == 0), stop=(kc == KD - 1)), reads=['hT', 'WIN'], writes=[pk])
                            P.op('act', lambda e, pt=pt: e.activation(out=zt[:], in_=pt[:, 0:GDW], func=AF.Silu), reads=[pk], writes=['zt'])
                            P.dma('sp', 'st1', lambda e, i=i: e.dma_start(out=ZS[j0 + i * 128:j0 + (i + 1) * 128, :], in_=zt[:]), reads=['zt'], writes=['ZS'], defer=True)
                        pt, pk = psr.next()
                        for kc in range(KD):
                            P.op('pe', lambda e, kc=kc, i=i, pt=pt: e.matmul(pt[:, 0:16], lhsT=hT[:, kc, i * 128:(i + 1) * 128], rhs=WIN[:, kc, 2048:2064],
                                                                             start=(kc == 0), stop=(kc == KD - 1)), reads=['hT', 'WIN'], writes=[pk])
                        pv = pt[:, 0:16].rearrange("p (d a h) -> p d a h", d=2, a=2)
                        av = abc[:, 2, :].rearrange("p (d h) -> p d h", d=2)
                        dtb = abc[:, 0, :].rearrange("p (d h) -> p d h", d=2)
                        nea = abc[:, 1, :].rearrange("p (d h) -> p d h", d=2)
                        P.op('dve', lambda e, pv=pv, av=av, dtb=dtb: e.tensor_tensor(out=av, in0=pv[:, :, 0, :], in1=dtb, op=ALU.add), reads=[pk, 'abc'], writes=['abc2'])
                        P.op('act', lambda e, av=av: e.activation(out=av, in_=av, func=AF.Exp), reads=['abc2'], writes=['abc2'])
                        P.op('act', lambda e, av=av: e.activation(out=av, in_=av, func=AF.Ln, bias=1.0), reads=['abc2'], writes=['abc2'])
                        P.op('dve', lambda e, av=av, nea=nea, ci=ci: e.tensor_tensor(out=GBt[:, ci, :, 0, :], in0=av, in1=nea, op=ALU.mult), reads=['abc2', 'abc'], writes=['GBt'])
                        bv = abt[:, 0:8].rearrange("p (d h) -> p d h", d=2)
                        P.op('dve', lambda e, pv=pv, bv=bv: e.tensor_copy(out=bv, in_=pv[:, :, 1, :]), reads=[pk], writes=['abt'])
                        P.op('act', lambda e, bv=bv, ci=ci: e.activation(out=GBt[:, ci, :, 1, :], in_=bv, func=AF.Sigmoid), reads=['abt'], writes=['GBt'])

                p1_gdn_group('ctx', 0)
                for g in range(NGL):
                    p1_gdn_group('lat', g)
            P.barrier()

        if stop_after >= 3:
            with contextlib.ExitStack() as ph:
                GW = sb('GW', [128, 2, 2, 4, 128], stack=ph)
                gbv = sb('gbv', [128, 16], stack=ph)
                nsp = sb('nsp', [128, 8], stack=ph)
                carry = sb('carry', [128, 4], stack=ph)
                xct_r = Ring(nc, ph, 'XCt', [128, 4, GRP], F32, 2)
                Rg = sb('Rg', [128, 4, GRP], stack=ph)
                Ig = sb('Ig', [128, 4, GRP], stack=ph)
                Ag = sb('Ag', [128, 4, GRP], stack=ph)
                Bg = sb('Bg', [128, 4, GRP], stack=ph)
                Hg = sb('Hg', [128, 4, GRP], stack=ph)
                hft_r = Ring(nc, ph, 'HFt', [128, 4, GRP], F32, 2)
                ggt_r = Ring(nc, ph, 'GGt', [128, 4, GRP], F32, 2)
                P.op('pool', lambda e: e.memset(GW[:].rearrange("p a b c f -> p (a b c f)"), 0.0), writes=['GW'])
                for d_ in range(2):
                    for gi in range(2):
                        for n in range(8):
                            c, hb = n // 2, n % 2
                            P.dma('sp', 'c%d' % (n % 2), lambda e, d_=d_, gi=gi, n=n, c=c, hb=hb: e.dma_start(
                                out=GW[hb * 64:(hb + 1) * 64, d_, gi, c, hb * 64:(hb + 1) * 64], in_=rg_gate_w[d_, gi, n]), writes=['GW'])
                load_cols(gbv[:, :], rg_gate_b.rearrange("a (c p) -> (a c) p", p=128), 'gbv')
                load_cols(nsp[:, :], rg_lambda.rearrange("a (c p) -> (a c) p", p=128), 'nsp')
                P.op('act', lambda e: e.activation(out=nsp[:], in_=nsp[:], func=AF.Exp, scale=-1.0), reads=['nsp'], writes=['nsp'])
                P.op('act', lambda e: e.activation(out=nsp[:], in_=nsp[:], func=AF.Ln, bias=1.0), reads=['nsp'], writes=['nsp'])
                P.op('dve', lambda e: e.tensor_scalar(out=nsp[:], in0=nsp[:], scalar1=-8.0, scalar2=None, op0=ALU.mult), reads=['nsp'], writes=['nsp'])
                hf_v = HF.rearrange("(c p) t -> p c t", p=128)
                ytr_v = YTR.rearrange("(c p) t -> p c t", p=128)

                def rg_group(d_, seq, g):
                    width = CTX if seq == 'ctx' else GRP
                    seqoff = 0 if seq == 'ctx' else CTX
                    t0 = g * GRP
                    XCt, xctk = xct_r.next()
                    P.dma('sp', 'l%d' % (xct_r.i % 2), lambda e: e.dma_start(out=XCt[:, :, 0:width], in_=xc_v[:, :, seqoff + t0:seqoff + t0 + width]), reads=['XC'], writes=[xctk])
                    if seq == 'lat' and d_ == 1:
                        HFt, hftk = hft_r.next()
                        GGt, ggtk = ggt_r.next()
                        P.dma('sp', 'l%d' % (2 + hft_r.i % 2), lambda e: e.dma_start(out=HFt[:, :, 0:width], in_=hf_v[:, :, t0:t0 + width]), reads=['HF'], writes=[hftk])
                        P.dma('sp', 'm%d' % (ggt_r.i % 2), lambda e: e.dma_start(out=GGt[:, :, 0:width], in_=gg_v[:, :, t0:t0 + width]), reads=['GG'], writes=[ggtk])
                    P.flush()
                    for gi, dstb, dk in ((0, Rg, 'Rg'), (1, Ig, 'Ig')):
                        for c in range(4):
                            pt, pk = psr.next()
                            P.op('pe', lambda e, gi=gi, c=c, pt=pt: e.matmul(pt[:, 0:width], lhsT=GW[:, d_, gi, c, :], rhs=XCt[:, c, 0:width], start=True, stop=True),
                                 reads=['GW', xctk], writes=[pk])
                            col = (d_ * 2 + gi) * 4 + c
                            P.op('act', lambda e, c=c, pt=pt, dstb=dstb, col=col: e.activation(out=dstb[:, c, 0:width], in_=pt[:, 0:width], func=AF.Sigmoid, bias=gbv[:, col:col + 1]),
                                 reads=[pk, 'gbv'], writes=[dk])
                    for c in range(4):
                        P.op('act', lambda e, c=c: e.activation(out=Ag[:, c, 0:width], in_=Rg[:, c, 0:width], func=AF.Exp, scale=nsp[:, d_ * 4 + c:d_ * 4 + c + 1]),
                             reads=['Rg', 'nsp'], writes=['Ag'])
                    av_ = Ag[:, :, 0:width]
                    P.op('pool', lambda e: e.tensor_tensor(out=Rg[:, :, 0:width], in0=av_, in1=av_, op=ALU.mult), reads=['Ag'], writes=['Rg'])
                    P.op('act', lambda e: e.activation(out=Rg[:, :, 0:width], in_=Rg[:, :, 0:width], func=AF.Sqrt, scale=-1.0, bias=1.0), reads=['Rg'], writes=['Rg'])
                    P.op('dve', lambda e: e.tensor_tensor(out=Bg[:, :, 0:width], in0=Ig[:, :, 0:width], in1=XCt[:, :, 0:width], op=ALU.mult), reads=['Ig', xctk], writes=['Bg'])
                    P.op('dve', lambda e: e.tensor_tensor(out=Bg[:, :, 0:width], in0=Bg[:, :, 0:width], in1=Rg[:, :, 0:width], op=ALU.mult), reads=['Bg', 'Rg'], writes=['Bg'])
                    for c in range(4):
                        if d_ == 0:
                            o_, a_, b_ = Hg[:, c, 0:width], Ag[:, c, 0:width], Bg[:, c, 0:width]
                        else:
                            o_, a_, b_ = Hg[:, c, width - 1::-1] if False else Hg[:, c, 0:width][:, ::-1], Ag[:, c, 0:width][:, ::-1], Bg[:, c, 0:width][:, ::-1]
                        P.op('dve', lambda e, c=c, o_=o_, a_=a_, b_=b_: e.tensor_tensor_scan(out=o_, data0=a_, data1=b_, initial=carry[:, c:c + 1], op0=ALU.mult, op1=ALU.add),
                             reads=['Ag', 'Bg', 'carry'], writes=['Hg'])
                    last = width - 1 if d_ == 0 else 0
                    P.op('dve', lambda e: e.tensor_copy(out=carry[:, :], in_=Hg[:, :, last]), reads=['Hg'], writes=['carry'])
                    if seq == 'lat' and d_ == 0:
                        P.dma('sp', 'st0', lambda e: e.dma_start(out=hf_v[:, :, t0:t0 + width], in_=Hg[:, :, 0:width]), reads=['Hg'], writes=['HF'], defer=True)
                    if seq == 'lat' and d_ == 1:
                        P.op('pool', lambda e: e.tensor_tensor(out=Hg[:, :, 0:width], in0=Hg[:, :, 0:width], in1=HFt[:, :, 0:width], op=ALU.add), reads=['Hg', hftk], writes=['Hg'])
                        P.op('dve', lambda e: e.tensor_tensor(out=Hg[:, :, 0:width], in0=Hg[:, :, 0:width], in1=GGt[:, :, 0:width], op=ALU.mult), reads=['Hg', ggtk], writes=['Hg'])
                        P.dma('sp', 'st1', lambda e: e.dma_start(out=ytr_v[:, :, t0:t0 + width], in_=Hg[:, :, 0:width]), reads=['Hg'], writes=['YTR'], defer=True)

                for d_ in range(2):
                    P.op('pool', lambda e: e.memset(carry[:], 0.0), writes=['carry'])
                    rg_group(d_, 'ctx', 0)
                    for g in (range(NGL) if d_ == 0 else range(NGL - 1, -1, -1)):
                        rg_group(d_, 'lat', g)
                P.barrier()

        if stop_after >= 4:
            with contextlib.ExitStack() as ph:
                qk_r = Ring(nc, ph, 'qkvt', [128, 12, 128], F32, 2)
                Sst = sb('Sst', [128, 4, 128], stack=ph)
                S1 = sb('S1', [128, 4, 128], stack=ph)
                SCg = sb('SCg', [128, 8], stack=ph)
                E1_r = Ring(nc, ph, 'E1', [128, 8], F32, 2)
                KDs = sb('KDs', [128, 4], stack=ph)
                NBt = sb('NBt', [128, 4], stack=ph)
                BGt = sb('BGt', [128, 4], stack=ph)
                GBC = sb('GBC', [128, 4, 128], stack=ph)
                Dm = sb('Dm', [128, 4, 128], stack=ph)
                NBN = sb('NBN', [128, 4, 128], stack=ph)
                T1 = sb('T1', [128, 4, 128], stack=ph)
                ATT = sb('ATT', [128, 4, 128], stack=ph)
                ATTT_r = Ring(nc, ph, 'ATTT', [128, 4, 128], F32, 2)
                XYr = Ring(nc, ph, 'XY', [128, 4, 256], F32, 2)
                TTr = Ring(nc, ph, 'TTm', [128, 4, 128], F32, 2)
                VB = sb('VB', [128, 4, 128], stack=ph)
                KBG = sb('KBG', [128, 4, 128], stack=ph)
                KDEC_r = Ring(nc, ph, 'KDEC', [128, 4, 128], F32, 2)
                WV_r = Ring(nc, ph, 'WV', [128, 4, 128], F32, 2)
                KCT_r = Ring(nc, ph, 'KCT', [128, 4, 128], F32, 2)
                VN = sb('VN', [128, 4, 128], stack=ph)
                O1 = sb('O1', [128, 4, 128], stack=ph)
                Ot = sb('Ot', [128, 4, 128], stack=ph)
                oft_r = Ring(nc, ph, 'OFt', [128, 4, 128], F32, 2)
                zst_r = Ring(nc, ph, 'ZSt', [128, 4, 128], F32, 2)
                NGb = sb('NGb', [128, 128], stack=ph)
                rsd = sb('rsd', [128, 4], stack=ph)
                P.dma('sp', 'c0', lambda e: e.dma_start(out=NGb[:], in_=gdn_norm_g[0:1, :].partition_broadcast(128)), writes=['NGb'])

                def bc_h(col4):
                    return col4[:, :, None].broadcast_to([128, 4, 128]) if False else col4.unsqueeze(2).broadcast_to([128, 4, 128])

                def bc_m(m):
                    return m.unsqueeze(1).broadcast_to([128, 4, 128])

                def mm4(lhs_fn, rhs_fn, reads, width=128):
                    pt, pk = psr.next()
                    for h in range(4):
                        P.op('pe', lambda e, h=h, pt=pt: e.matmul(pt[:, h * 128:(h + 1) * 128], lhsT=lhs_fn(h), rhs=rhs_fn(h), start=True, stop=True), reads=reads, writes=[pk])
                    return pt[:, :].rearrange("p (h f) -> p h f", h=4), pk

                def tr4(src_fn, reads):
                    pt, pk = psr.next()
                    for h in range(4):
                        P.op('pe', lambda e, h=h, pt=pt: e.transpose(out=pt[:, h * 128:(h + 1) * 128], in_=src_fn(h), identity=ident[:]), reads=reads + ['ident'], writes=[pk])
                    return pt[:, :].rearrange("p (h f) -> p h f", h=4), pk

                def gdn_chunk(d_, seq, cidx):
                    ci = cidx + (0 if seq == 'ctx' else NCC)
                    E1, e1k = E1_r.next()
                    ATTT, atttk = ATTT_r.next()
                    KDEC, kdeck = KDEC_r.next()
                    WV, wvk = WV_r.next()
                    KCT, kctk = KCT_r.next()
                    j0 = cidx * 128
                    seqoff = 0 if seq == 'ctx' else CTX
                    qt, qk_ = qk_r.next()
                    P.dma('sp', 'l%d' % (qk_r.i % 2), lambda e: e.dma_start(out=qt[:], in_=qkv_v[:, :, seqoff + j0:seqoff + j0 + 128]), reads=['QKV'], writes=[qk_])
                    if seq == 'lat' and d_ == 1:
                        OFt, oftk = oft_r.next()
                        ZSt, zstk = zst_r.next()
                        P.dma('sp', 'l%d' % (2 + oft_r.i % 2), lambda e: e.dma_start(out=OFt[:].rearrange("p h f -> p (h f)"), in_=OF[j0:j0 + 128, :]), reads=['OF'], writes=[oftk])
                        P.dma('sp', 'm%d' % (zst_r.i % 2), lambda e: e.dma_start(out=ZSt[:].rearrange("p h f -> p (h f)"), in_=ZS[j0:j0 + 128, :]), reads=['ZS'], writes=[zstk])
                    P.flush()
                    g_ = GBt[:, ci, d_, 0, :]
                    be_ = GBt[:, ci, d_, 1, :]
                    qT = lambda h: qt[:, h, :]
                    kT = lambda h: qt[:, 4 + h, :]
                    vT = lambda h: qt[:, 8 + h, :]
                    pt, pk = psr.next()
                    P.op('pe', lambda e: e.matmul(pt[:, 0:4], lhsT=Mm[d_][:], rhs=g_, start=True, stop=True), reads=[MK[d_], 'GBt'], writes=[pk])
                    P.op('pe', lambda e: e.matmul(pt[:, 4:8], lhsT=ones[:], rhs=g_, start=True, stop=True), reads=['ones', 'GBt'], writes=[pk])
                    P.op('dve', lambda e: e.tensor_copy(out=SCg[:], in_=pt[:, 0:8]), reads=[pk], writes=['SCg'])
                    P.op('act', lambda e: e.activation(out=E1[:], in_=SCg[:], func=AF.Exp), reads=['SCg'], writes=[e1k])
                    P.op('dve', lambda e: e.tensor_tensor(out=KDs[:], in0=SCg[:, 4:8], in1=SCg[:, 0:4], op=ALU.subtract), reads=['SCg'], writes=['KDs'])
                    P.op('act', lambda e: e.activation(out=KDs[:], in_=KDs[:], func=AF.Exp), reads=['KDs'], writes=['KDs'])
                    P.op('dve', lambda e: e.tensor_scalar(out=NBt[:], in0=be_, scalar1=-1.0, scalar2=None, op0=ALU.mult), reads=['GBt'], writes=['NBt'])
                    P.op('dve', lambda e: e.tensor_tensor(out=BGt[:], in0=be_, in1=E1[:, 0:4], op=ALU.mult), reads=['GBt', e1k], writes=['BGt'])
                    if gdn_cut <= 1:
                        return
                    P.op('dve', lambda e: e.tensor_copy(out=GBC[:], in_=bc_h(g_)), reads=['GBt'], writes=['GBC'])
                    pt, pk = psr.next()
                    for h in range(4):
                        P.op('pe', lambda e, h=h, pt=pt: e.matmul(pt[:, h * 128:(h + 1) * 128], lhsT=GBC[:, h, :], rhs=Mm[d_][:], start=True, stop=False), reads=['GBC', MK[d_]], writes=[pk])
                        P.op('pe', lambda e, h=h, pt=pt: e.matmul(pt[:, h * 128:(h + 1) * 128], lhsT=ident[:], rhs=BGm[d_][:], start=False, stop=True), reads=['ident', BK[d_]], writes=[pk])
                    for h in range(4):
                        P.op('act', lambda e, h=h, pt=pt: e.activation(out=Dm[:, h, :], in_=pt[:, h * 128:(h + 1) * 128], func=AF.Exp, scale=-1.0, bias=SCg[:, h:h + 1]),
                             reads=[pk, 'SCg'], writes=['Dm'])
                    if gdn_cut <= 2:
                        return
                    P.op(p3pool, lambda e: e.tensor_tensor(out=NBN[:], in0=bc_m(noti[:]), in1=bc_h(NBt[:]), op=ALU.mult), reads=['noti', 'NBt'], writes=['NBN'])
                    pkk, pkkk = mm4(kT, kT, [qk_])
                    P.op('dve', lambda e: e.tensor_tensor(out=T1[:], in0=pkk, in1=Dm[:], op=ALU.mult), reads=[pkkk, 'Dm'], writes=['T1'])
                    XY, xyk = XYr.next()
                    P.op('dve', lambda e: e.tensor_tensor(out=XY[:, :, 0:128], in0=T1[:], in1=NBN[:], op=ALU.mult), reads=['T1', 'NBN'], writes=[xyk])
                    if seq == 'lat':
                        pqk, pqkk = mm4(qT, kT, [qk_])
                        P.op('dve', lambda e: e.tensor_tensor(out=ATT[:], in0=pqk, in1=Dm[:], op=ALU.mult), reads=[pqkk, 'Dm'], writes=['ATT'])
                        pa, pak = tr4(lambda h: ATT[:, h, :], ['ATT'])
                        P.op('act', lambda e: e.copy(out=ATTT[:], in_=pa), reads=[pak], writes=[atttk])
                    if gdn_cut <= 3:
                        return
                    py, pyk = tr4(lambda h: XY[:, h, 0:128], [xyk])
                    if gdn_cut <= 3.1:
                        return
                    P.op('dve', lambda e: e.tensor_copy(out=XY[:, :, 128:256], in_=py), reads=[pyk], writes=[xyk])
                    if gdn_cut <= 3.2:
                        return
                    TTm, ttk = TTr.next()
                    P.op('dve', lambda e: e.tensor_tensor(out=TTm[:], in0=py, in1=bc_m(ident[:]), op=ALU.add), reads=[pyk, 'ident'], writes=[ttk])
                    if gdn_cut <= 3.4:
                        return
                    pkt, pktk = tr4(kT, [qk_])
                    P.op('dve', lambda e: e.tensor_tensor(out=KBG[:], in0=pkt, in1=bc_h(BGt[:]), op=ALU.mult), reads=[pktk, 'BGt'], writes=['KBG'])
                    P.op('dve', lambda e: e.tensor_tensor(out=KDEC[:], in0=pkt, in1=bc_h(KDs[:]), op=ALU.mult), reads=[pktk, 'KDs'], writes=[kdeck])
                    if gdn_cut <= 3.6:
                        return
                    pvt, pvtk = tr4(vT, [qk_])
                    P.op('dve', lambda e: e.tensor_tensor(out=VB[:], in0=pvt, in1=bc_h(be_), op=ALU.mult), reads=[pvtk, 'GBt'], writes=['VB'])
                    if gdn_cut <= 4:
                        return
                    for lvl in range(6):
                        XYn, xynk = XYr.next()
                        for half in range(2):
                            pt, pk = psr.next()
                            for hh in range(2):
                                h = half * 2 + hh
                                P.op('pe', lambda e, h=h, hh=hh, pt=pt: e.matmul(pt[:, hh * 256:hh * 256 + 128], lhsT=XY[:, h, 128:256], rhs=XY[:, h, 0:128], start=True, stop=True),
                                     reads=[xyk], writes=[pk])
                                P.op('pe', lambda e, h=h, hh=hh, pt=pt: e.matmul(pt[:, hh * 256 + 128:hh * 256 + 256], lhsT=XY[:, h, 0:128], rhs=XY[:, h, 128:256], start=True, stop=True),
                                     reads=[xyk], writes=[pk])
                            dstv = XYn[:, half * 2:half * 2 + 2, :]
                            srcv = pt[:, :].rearrange("p (h f) -> p h f", h=2)
                            if half == 0:
                                P.op('act', lambda e, dstv=dstv, srcv=srcv: e.copy(out=dstv, in_=srcv), reads=[pk], writes=[xynk])
                            else:
                                P.op('dve', lambda e, dstv=dstv, srcv=srcv: e.tensor_copy(out=dstv, in_=srcv), reads=[pk], writes=[xynk])
                        ptt, pttk = mm4(lambda h: XYn[:, h, 0:128], lambda h: TTm[:, h, :], [xynk, ttk])
                        TTn, ttnk = TTr.next()
                        P.op('dve', lambda e, TTn=TTn, TTm=TTm, ptt=ptt: e.tensor_tensor(out=TTn[:], in0=ptt, in1=TTm[:], op=ALU.add), reads=[pttk, ttk], writes=[ttnk])
                        XY, xyk, TTm, ttk = XYn, xynk, TTn, ttnk
                    if gdn_cut <= 5:
                        return
                    pw, pwk = mm4(lambda h: TTm[:, h, :], lambda h: VB[:, h, :], [ttk, 'VB'])
                    P.op('act', lambda e: e.copy(out=WV[:], in_=pw), reads=[pwk], writes=[wvk])
                    pc, pck = mm4(lambda h: KBG[:, h, :], lambda h: TTm[:, h, :], [ttk, 'KBG'])
                    P.op('dve', lambda e: e.tensor_copy(out=KCT[:], in_=pc), reads=[pck], writes=[kctk])
                    def rec():
                        pa_, pak_ = mm4(lambda h: KCT[:, h, :], lambda h: Sst[:, h, :], [kctk, 'Sst'])
                        P.op('dve', lambda e: e.tensor_tensor(out=VN[:], in0=WV[:], in1=pa_, op=ALU.subtract), reads=[wvk, pak_], writes=['VN'])
                        if seq == 'lat':
                            po1, po1k = mm4(qT, lambda h: Sst[:, h, :], [qk_, 'Sst'])
                            po2, po2k = mm4(lambda h: ATTT[:, h, :], lambda h: VN[:, h, :], [atttk, 'VN'])
                            P.op('dve', lambda e: e.tensor_tensor(out=O1[:], in0=po1, in1=bc_h(E1[:, 0:4]), op=ALU.mult), reads=[po1k, e1k], writes=['O1'])
                            P.op('dve', lambda e: e.tensor_tensor(out=Ot[:], in0=po2, in1=O1[:], op=ALU.add), reads=[po2k, 'O1'], writes=['Ot'])
                        ps_, psk_ = mm4(lambda h: KDEC[:, h, :], lambda h: VN[:, h, :], [kdeck, 'VN'])
                        P.op(p3pool, lambda e: e.tensor_tensor(out=S1[:], in0=Sst[:], in1=bc_h(E1[:, 4:8]), op=ALU.mult), reads=['Sst', e1k], writes=['S1'])
                        P.op('dve', lambda e: e.tensor_tensor(out=Sst[:], in0=ps_, in1=S1[:], op=ALU.add), reads=[psk_, 'S1'], writes=['Sst'])
                        if seq == 'lat' and d_ == 0:
                            P.dma('sp', 'st0', lambda e: e.dma_start(out=OF[j0:j0 + 128, :], in_=Ot[:].rearrange("p h f -> p (h f)")), reads=['Ot'], writes=['OF'], defer=True)
                        if seq == 'lat' and d_ == 1:
                            P.op(p3pool, lambda e: e.tensor_tensor(out=Ot[:], in0=Ot[:], in1=OFt[:], op=ALU.add), reads=['Ot', oftk], writes=['Ot'])
                            for h in range(4):
                                P.op('act', lambda e, h=h: e.activation(out=junk[:, 0:128], in_=Ot[:, h, :], func=AF.Square, accum_out=rsd[:, h:h + 1]), reads=['Ot'], writes=['junk', 'rsd'])
                            P.op('act', lambda e: e.activation(out=rsd[:], in_=rsd[:], func=AF.Sqrt, scale=1.0 / 128, bias=EPS), reads=['rsd'], writes=['rsd'])
                            P.op('dve', lambda e: e.reciprocal(out=rsd[:], in_=rsd[:]), reads=['rsd'], writes=['rsd'])
                            P.op('dve', lambda e: e.tensor_tensor(out=Ot[:], in0=Ot[:], in1=bc_h(rsd[:]), op=ALU.mult), reads=['Ot', 'rsd'], writes=['Ot'])
                            P.op(p3pool, lambda e: e.tensor_tensor(out=Ot[:], in0=Ot[:], in1=bc_m(NGb[:]), op=ALU.mult), reads=['Ot', 'NGb'], writes=['Ot'])
                            P.op('dve', lambda e: e.tensor_tensor(out=Ot[:], in0=Ot[:], in1=ZSt[:], op=ALU.mult), reads=['Ot', zstk], writes=['Ot'])
                            ncol = 128 // ROWS
                            for cc_ in range(ncol):
                                col = cidx * ncol + cc_
                                P.dma('sp', 'st%d' % (cc_ % 2), lambda e, cc_=cc_, col=col: e.dma_start(out=yg_cm[col], in_=Ot[cc_ * ROWS:(cc_ + 1) * ROWS].rearrange("p h f -> p (h f)")),
                                      reads=['Ot'], writes=['YG'], defer=True)
                    return rec

                for d_ in range(2):
                    P.op('pool', lambda e: e.memset(Sst[:].rearrange("p h f -> p (h f)"), 0.0), writes=['Sst'])
                    order = [('ctx', cidx) for cidx in (range(NCC) if d_ == 0 else range(NCC - 1, -1, -1))]
                    order += [('lat', cidx) for cidx in (range(NCL) if d_ == 0 else range(NCL - 1, -1, -1))]
                    prev = gdn_chunk(d_, order[0][0], order[0][1])
                    for k in range(1, len(order)):
                        nxt = gdn_chunk(d_, order[k][0], order[k][1])
                        if prev is not None:
                            prev()
                        prev = nxt
                    if prev is not None:
                        prev()
                P.barrier()

        st_a.close()
        NT = L // 128
        if stop_after >= 5:
            with contextlib.ExitStack() as ph:
                NXR = 2
                xr = Ring(nc, ph, 'xq', [128, D], F32, NXR)
                WO = sb('WO', [128, KD, D], stack=ph)
                WQ = sb('WQ', [128, KD, 2048], stack=ph)
                yrt_r = Ring(nc, ph, 'YRt', [128, 4, 128], F32, 2)
                ygt_r = Ring(nc, ph, 'YGt', [128, GDW], F32, 2)
                YGT = sb('YGT', [128, 4, 128], stack=ph)
                x1r = Ring(nc, ph, 'X1', [128, D], F32, 2)
                h2t_r = Ring(nc, ph, 'h2T', [128, KD, 128], F32, 2)
                qtr = Ring(nc, ph, 'QT', [128, 16, 128], F32, 2)
                stg_r = Ring(nc, ph, 'stg', [128, 1, D], F32, 2)
                stb_r = Ring(nc, ph, 'stb', [128, 1, D], BF16, 2)
                NCH = 256
                uv_v = UV16.rearrange("(p r) d -> p r d", p=128)
                tabs = [(peer_u.rearrange("(p r) d -> p r d", p=128), uv_v[:, :, 0:D]),
                        (peer_v.rearrange("(p r) d -> p r d", p=128), uv_v[:, :, D:2 * D])]

                def convert_chunk(q):
                    src, dst = tabs[q // 128]
                    j = q % 128
                    sg, sgk = stg_r.next()
                    P.dma('sp', 'cl%d' % (stg_r.i % 2), lambda e: e.dma_start(out=sg[:], in_=src[:, j:j + 1, :]), writes=[sgk])
                    P.flush()
                    sbt, sbk = stb_r.next()
                    P.op('pool', lambda e: e.tensor_copy(out=sbt[:], in_=sg[:]), reads=[sgk], writes=[sbk])
                    P.dma('sp', 'cs%d' % (stb_r.i % 2), lambda e: e.dma_start(out=dst[:, j:j + 1, :], in_=sbt[:]), reads=[sbk], writes=['T16'], defer=True)
                P.dma('sp', 'w0', lambda e: e.dma_start(out=WO[:], in_=w_out.rearrange("(k p) n -> p k n", p=128)), writes=['WO'])
                P.dma('sp', 'w1', lambda e: e.dma_start(out=WQ[:], in_=peer_wq.rearrange("(k p) n -> p k n", p=128)), writes=['WQ'])
                ytr_v2 = YTR.rearrange("(c p) t -> p c t", p=128)
                qts_v = QTS.rearrange("(a p) t -> p a t", p=128)
                def p4a_front(i):
                    t0 = i * 128
                    xt, xk = xr.next()
                    P.dma('sp', 'x%d' % (xr.i % NXR), lambda e: e.dma_start(out=xt[:], in_=x[t0:t0 + 128, :]), writes=[xk])
                    YRt, yrtk = yrt_r.next()
                    YGt, ygtk = ygt_r.next()
                    P.dma('sp', 'l%d' % (yrt_r.i % 2), lambda e: e.dma_start(out=YRt[:], in_=ytr_v2[:, :, t0:t0 + 128]), reads=['YTR'], writes=[yrtk])
                    P.dma('sp', 'l%d' % (2 + ygt_r.i % 2), lambda e: e.dma_start(out=YGt[:], in_=YG[t0:t0 + 128, :]), reads=['YG'], writes=[ygtk])
                    P.flush()
                    for q in range(i * NCH // NT, (i + 1) * NCH // NT):
                        convert_chunk(q)
                    pt, pk = psr.next()
                    for c in range(4):
                        P.op('pe', lambda e, c=c, pt=pt: e.transpose(out=pt[:, c * 128:(c + 1) * 128], in_=YGt[:, c * 128:(c + 1) * 128], identity=ident[:]), reads=[ygtk, 'ident'], writes=[pk])
                    P.op('act', lambda e, pt=pt: e.copy(out=YGT[:].rearrange("p c t -> p (c t)"), in_=pt[:]), reads=[pk], writes=['YGT'])
                    X1, x1k = x1r.next()
                    for n in range(2):
                        pt, pk = psr.next()
                        for kc in range(8):
                            lhs = YRt[:, kc, :] if kc < 4 else YGT[:, kc - 4, :]
                            P.op('pe', lambda e, kc=kc, n=n, pt=pt, lhs=lhs: e.matmul(pt[:], lhsT=lhs, rhs=WO[:, kc, n * 512:(n + 1) * 512], start=(kc == 0), stop=(kc == 7)),
                                 reads=[yrtk, 'YGT', 'WO'], writes=[pk])
                        P.op('dve', lambda e, n=n, pt=pt, X1=X1: e.tensor_tensor(out=X1[:, n * 512:(n + 1) * 512], in0=pt[:], in1=MOD['GT1'][:, n * 512:(n + 1) * 512], op=ALU.mult),
                             reads=[pk, 'mod_GT1'], writes=[x1k])
                    P.op('pool', lambda e, xt=xt, X1=X1: e.tensor_tensor(out=X1[:], in0=X1[:], in1=xt[:], op=ALU.add), reads=[x1k, xk], writes=[x1k])
                    P.dma('sp', 'st0', lambda e, X1=X1: e.dma_start(out=X1S[t0:t0 + 128, :], in_=X1[:]), reads=[x1k], writes=['X1S'], defer=True)
                    H2, h2k = norm_mod(X1, x1k, 'G2', 'SH2')
                    P.dma('sp', 'st1', lambda e, H2=H2: e.dma_start(out=H2S[t0:t0 + 128, :], in_=H2[:]), reads=[h2k], writes=['H2S'], defer=True)
                    h2T, h2tk = h2t_r.next()
                    transpose_to(H2, h2k, h2T, h2tk, 0)
                    def qproj():
                        QT, qtk = qtr.next()
                        for qb in range(4):
                            pt, pk = psr.next()
                            for j in range(4):
                                hh = qb * 4 + j
                                for kc in range(KD):
                                    P.op('pe', lambda e, j=j, hh=hh, kc=kc, pt=pt: e.matmul(pt[:, j * 128:(j + 1) * 128], lhsT=WQ[:, kc, hh * 128:(hh + 1) * 128], rhs=h2T[:, kc, :],
                                                                                           start=(kc == 0), stop=(kc == KD - 1)), reads=['WQ', h2tk], writes=[pk])
                            if qb % 2 == 0:
                                P.op('act', lambda e, qb=qb, pt=pt, QT=QT: e.copy(out=QT[:, qb * 4:qb * 4 + 4, :].rearrange("p a t -> p (a t)"), in_=pt[:]), reads=[pk], writes=[qtk])
                            else:
                                P.op('dve', lambda e, qb=qb, pt=pt, QT=QT: e.tensor_copy(out=QT[:, qb * 4:qb * 4 + 4, :].rearrange("p a t -> p (a t)"), in_=pt[:]), reads=[pk], writes=[qtk])
                        P.dma('sp', 'st2', lambda e, QT=QT: e.dma_start(out=qts_v[:, :, t0:t0 + 128], in_=QT[:]), reads=[qtk], writes=['QTS'], defer=True)
                    return qproj

                fq = p4a_front(0)
                for i in range(NT):
                    nfq = p4a_front(i + 1) if i + 1 < NT else None
                    fq()
                    fq = nfq
                P.barrier()

            st_b.close()
            with contextlib.ExitStack() as ph:
                NXR = 2
                xr = Ring(nc, ph, 'xw', [128, D], F32, NXR)
                NUB = cfg.get('nub', 27)
                KEY = sb('KEY', [128, 2, 128], stack=ph)
                KEYT = sb('KEYT', [128, 2, 128], stack=ph)
                QT = sb('QTt', [128, 16, 128], stack=ph)
                SC = sb('SC', [128, 16, 128], stack=ph)
                SCB = sb('SCB', [128, 16, 128], stack=ph)
                CAND = SC[:].rearrange("p a t -> p (a t)").rearrange("p (h c) -> p h c", h=8)
                OH = QT[:].rearrange("p a t -> p (a t)").rearrange("p (k j) -> p k j", j=16)
                TOPV = sb('TOPV', [128, 16, 16], stack=ph)
                TOPI = sb('TOPI', [128, 16, 16], U32, stack=ph)
                TOPIF = sb('TOPIF', [128, 16, 16], stack=ph)
                CANDB = sb('CANDB', [128, 8, 256], stack=ph)
                TS = sb('TS', [128, 8, 16], stack=ph)
                POS = sb('POS', [128, 8, 16], U32, stack=ph)
                PAB = sb('PAB', [128, 2, 128], U32, stack=ph)
                PABF = sb('PABF', [128, 2, 128], stack=ph)
                IOT = sb('IOT', [128, 16], stack=ph)
                ISEL = sb('ISEL', [128, 2, 128], stack=ph)
                IDXF = sb('IDXF', [128, 128], stack=ph)
                idx_r = Ring(nc, ph, 'IDX', [128, 128], U32, 2)
                gate_r = Ring(nc, ph, 'GATE', [128, 8, 16], F32, 2)
                gsum = sb('gsum', [128, 8], stack=ph)
                DOT = sb('DOT', [128, 128], stack=ph)
                DOTG = sb('DOTG', [128, 128], stack=ph)
                jk_r = Ring(nc, ph, 'jk', [128, D], BF16, 4)
                WGT = sb('WGT', [128, 128], stack=ph)
                GTMP = sb('GTMP', [128, 1, 128], stack=ph)
                FIN = sb('FIN', [128, D], stack=ph)
                ub_r = Ring(nc, ph, 'UB', [128, 2 * D], BF16, NUB)
                GT_ = sb('GT_', [128, 128], stack=ph)
                dg_r = Ring(nc, ph, 'DG', [128, 128], BF16, 8)
                fin = sb('fin', [128, 2], stack=ph)
                P.dma('sp', 'c0', lambda e: e.dma_start(out=KEY[:], in_=peer_keys.rearrange("x k d -> k x d")), writes=['KEY'])
                pt, pk = psr.next()
                for x_ in range(2):
                    P.op('pe', lambda e, x_=x_, pt=pt: e.transpose(out=pt[:, x_ * 128:(x_ + 1) * 128], in_=KEY[:, x_, :], identity=ident[:]), reads=['KEY', 'ident'], writes=[pk])
                P.op('dve', lambda e: e.tensor_copy(out=KEYT[:].rearrange("p x k -> p (x k)"), in_=pt[:, 0:256]), reads=[pk], writes=['KEYT'])
                P.op('pool', lambda e: e.iota(IOT[:], pattern=[[1, 16]], base=0, channel_multiplier=0, allow_small_or_imprecise_dtypes=True), writes=['IOT'])
                qts_v = QTS.rearrange("(a p) t -> p a t", p=128)

                def top16_multi(n, vals_fn, vkey, scr_fn, skey, outv_fn, outi_fn, okeys):
                    for j in range(n):
                        P.op('dve', lambda e, j=j: e.max(out=outv_fn(j)[:, 0:8], in_=vals_fn(j)), reads=[vkey], writes=['%s%d' % (okeys[0], j)])
                    for j in range(n):
                        P.op('dve', lambda e, j=j: e.max_index(out=outi_fn(j)[:, 0:8], in_max=outv_fn(j)[:, 0:8], in_values=vals_fn(j)), reads=[vkey, '%s%d' % (okeys[0], j)], writes=['%s%d' % (okeys[1], j)])
                    for j in range(n):
                        P.op('dve', lambda e, j=j: e.match_replace(out=scr_fn(j), in_to_replace=outv_fn(j)[:, 0:8], in_values=vals_fn(j), imm_value=NEG), reads=[vkey, '%s%d' % (okeys[0], j)], writes=['%s%d' % (skey, j)])
                    for j in range(n):
                        P.op('dve', lambda e, j=j: e.max(out=outv_fn(j)[:, 8:16], in_=scr_fn(j)), reads=['%s%d' % (skey, j)], writes=['%s%d' % (okeys[0], j)])
                    for j in range(n):
                        P.op('dve', lambda e, j=j: e.max_index(out=outi_fn(j)[:, 8:16], in_max=outv_fn(j)[:, 8:16], in_values=scr_fn(j)), reads=['%s%d' % (skey, j), '%s%d' % (okeys[0], j)], writes=['%s%d' % (okeys[1], j)])

                def prep(i):
                    t0 = i * 128
                    X1t, x1k = xr.next()
                    P.dma('sp', 'x%d' % (xr.i % NXR), lambda e: e.dma_start(out=X1t[:], in_=X1S[t0:t0 + 128, :]), reads=['X1S'], writes=[x1k])
                    H2t, h2k = hr.next()
                    P.dma('sp', 'l%d' % (hr.i % 2), lambda e: e.dma_start(out=H2t[:], in_=H2S[t0:t0 + 128, :]), reads=['H2S'], writes=[h2k])
                    P.dma('sp', 'l2', lambda e: e.dma_start(out=QT[:], in_=qts_v[:, :, t0:t0 + 128]), reads=['QTS'], writes=['QT'])
                    P.flush()
                    for qb in range(4):
                        pt, pk = psr.next()
                        for j in range(4):
                            hh = qb * 4 + j
                            P.op('pe', lambda e, j=j, hh=hh, pt=pt: e.matmul(pt[:, j * 128:(j + 1) * 128], lhsT=QT[:, hh, :], rhs=KEYT[:, hh % 2, :], start=True, stop=True),
                                 reads=['QT', 'KEYT'], writes=[pk])
                        P.op('act', lambda e, qb=qb, pt=pt: e.copy(out=SC[:, qb * 4:qb * 4 + 4, :].rearrange("p a t -> p (a t)"), in_=pt[:]), reads=[pk], writes=['SC'])
                    top16_multi(16, lambda j: SC[:, j, :], 'SC', lambda j: SCB[:, j, :], 'SCB', lambda j: TOPV[:, j, :], lambda j: TOPI[:, j, :], ('TOPV', 'TOPI'))
                    tvk = ['TOPV%d' % j for j in range(16)]
                    tik = ['TOPI%d' % j for j in range(16)]
                    P.op('dve', lambda e: e.tensor_copy(out=TOPIF[:], in_=TOPI[:]), reads=tik, writes=['TOPIF'])
                    tv4 = TOPV[:].rearrange("p (h x) k -> p h x k", x=2)
                    ti4 = TOPIF[:].rearrange("p (h x) k -> p h x k", x=2)
                    P.op('dve', lambda e: e.tensor_tensor(out=CAND.rearrange("p h (a b) -> p h a b", a=16),
                                                          in0=tv4[:, :, 0, :].unsqueeze(3).broadcast_to([128, 8, 16, 16]),
                                                          in1=tv4[:, :, 1, :].unsqueeze(2).broadcast_to([128, 8, 16, 16]), op=ALU.add), reads=tvk + ['SCB%d' % j for j in range(16)], writes=['SC'])
                    top16_multi(8, lambda j: CAND[:, j, :], 'SC', lambda j: CANDB[:, j, :], 'CANDB', lambda j: TS[:, j, :], lambda j: POS[:, j, :], ('TS', 'POS'))
                    tsk = ['TS%d' % j for j in range(8)]
                    posk = ['POS%d' % j for j in range(8)]
                    posf = POS[:].rearrange("p h k -> p (h k)")
                    P.op('dve', lambda e: e.tensor_single_scalar(out=PAB[:, 0, :], in_=posf, scalar=4, op=ALU.arith_shift_right), reads=posk, writes=['PAB'])
                    P.op('dve', lambda e: e.tensor_single_scalar(out=PAB[:, 1, :], in_=posf, scalar=15, op=ALU.bitwise_and), reads=posk, writes=['PAB'])
                    P.op('dve', lambda e: e.tensor_copy(out=PABF[:], in_=PAB[:]), reads=['PAB'], writes=['PABF'])
                    for x_ in range(2):
                        P.op('dve', lambda e, x_=x_: e.tensor_tensor(out=OH, in0=PABF[:, x_, :].unsqueeze(2).broadcast_to([128, 128, 16]),
                                                                      in1=IOT[:].unsqueeze(1).broadcast_to([128, 128, 16]), op=ALU.is_equal), reads=['PABF', 'IOT'], writes=['QT'])
                        oh4 = OH.rearrange("p (h k) j -> p h k j", h=8)
                        P.op('dve', lambda e, x_=x_, oh4=oh4: e.tensor_tensor(out=oh4, in0=oh4, in1=ti4[:, :, x_, :].unsqueeze(2).broadcast_to([128, 8, 16, 16]), op=ALU.mult),
                             reads=['QT', 'TOPIF'], writes=['QT'])
                        P.op('dve', lambda e, x_=x_: e.tensor_reduce(out=ISEL[:, x_, :], in_=OH, axis=AX.X, op=ALU.add), reads=['QT'], writes=['ISEL'])
                    IDX, idxk = idx_r.next()
                    GATE, gatek = gate_r.next()
                    P.op('dve', lambda e: e.scalar_tensor_tensor(out=IDXF[:], in0=ISEL[:, 0, :], scalar=128.0, in1=ISEL[:, 1, :], op0=ALU.mult, op1=ALU.add), reads=['ISEL'], writes=['IDXF'])
                    P.op('dve', lambda e: e.tensor_copy(out=IDX[:], in_=IDXF[:]), reads=['IDXF'], writes=[idxk])
                    P.op('dve', lambda e: e.tensor_tensor(out=GATE[:], in0=TS[:], in1=TS[:, :, 0:1].broadcast_to([128, 8, 16]), op=ALU.subtract), reads=tsk, writes=[gatek])
                    P.op('act', lambda e: e.activation(out=GATE[:], in_=GATE[:], func=AF.Exp), reads=[gatek], writes=[gatek])
                    P.op('dve', lambda e: e.tensor_reduce(out=gsum[:], in_=GATE[:], axis=AX.X, op=ALU.add), reads=[gatek], writes=['gsum'])
                    P.op('dve', lambda e: e.reciprocal(out=gsum[:], in_=gsum[:]), reads=['gsum'], writes=['gsum'])
                    P.op('dve', lambda e: e.tensor_tensor(out=GATE[:], in0=GATE[:], in1=gsum[:].unsqueeze(2).broadcast_to([128, 8, 16]), op=ALU.mult), reads=[gatek, 'gsum'], writes=[gatek])
                    return dict(X1t=X1t, x1k=x1k, H2t=H2t, h2k=h2k, IDX=IDX, idxk=idxk, GATE=GATE, gatek=gatek, t0=t0)

                def gather(tbl, c, hk):
                    ub, ubk = ub_r.next()
                    P.dma('pool', 'g%d' % (ub_r.i % NUB), lambda e: e.indirect_dma_start(
                        out=ub[:], out_offset=None, in_=tbl[:, :], in_offset=bass.IndirectOffsetOnAxis(ap=c['IDX'][:, hk:hk + 1], axis=0)),
                        reads=[c['idxk'], 'T16'], writes=[ubk])
                    return ub, ubk

                def uv_phase(c, mid_hook=None):
                    gflat = c['GATE'][:].rearrange("p h k -> p (h k)")
                    for g in range(16):
                        ubs = []
                        for j in range(8):
                            hk = g * 8 + j
                            ub, ubk = gather(UV16, c, hk)
                            jk, jkk = jk_r.next()
                            P.op('dve', lambda e, hk=hk, ub=ub, jk=jk: e.scalar_tensor_tensor(out=jk[:], in0=ub[:, 0:D], scalar=1.0, in1=c['H2t'][:], op0=ALU.mult, op1=ALU.mult,
                                                                                             accum_out=DOT[:, hk:hk + 1]), reads=[ubk, c['h2k']], writes=[jkk, 'DOT%d' % hk])
                            ubs.append((ub, ubk))
                        c0_, c1_ = g * 8, (g + 1) * 8
                        dks = ['DOT%d' % hk for hk in range(c0_, c1_)]
                        gk = 'GT%d' % g
                        t = GT_[:, c0_:c1_]
                        v = DOT[:, c0_:c1_]
                        P.op('dve', lambda e, t=t, v=v: e.tensor_tensor(out=t, in0=v, in1=v, op=ALU.mult), reads=dks, writes=[gk])
                        P.op('dve', lambda e, t=t: e.tensor_scalar(out=t, in0=t, scalar1=0.044715, scalar2=1.0, op0=ALU.mult, op1=ALU.add), reads=[gk], writes=[gk])
                        P.op('dve', lambda e, t=t, v=v: e.tensor_tensor(out=t, in0=t, in1=v, op=ALU.mult), reads=[gk] + dks, writes=[gk])
                        P.op('act', lambda e, t=t: e.activation(out=t, in_=t, func=AF.Sigmoid, scale=1.5957691216), reads=[gk], writes=[gk])
                        P.op('dve', lambda e, t=t, v=v: e.tensor_tensor(out=t, in0=t, in1=v, op=ALU.mult), reads=[gk] + dks, writes=[gk])
                        wk = 'WG%d' % g
                        P.op('dve', lambda e, t=t, c0_=c0_, c1_=c1_: e.tensor_tensor(out=WGT[:, c0_:c1_], in0=t, in1=gflat[:, c0_:c1_], op=ALU.mult), reads=[gk, c['gatek']], writes=[wk])
                        for j, (ub, ubk) in enumerate(ubs):
                            hk = g * 8 + j
                            dg, dgk = dg_r.next()
                            P.op('act', lambda e, hk=hk, dg=dg: e.activation(out=dg[:], in_=ident[:], func=AF.Copy, scale=WGT[:, hk:hk + 1]), reads=['ident', wk], writes=[dgk])
                            for half, (acc, ak) in enumerate(((accA, 'accA'), (accB, 'accB'))):
                                P.op('pe', lambda e, hk=hk, dg=dg, ub=ub, half=half, acc=acc: e.matmul(acc[:], lhsT=dg[:], rhs=ub[:, D + half * 512:D + (half + 1) * 512],
                                                                                                     start=(hk == 0), stop=(hk == 127)), reads=[dgk, ubk], writes=[ak])
                        if g == 3 and mid_hook is not None:
                            mid_hook()

                def fin_phase(c):
                    for half, (acc, ak) in enumerate(((accA, 'accA'), (accB, 'accB'))):
                        P.op('dve', lambda e, half=half, acc=acc: e.tensor_tensor(out=FIN[:, half * 512:(half + 1) * 512], in0=acc[:], in1=MOD['GT2'][:, half * 512:(half + 1) * 512], op=ALU.mult),
                             reads=[ak, 'mod_GT2'], writes=['FIN'])
                    P.op('dve', lambda e: e.tensor_tensor(out=FIN[:], in0=FIN[:], in1=c['X1t'][:], op=ALU.add), reads=['FIN', c['x1k']], writes=['FIN'])
                    rms_rstd(FIN[:], 'FIN', D, fin[:, 0:1], 'fin')
                    P.op('dve', lambda e: e.scalar_tensor_tensor(out=FIN[:], in0=FIN[:], scalar=fin[:, 0:1], in1=FG[:], op0=ALU.mult, op1=ALU.mult), reads=['FIN', 'fin', 'FG'], writes=['FIN'])
                    t0 = c['t0']
                    P.dma('sp', 'o0', lambda e: e.dma_start(out=out[t0:t0 + 128, :], in_=FIN[:]), reads=['FIN'], writes=['OUT'], defer=True)

                cur = prep(0)
                for i in range(NT):
                    box = {}

                    def hook(i=i, box=box):
                        box['n'] = prep(i + 1) if i + 1 < NT else None
                    uv_phase(cur, hook)
                    fin_phase(cur)
                    cur = box.get('n')
                P.barrier()
        if stop_after < 5:
            st_b.close()
        P.finish()
        nc._prog_nins = P.nins
    return nc


FULL_CFG = dict(rows=64, ctx=256, grp=512)
_NC_CACHE = {}


def make_in_maps(inputs, nb):
    f = lambda a: np.ascontiguousarray(np.asarray(a, dtype=np.float32))
    maps = []
    for b in range(nb):
        m = {
            "x": f(inputs['x'][b]), "ctxx": f(inputs['ctx'][b]),
            "cc": f(np.stack([np.asarray(inputs['c'][b]), np.asarray(inputs['c_ctx'])], 0)),
            "w_mod": f(inputs['w_mod'][0]), "b_mod": f(inputs['b_mod'][0][None, :]),
            "norm1_g": f(inputs['norm1_g'][0][None, :]), "norm2_g": f(inputs['norm2_g'][0][None, :]),
            "w_in": f(inputs['w_in'][0]), "rg_conv_w": f(inputs['rg_conv_w'][0]), "rg_conv_b": f(inputs['rg_conv_b'][0][None, :]),
            "rg_gate_w": f(inputs['rg_gate_w'][0]), "rg_gate_b": f(np.asarray(inputs['rg_gate_b'][0]).reshape(4, RGW)),
            "rg_lambda": f(inputs['rg_lambda'][0]), "gdn_conv_w": f(inputs['gdn_conv_w'][0]),
            "gdn_a_log": f(np.asarray(inputs['gdn_a_log'][0]).reshape(1, 8)), "gdn_dt_bias": f(np.asarray(inputs['gdn_dt_bias'][0]).reshape(1, 8)),
            "gdn_norm_g": f(inputs['gdn_norm_g'][0][None, :]), "w_out": f(inputs['w_out'][0]), "peer_wq": f(inputs['peer_wq'][0]),
            "peer_keys": f(inputs['peer_keys'][0]), "peer_u": f(inputs['peer_u'][0]), "peer_v": f(inputs['peer_v'][0]),
            "final_g": f(np.asarray(inputs['final_g'])[None, :]),
        }
        maps.append(m)
    return maps


def kernel(**inputs):
    nb = 8
    if 'full' not in _NC_CACHE:
        _NC_CACHE['full'] = build_nc(FULL_CFG)
    nc = _NC_CACHE['full']
    in_maps = make_in_maps(inputs, nb)
    res = run_bass_kernel_spmd(nc, in_maps, core_ids=list(range(nb)))
    return np.stack([np.asarray(r["out"], dtype=np.float32) for r in res.results], axis=0)
```

```python
import contextlib
import numpy as np
import concourse.bass as bass
import concourse.mybir as mybir
from concourse.bass_utils import run_bass_kernel_spmd

F32 = mybir.dt.float32
U32 = mybir.dt.uint32
BF16 = mybir.dt.bfloat16
AF = mybir.ActivationFunctionType
ALU = mybir.AluOpType
AX = mybir.AxisListType

D = 1024
KD = 8
EPS = 1e-6
RGW = 512
GDW = 512
INC = 3088
NEXP = 16384
BIG = 30000.0
NEG = -1.0e30


class _Rec:
    def dma_start(self, **kw):
        self.kw = kw
        return self


class Prog:
    def __init__(self, nc, st):
        self.nc = nc
        self.st = st
        self.eng = {'pe': nc.tensor, 'act': nc.scalar, 'dve': nc.vector, 'pool': nc.gpsimd, 'sp': nc.sync}
        self.cnt = {e: 0 for e in self.eng}
        self.sem = {}
        for e in ('pe', 'act', 'dve', 'pool'):
            self.sem[e] = st.enter_context(nc.semaphore('sem_' + e))
        self.dcnt = {}
        self.seen = {e: {} for e in self.eng}
        self.lastw = {}
        self.readers = {}
        self.nins = 0
        self.pending = []

    def flush(self):
        p, self.pending = self.pending, []
        for a in p:
            self.dma(*a)

    def _deps(self, eng, reads, writes):
        if self.pending:
            rs, ws = set(reads), set(writes)
            for (_, _, _, pr, pw) in self.pending:
                if (set(pr) & ws) or (set(pw) & (rs | ws)):
                    self.flush()
                    break
        need = {}

        def add(t):
            if t is None:
                return
            k, v = t
            if need.get(k, 0) < v:
                need[k] = v
        for k in reads:
            add(self.lastw.get(k))
        for k in writes:
            add(self.lastw.get(k))
            for t in self.readers.get(k, {}).items():
                add(t)
        e = self.eng[eng]
        for k, v in need.items():
            if k == 'pe' and eng == 'pe':
                continue
            if self.seen[eng].get(k, 0) >= v:
                continue
            self.seen[eng][k] = v
            e.wait_ge(self.sem[k], v)

    def _commit(self, t, reads, writes):
        for k in reads:
            r = self.readers.setdefault(k, {})
            if r.get(t[0], 0) < t[1]:
                r[t[0]] = t[1]
        for k in writes:
            self.lastw[k] = t
            self.readers[k] = {}

    def op(self, eng, fn, reads=(), writes=()):
        self._deps(eng, reads, writes)
        ins = fn(self.eng[eng])
        self.cnt[eng] += 1
        ins.then_inc(self.sem[eng], 1)
        self._commit((eng, self.cnt[eng]), reads, writes)
        self.nins += 1

    def dma(self, q, slot, fn, reads=(), writes=(), defer=False):
        if defer:
            rec = _Rec()
            fn(rec)
            kw = rec.kw
            self.pending.append((q, slot, (lambda e, kw=kw: e.dma_start(**kw)), tuple(reads), tuple(writes)))
            return
        key = 'd_' + slot
        if key not in self.sem:
            self.sem[key] = self.st.enter_context(self.nc.semaphore(key))
            self.dcnt[key] = 0
        self._deps(q, reads, writes)
        e = self.eng[q]
        if self.dcnt[key] > 0 and self.seen[q].get(key, 0) < self.dcnt[key]:
            self.seen[q][key] = self.dcnt[key]
            e.wait_ge(self.sem[key], self.dcnt[key])
        ins = fn(e)
        self.dcnt[key] += 16
        ins.then_inc(self.sem[key], 16)
        self._commit((key, self.dcnt[key]), reads, writes)
        self.nins += 1

    def barrier(self):
        self.flush()
        for e in ('pe', 'act', 'dve', 'pool', 'sp'):
            eo = self.eng[e]
            for k in ('pe', 'act', 'dve', 'pool'):
                if k != e and self.cnt[k] > self.seen[e].get(k, 0):
                    self.seen[e][k] = self.cnt[k]
                    eo.wait_ge(self.sem[k], self.cnt[k])
            for k, v in self.dcnt.items():
                if v > self.seen[e].get(k, 0):
                    self.seen[e][k] = v
                    eo.wait_ge(self.sem[k], v)

    def finish(self):
        self.flush()
        eo = self.eng['sp']
        for k, v in self.dcnt.items():
            if v > self.seen['sp'].get(k, 0):
                self.seen['sp'][k] = v
                eo.wait_ge(self.sem[k], v)


class Ring:
    def __init__(self, nc, st, name, shape, dtype, n, psum=False):
        self.items = []
        for i in range(n):
            nm = '%s%d' % (name, i)
            t = st.enter_context((nc.psum_tensor if psum else nc.sbuf_tensor)(nm, shape, dtype))
            self.items.append((t, nm))
        self.i = 0

    def next(self):
        it = self.items[self.i % len(self.items)]
        self.i += 1
        return it


def build_nc(cfg):
    ROWS = cfg['rows']
    L = ROWS * 64
    CTX = cfg['ctx']
    GRP = cfg['grp']
    stop_after = cfg.get('stop_after', 99)
    gdn_cut = cfg.get('gdn_cut', 99)
    p3pool = cfg.get('p3pool', 'pool')
    TT = CTX + L
    NGL = L // GRP
    NTG = GRP // 128
    NCC = CTX // 128
    NCL = L // 128
    CPT = 128 // ROWS if ROWS < 128 else 1
    assert ROWS <= 128 and 128 % ROWS == 0 and GRP % 128 == 0 and L % GRP == 0 and CTX % 128 == 0 and CTX <= GRP

    nc = bass.Bass("TRN2", target_bir_lowering=False)

    def din(name, shape):
        return nc.dram_tensor(name, shape, F32, kind="ExternalInput").ap()
    x = din("x", [L, D])
    ctxx = din("ctxx", [CTX, D])
    cc = din("cc", [2, D])
    w_mod = din("w_mod", [D, 6 * D])
    b_mod = din("b_mod", [1, 6 * D])
    norm1_g = din("norm1_g", [1, D])
    norm2_g = din("norm2_g", [1, D])
    w_in = din("w_in", [D, INC])
    rg_conv_w = din("rg_conv_w", [4, RGW])
    rg_conv_b = din("rg_conv_b", [1, RGW])
    rg_gate_w = din("rg_gate_w", [2, 2, 8, 64, 64])
    rg_gate_b = din("rg_gate_b", [4, RGW])
    rg_lambda = din("rg_lambda", [2, RGW])
    gdn_conv_w = din("gdn_conv_w", [4, 3 * GDW])
    gdn_a_log = din("gdn_a_log", [1, 8])
    gdn_dt_bias = din("gdn_dt_bias", [1, 8])
    gdn_norm_g = din("gdn_norm_g", [1, 128])
    w_out = din("w_out", [D, D])
    peer_wq = din("peer_wq", [D, 2048])
    peer_keys = din("peer_keys", [2, 128, 128])
    peer_u = din("peer_u", [NEXP, D])
    peer_v = din("peer_v", [NEXP, D])
    final_g = din("final_g", [1, D])
    out = nc.dram_tensor("out", [L, D], F32, kind="ExternalOutput").ap()

    def dscr(name, shape):
        return nc.dram_tensor(name, shape, F32).ap()
    XC = dscr("s_xc", [RGW, TT])
    GG = dscr("s_gg", [RGW, L])
    HF = dscr("s_hf", [RGW, L])
    YTR = dscr("s_ytr", [RGW, L])
    QKV = dscr("s_qkv", [3 * GDW, TT])
    ZS = dscr("s_zs", [L, GDW])
    OF = dscr("s_of", [L, GDW])
    YG = dscr("s_yg", [L, GDW])
    X1S = dscr("s_x1", [L, D])
    H2S = dscr("s_h2", [L, D])
    QTS = dscr("s_qt", [2048, L])
    UV16 = nc.dram_tensor("s_uv16", [NEXP, 2 * D], BF16).ap()

    x_cm = x.rearrange("(r c) f -> c r f", c=64)
    yg_cm = YG.rearrange("(r c) f -> c r f", c=64)

    with contextlib.ExitStack() as st:
        P = Prog(nc, st)

        def sb(name, shape, dtype=F32, stack=None):
            return (stack or st).enter_context(nc.sbuf_tensor(name, shape, dtype))
        psr = Ring(nc, st, 'psb', [128, 512], F32, 8, psum=True)

        ident = sb('ident', [128, 128])
        ones = sb('ones', [128, 128])
        noti = sb('noti', [128, 128])
        Mm = [sb('Mf', [128, 128]), sb('Mb', [128, 128])]
        BGm = [sb('BGf', [128, 128]), sb('BGb', [128, 128])]
        bigs = sb('bigs', [128, 128])
        P.op('pool', lambda e: e.memset(ones[:], 1.0), writes=['ones'])
        P.op('pool', lambda e: e.memset(bigs[:], BIG), writes=['bigs'])
        P.op('pool', lambda e: e.memset(ident[:], 0.0), writes=['ident'])
        P.op('pool', lambda e: e.affine_select(out=ident[:], in_=ident[:], pattern=[[-1, 128]], compare_op=ALU.not_equal,
                                               fill=1.0, base=0, channel_multiplier=1), reads=['ident'], writes=['ident'])
        P.op('pool', lambda e: e.affine_select(out=noti[:], in_=ones[:], pattern=[[-1, 128]], compare_op=ALU.not_equal,
                                               fill=0.0, base=0, channel_multiplier=1), reads=['ones'], writes=['noti'])
        P.op('pool', lambda e: e.affine_select(out=Mm[0][:], in_=ones[:], pattern=[[1, 128]], compare_op=ALU.is_ge,
                                               fill=0.0, base=0, channel_multiplier=-1), reads=['ones'], writes=['Mf'])
        P.op('pool', lambda e: e.affine_select(out=Mm[1][:], in_=ones[:], pattern=[[-1, 128]], compare_op=ALU.is_ge,
                                               fill=0.0, base=0, channel_multiplier=1), reads=['ones'], writes=['Mb'])
        P.op('pool', lambda e: e.affine_select(out=BGm[0][:], in_=bigs[:], pattern=[[1, 128]], compare_op=ALU.is_gt,
                                               fill=0.0, base=0, channel_multiplier=-1), reads=['bigs'], writes=['BGf'])
        P.op('pool', lambda e: e.affine_select(out=BGm[1][:], in_=bigs[:], pattern=[[-1, 128]], compare_op=ALU.is_gt,
                                               fill=0.0, base=0, channel_multiplier=1), reads=['bigs'], writes=['BGb'])
        MK = ['Mf', 'Mb']
        BK = ['BGf', 'BGb']

        junk = sb('junk', [128, D])
        ssr = Ring(nc, st, 'ss', [128, 2], F32, 2)
        hr = Ring(nc, st, 'hh', [128, D], F32, 2)
        MOD = {n: sb('mod_' + n, [128, D]) for n in ['GT2']}
        FG = sb('FG', [128, D])
        st_b = contextlib.ExitStack()
        st_b.__enter__()
        for n in ['GT1', 'SH2', 'G2']:
            MOD[n] = sb('mod_' + n, [128, D], stack=st_b)
        st_a = contextlib.ExitStack()
        st_a.__enter__()
        NXR = 4
        xr = Ring(nc, st_a, 'xt', [128, D], F32, NXR)
        for n in ['SH1', 'G1', 'CSH1', 'CG1']:
            MOD[n] = sb('mod_' + n, [128, D], stack=st_a)
        GBt = sb('GBt', [128, (CTX + L) // 128, 2, 2, 4], stack=st_a)
        P.dma('sp', 'c0', lambda e: e.dma_start(out=FG[:], in_=final_g[0:1, :].partition_broadcast(128)), writes=['FG'])

        with contextlib.ExitStack() as ph:
            cc2 = sb('cc2', [2, D], stack=ph)
            sc2 = sb('sc2', [2, D], stack=ph)
            scT = sb('scT', [128, KD, 2], stack=ph)
            scB = sb('scB', [128, KD, 2, 128], stack=ph)
            bmod = sb('bmod', [1, 6 * D], stack=ph)
            ng = sb('ng', [128, D], stack=ph)
            wmr = Ring(nc, ph, 'wm', [128, KD, 512], F32, 2)
            P.dma('sp', 'c0', lambda e: e.dma_start(out=cc2[:], in_=cc[:, :]), writes=['cc2'])
            P.dma('sp', 'c1', lambda e: e.dma_start(out=bmod[:], in_=b_mod[:, :]), writes=['bmod'])
            P.op('act', lambda e: e.activation(out=sc2[:], in_=cc2[:], func=AF.Silu), reads=['cc2'], writes=['sc2'])
            pt, pk = psr.next()
            for kc in range(KD):
                P.op('pe', lambda e, kc=kc: e.transpose(out=pt[:, kc * 2:kc * 2 + 2], in_=sc2[0:2, kc * 128:(kc + 1) * 128],
                                                         identity=ident[0:2, 0:2]), reads=['sc2', 'ident'], writes=[pk])
            P.op('dve', lambda e: e.tensor_copy(out=scT[:].rearrange("p k r -> p (k r)"), in_=pt[:, 0:2 * KD]), reads=[pk], writes=['scT'])
            for r in range(2):
                P.op('dve', lambda e, r=r: e.tensor_copy(out=scB[:, :, r, :], in_=scT[:, :, r:r + 1].broadcast_to([128, KD, 128])),
                     reads=['scT'], writes=['scB'])
            wmv = w_mod.rearrange("(k p) n -> p k n", p=128)
            order = [('SH1', 'CSH1'), ('G1', 'CG1'), ('GT1', None), ('SH2', None), ('G2', None), ('GT2', None)]
            for n in range(12):
                wt, wk = wmr.next()
                P.dma('sp', 'wm%d' % (n % 2), lambda e, n=n, wt=wt: e.dma_start(out=wt[:], in_=wmv[:, :, n * 512:(n + 1) * 512]), writes=[wk])
                for r in range(2):
                    dst = order[n // 2][r]
                    if dst is None:
                        continue
                    pt, pk = psr.next()
                    for kc in range(KD):
                        P.op('pe', lambda e, kc=kc, r=r, wt=wt, pt=pt: e.matmul(pt[:], lhsT=scB[:, kc, r, :], rhs=wt[:, kc, :], start=(kc == 0), stop=False),
                             reads=['scB', wk], writes=[pk])
                    P.op('pe', lambda e, n=n, pt=pt: e.matmul(pt[:], lhsT=ones[0:1, :], rhs=bmod[0:1, n * 512:(n + 1) * 512], start=False, stop=True),
                         reads=['ones', 'bmod'], writes=[pk])
                    P.op('act', lambda e, dst=dst, n=n, pt=pt: e.copy(out=MOD[dst][:, (n % 2) * 512:(n % 2 + 1) * 512], in_=pt[:]),
                         reads=[pk], writes=['mod_' + dst])
            for gsrc, names in ((norm1_g, ('G1', 'CG1')), (norm2_g, ('G2',))):
                P.dma('sp', 'c0', lambda e, gsrc=gsrc: e.dma_start(out=ng[:], in_=gsrc[0:1, :].partition_broadcast(128)), writes=['ng'])
                for nm in names:
                    P.op('dve', lambda e, nm=nm: e.scalar_tensor_tensor(out=MOD[nm][:], in0=MOD[nm][:], scalar=1.0, in1=ng[:], op0=ALU.add, op1=ALU.mult),
                         reads=['mod_' + nm, 'ng'], writes=['mod_' + nm])
            P.barrier()


        def rms_rstd(src, skey, width, dstcol, dkey, np_=128):
            P.op('act', lambda e: e.activation(out=junk[0:np_, 0:width], in_=src, func=AF.Square, accum_out=dstcol),
                 reads=[skey], writes=['junk', dkey])
            P.op('act', lambda e: e.activation(out=dstcol, in_=dstcol, func=AF.Sqrt, scale=1.0 / width, bias=EPS),
                 reads=[dkey], writes=[dkey])
            P.op('dve', lambda e: e.reciprocal(out=dstcol, in_=dstcol), reads=[dkey], writes=[dkey])

        def norm_mod(xt, xk, gname, shname, np_=128):
            s_, sk = ssr.next()
            rms_rstd(xt[0:np_, :], xk, D, s_[0:np_, 0:1], sk, np_=np_)
            h, hk = hr.next()
            P.op('dve', lambda e: e.scalar_tensor_tensor(out=h[0:np_, :], in0=xt[0:np_, :], scalar=s_[0:np_, 0:1], in1=MOD[gname][0:np_, :],
                                                         op0=ALU.mult, op1=ALU.mult), reads=[xk, sk, 'mod_' + gname], writes=[hk])
            P.op('pool', lambda e: e.tensor_tensor(out=h[0:np_, :], in0=h[0:np_, :], in1=MOD[shname][0:np_, :], op=ALU.add),
                 reads=[hk, 'mod_' + shname], writes=[hk])
            return h, hk

        def transpose_to(h, hk, hT, hTk, tok0, np_=128, flip=0):
            for half in range(2):
                pt, pk = psr.next()
                for j in range(4):
                    kc = half * 4 + j
                    P.op('pe', lambda e, j=j, kc=kc, pt=pt: e.transpose(out=pt[:, j * np_:(j + 1) * np_], in_=h[0:np_, kc * 128:(kc + 1) * 128],
                                                                       identity=ident[0:np_, 0:np_]), reads=[hk, 'ident'], writes=[pk])
                eng = 'act' if (half + flip) % 2 == 0 else 'dve'
                src = pt[:, 0:4 * np_].rearrange("p (k t) -> p k t", k=4)
                dst = hT[:, half * 4:half * 4 + 4, tok0:tok0 + np_]
                if eng == 'act':
                    P.op('act', lambda e, src=src, dst=dst: e.copy(out=dst, in_=src), reads=[pk], writes=[hTk])
                else:
                    P.op('dve', lambda e, src=src, dst=dst: e.tensor_copy(out=dst, in_=src), reads=[pk], writes=[hTk])

        def gelu_inplace_g(buf, key, nchk, width, tmp, tkey, sq_eng='pool'):
            v = buf[:, 0:nchk, 0:width]
            t = tmp[:, 0:nchk, 0:width]
            P.op(sq_eng, lambda e: e.tensor_tensor(out=t, in0=v, in1=v, op=ALU.mult), reads=[key], writes=[tkey])
            P.op('dve', lambda e: e.tensor_scalar(out=t, in0=t, scalar1=0.044715, scalar2=1.0, op0=ALU.mult, op1=ALU.add), reads=[tkey], writes=[tkey])
            P.op('dve', lambda e: e.tensor_tensor(out=t, in0=t, in1=v, op=ALU.mult), reads=[tkey, key], writes=[tkey])
            P.op('act', lambda e: e.activation(out=t, in_=t, func=AF.Sigmoid, scale=1.5957691216), reads=[tkey], writes=[tkey])
            P.op('dve', lambda e: e.tensor_tensor(out=v, in0=v, in1=t, op=ALU.mult), reads=[tkey, key], writes=[key])


        def load_cols(dst, src2d, key, slot='c0'):
            P.dma('sp', slot, lambda e: e.dma_start(out=dst, in_=src2d.rearrange("n p -> p n"), allow_slow_non_contiguous=True), writes=[key])

        def seq_tile_loads(seq, order, i, xt, xk):
            if seq == 'ctx':
                P.dma('sp', 'x%d' % (xr.i % NXR), lambda e: e.dma_start(out=xt[:], in_=ctxx[i * 128:(i + 1) * 128, :]), writes=[xk])
            elif order == 'raster':
                P.dma('sp', 'x%d' % (xr.i % NXR), lambda e: e.dma_start(out=xt[:], in_=x[i * 128:(i + 1) * 128, :]), writes=[xk])
            else:
                ncol = 128 // ROWS
                for ci in range(ncol):
                    col = i * ncol + ci
                    P.dma('sp', 'x%d' % (xr.i % NXR), lambda e, ci=ci, col=col: e.dma_start(out=xt[ci * ROWS:(ci + 1) * ROWS, :], in_=x_cm[col]), writes=[xk])

        with contextlib.ExitStack() as ph:
            WIN = sb('WIN', [128, KD, 2064], stack=ph)
            hT = sb('hT', [128, KD, GRP], stack=ph)
            hTb = sb('hTb', [128, KD, 16], stack=ph)
            PTb = sb('PTb', [128, 12, 16], stack=ph)
            PT = sb('PT', [128, 4, GRP + 3], stack=ph)
            CV = sb('CV', [128, 4, GRP], stack=ph)
            SQ = sb('SQ', [128, 4, GRP], stack=ph)
            RS = sb('RS', [128, GRP], stack=ph)
            GEL = sb('GEL', [128, 4, GRP], stack=ph)
            HALO = sb('HALO', [128, 12, 2], stack=ph)
            cw = sb('cw', [128, 12, 4], stack=ph)
            cb = sb('cb', [128, 4], stack=ph)
            zt = sb('zt', [128, GDW], stack=ph)
            abc = sb('abc', [128, 3, 8], stack=ph)
            abt = sb('abt', [128, 16], stack=ph)
            xb = sb('xb', [16, D], stack=ph)
            w_in_v = w_in.rearrange("(k p) n -> p k n", p=128)

            def inproj_fm(c0col, nch, width, rhsT, rhsk, dst_fn, dkey):
                for c in range(nch):
                    pt, pk = psr.next()
                    for kc in range(KD):
                        P.op('pe', lambda e, c=c, kc=kc, pt=pt: e.matmul(pt[:, 0:width], lhsT=WIN[:, kc, c0col + c * 128:c0col + (c + 1) * 128],
                                                                           rhs=rhsT[:, kc, 0:width], start=(kc == 0), stop=(kc == KD - 1)),
                             reads=['WIN', rhsk], writes=[pk])
                    if c % 2 == 0:
                        P.op('act', lambda e, c=c, pt=pt: e.copy(out=dst_fn(c), in_=pt[:, 0:width]), reads=[pk], writes=[dkey])
                    else:
                        P.op('dve', lambda e, c=c, pt=pt: e.tensor_copy(out=dst_fn(c), in_=pt[:, 0:width]), reads=[pk], writes=[dkey])

            def boundary(tokens, c0col, nch):
                nb = len(tokens)
                if nb == 0:
                    return
                for i_, t in enumerate(tokens):
                    P.dma('sp', 'c0', lambda e, i_=i_, t=t: e.dma_start(out=xb[i_:i_ + 1, :], in_=x[t:t + 1, :]), writes=['xb'])
                h, hk = norm_mod(xb, 'xb', 'G1', 'SH1', np_=nb)
                transpose_to(h, hk, hTb, 'hTb', 0, np_=nb)
                inproj_fm(c0col, nch, nb, hTb, 'hTb', lambda c: PTb[:, c, 0:nb], 'PTb')

            def conv4(nchk, cwoff, bias, width):
                for c in range(nchk):
                    eng = 'dve' if c % 2 == 0 else 'pool'
                    if bias:
                        P.op(eng, lambda e, c=c: e.tensor_scalar(out=CV[:, c, 0:width], in0=PT[:, c, 0:width], scalar1=cw[:, cwoff + c, 0:1], scalar2=cb[:, c:c + 1],
                                                                 op0=ALU.mult, op1=ALU.add), reads=['PT', 'cw', 'cb'], writes=['CV%d' % c])
                    else:
                        P.op(eng, lambda e, c=c: e.tensor_scalar(out=CV[:, c, 0:width], in0=PT[:, c, 0:width], scalar1=cw[:, cwoff + c, 0:1], scalar2=None,
                                                                 op0=ALU.mult), reads=['PT', 'cw'], writes=['CV%d' % c])
                    for k in range(1, 4):
                        P.op('dve', lambda e, c=c, k=k: e.scalar_tensor_tensor(out=CV[:, c, 0:width], in0=PT[:, c, k:k + width], scalar=cw[:, cwoff + c, k:k + 1],
                                                                               in1=CV[:, c, 0:width], op0=ALU.mult, op1=ALU.add),
                             reads=['PT', 'cw', 'CV%d' % c], writes=['CV%d' % c])

            def gelu_inplace(buf, key, nchk, width, tmp, tkey):
                v = buf[:, 0:nchk, 0:width]
                t = tmp[:, 0:nchk, 0:width]
                P.op('pool', lambda e: e.tensor_tensor(out=t, in0=v, in1=v, op=ALU.mult), reads=[key], writes=[tkey])
                P.op('dve', lambda e: e.tensor_scalar(out=t, in0=t, scalar1=0.044715, scalar2=1.0, op0=ALU.mult, op1=ALU.add), reads=[tkey], writes=[tkey])
                P.op('dve', lambda e: e.tensor_tensor(out=t, in0=t, in1=v, op=ALU.mult), reads=[tkey, key], writes=[tkey])
                P.op('act', lambda e: e.activation(out=t, in_=t, func=AF.Sigmoid, scale=1.5957691216), reads=[tkey], writes=[tkey])
                P.op('dve', lambda e: e.tensor_tensor(out=v, in0=v, in1=t, op=ALU.mult), reads=[tkey, key], writes=[key])

            P.dma('sp', 'w0', lambda e: e.dma_start(out=WIN[:, :, 0:1024], in_=w_in_v[:, :, 0:1024]), writes=['WIN'])
            for k in range(4):
                load_cols(cw[:, 0:4, k], rg_conv_w[k:k + 1, :].rearrange("o (c p) -> (o c) p", p=128), 'cw')
            load_cols(cb[:, 0:4], rg_conv_b[0:1, :].rearrange("o (c p) -> (o c) p", p=128), 'cb')
            rg_bt = [g * GRP for g in range(1, NGL)]
            boundary(rg_bt, 0, 4)
            xc_v = XC.rearrange("(c p) t -> p c t", p=128)
            gg_v = GG.rearrange("(c p) t -> p c t", p=128)

            def p1_rg_group(seq, g):
                width = CTX if seq == 'ctx' else GRP
                ntile = width // 128
                seqoff = 0 if seq == 'ctx' else CTX
                t0 = g * GRP
                tl = []
                for i in range(ntile):
                    xt, xk = xr.next()
                    seq_tile_loads(seq, 'raster', g * NTG + i, xt, xk)
                    tl.append((xt, xk))
                P.flush()
                for i in range(ntile):
                    xt, xk = tl[i]
                    h, hk = norm_mod(xt, xk, 'CG1' if seq == 'ctx' else 'G1', 'CSH1' if seq == 'ctx' else 'SH1')
                    transpose_to(h, hk, hT, 'hT', i * 128, flip=i)
                if g == 0:
                    P.op('pool', lambda e: e.memset(PT[:, :, 0:2], 0.0), writes=['PT'])
                else:
                    P.op('pool', lambda e: e.tensor_copy(out=PT[:, :, 0:2], in_=PT[:, :, GRP:GRP + 2]), reads=['PT'], writes=['PT'])
                inproj_fm(0, 4, width, hT, 'hT', lambda c: PT[:, c, 2:2 + width], 'PT')
                if seq == 'ctx' or g == NGL - 1:
                    P.op('pool', lambda e: e.memset(PT[:, :, 2 + width:3 + width], 0.0), writes=['PT'])
                else:
                    P.op('pool', lambda e: e.tensor_copy(out=PT[:, :, 2 + width:3 + width], in_=PTb[:, 0:4, g:g + 1]), reads=['PTb'], writes=['PT'])
                conv4(4, 0, True, width)
                P.dma('sp', 'st0', lambda e: e.dma_start(out=xc_v[:, :, seqoff + t0:seqoff + t0 + width], in_=CV[:, :, 0:width]),
                      reads=['CV0', 'CV1', 'CV2', 'CV3'], writes=['XC'], defer=True)
                if seq == 'lat':
                    inproj_fm(512, 4, width, hT, 'hT', lambda c: SQ[:, c, 0:width], 'SQ')
                    gelu_inplace(SQ, 'SQ', 4, width, GEL, 'GEL')
                    P.dma('sp', 'st1', lambda e: e.dma_start(out=gg_v[:, :, t0:t0 + width], in_=SQ[:, :, 0:width]), reads=['SQ'], writes=['GG'], defer=True)

            def touch_cv():
                pass

            p1_rg_group('ctx', 0)
            for g in range(NGL):
                p1_rg_group('lat', g)

            if stop_after >= 2:
                P.dma('sp', 'w0', lambda e: e.dma_start(out=WIN[:, :, 0:2064], in_=w_in_v[:, :, 1024:3088]), reads=['WIN'], writes=['WIN'])
                for k in range(4):
                    load_cols(cw[:, 0:12, k], gdn_conv_w[k:k + 1, :].rearrange("o (c p) -> (o c) p", p=128), 'cw')
                P.dma('sp', 'c0', lambda e: e.dma_start(out=abc[:, 0, :], in_=gdn_dt_bias[0:1, :].partition_broadcast(128)), writes=['abc'])
                P.dma('sp', 'c0', lambda e: e.dma_start(out=abc[:, 1, :], in_=gdn_a_log[0:1, :].partition_broadcast(128)), writes=['abc'])
                P.op('act', lambda e: e.activation(out=abc[:, 1, :], in_=abc[:, 1, :], func=AF.Exp), reads=['abc'], writes=['abc'])
                P.op('dve', lambda e: e.tensor_scalar(out=abc[:, 1, :], in0=abc[:, 1, :], scalar1=-1.0, scalar2=None, op0=ALU.mult), reads=['abc'], writes=['abc'])
                gd_bt = [(g * GRP) // ROWS for g in range(1, NGL)]
                boundary(gd_bt, 0, 12)
                qkv_v = QKV.rearrange("(c p) t -> p c t", p=128)

                def p1_gdn_group(seq, g):
                    width = CTX if seq == 'ctx' else GRP
                    ntile = width // 128
                    seqoff = 0 if seq == 'ctx' else CTX
                    j0 = g * GRP
                    tl = []
                    for i in range(ntile):
                        xt, xk = xr.next()
                        seq_tile_loads(seq, 'cm', g * NTG + i, xt, xk)
                        tl.append((xt, xk))
                    P.flush()
                    for i in range(ntile):
                        xt, xk = tl[i]
                        h, hk = norm_mod(xt, xk, 'CG1' if seq == 'ctx' else 'G1', 'CSH1' if seq == 'ctx' else 'SH1')
                        transpose_to(h, hk, hT, 'hT', i * 128, flip=i)
                    for part in range(3):
                        if g == 0:
                            P.op('pool', lambda e: e.memset(PT[:, :, 0:2], 0.0), writes=['PT'])
                        else:
                            P.op('pool', lambda e, part=part: e.tensor_copy(out=PT[:, :, 0:2], in_=HALO[:, part * 4:part * 4 + 4, :]), reads=['HALO'], writes=['PT'])
                        inproj_fm(part * 512, 4, width, hT, 'hT', lambda c: PT[:, c, 2:2 + width], 'PT')
                        P.op('pool', lambda e, part=part: e.tensor_copy(out=HALO[:, part * 4:part * 4 + 4, :], in_=PT[:, :, width:width + 2]), reads=['PT'], writes=['HALO'])
                        if seq == 'ctx' or g == NGL - 1:
                            P.op('pool', lambda e: e.memset(PT[:, :, 2 + width:3 + width], 0.0), writes=['PT'])
                        else:
                            P.op('pool', lambda e, part=part: e.tensor_copy(out=PT[:, :, 2 + width:3 + width], in_=PTb[:, part * 4:part * 4 + 4, g:g + 1]), reads=['PTb'], writes=['PT'])
                        conv4(4, part * 4, False, width)
                        cvk = ['CV0', 'CV1', 'CV2', 'CV3']
                        P.op('act', lambda e: e.activation(out=CV[:, :, 0:width], in_=CV[:, :, 0:width], func=AF.Silu), reads=cvk, writes=cvk)
                        if part < 2:
                            P.op('pool', lambda e: e.tensor_tensor(out=SQ[:, :, 0:width], in0=CV[:, :, 0:width], in1=CV[:, :, 0:width], op=ALU.mult), reads=cvk, writes=['SQ'])
                            for c in range(4):
                                pt, pk = psr.next()
                                P.op('pe', lambda e, c=c, pt=pt: e.matmul(pt[:, 0:width], lhsT=ones[:], rhs=SQ[:, c, 0:width], start=True, stop=True), reads=['ones', 'SQ'], writes=[pk])
                                P.op('act', lambda e, pt=pt: e.activation(out=RS[:, 0:width], in_=pt[:, 0:width], func=AF.Sqrt, bias=EPS), reads=[pk], writes=['RS'])
                                P.op('dve', lambda e: e.reciprocal(out=RS[:, 0:width], in_=RS[:, 0:width]), reads=['RS'], writes=['RS'])
                                sc_ = (128.0 ** -0.5) if part == 0 else 1.0
                                P.op('dve', lambda e, c=c, sc_=sc_: e.scalar_tensor_tensor(out=CV[:, c, 0:width], in0=CV[:, c, 0:width], scalar=sc_, in1=RS[:, 0:width], op0=ALU.mult, op1=ALU.mult),
                                     reads=['CV%d' % c, 'RS'], writes=['CV%d' % c])
                        P.dma('sp', 'st0', lambda e, part=part: e.dma_start(out=qkv_v[:, part * 4:part * 4 + 4, seqoff + j0:seqoff + j0 + width], in_=CV[:, :, 0:width]),
                              reads=cvk, writes=['QKV'], defer=True)
                    for i in range(ntile):
                        ci = (g * NTG + i) + (0 if seq == 'ctx' else NCC)
                        if seq == 'lat':
                            pt, pk = psr.next()
                            for kc in range(KD):
                                P.op('pe', lambda e, kc=kc, i=i, pt=pt: e.matmul(pt[:, 0:GDW], lhsT=hT[:, kc, i * 128:(i + 1) * 128], rhs=WIN[:, kc, 1536:2048],
                                                                                 start=(kc == 0), stop=(kc == KD - 1)), reads=['hT', 'WIN'], writes=[pk])
                            P.op('act', lambda e, pt=pt: e.activation(out=zt[:], in_=pt[:, 0:GDW], func=AF.Silu), reads=[pk], writes=['zt'])
                            P.dma('sp', 'st1', lambda e, i=i: e.dma_start(out=ZS[j0 + i * 128:j0 + (i + 1) * 128, :], in_=zt[:]), reads=['zt'], writes=['ZS'], defer=True)
                        pt, pk = psr.next()
                        for kc in range(KD):
                            P.op('pe', lambda e, kc=kc, i=i, pt=pt: e.matmul(pt[:, 0:16], lhsT=hT[:, kc, i * 128:(i + 1) * 128], rhs=WIN[:, kc, 2048:2064],
                                                                             start=(kc == 0), stop=(kc == KD - 1)), reads=['hT', 'WIN'], writes=[pk])
                        pv = pt[:, 0:16].rearrange("p (d a h) -> p d a h", d=2, a=2)
                        av = abc[:, 2, :].rearrange("p (d h) -> p d h", d=2)
                        dtb = abc[:, 0, :].rearrange("p (d h) -> p d h", d=2)
                        nea = abc[:, 1, :].rearrange("p (d h) -> p d h", d=2)
                        P.op('dve', lambda e, pv=pv, av=av, dtb=dtb: e.tensor_tensor(out=av, in0=pv[:, :, 0, :], in1=dtb, op=ALU.add), reads=[pk, 'abc'], writes=['abc2'])
                        P.op('act', lambda e, av=av: e.activation(out=av, in_=av, func=AF.Exp), reads=['abc2'], writes=['abc2'])
                        P.op('act', lambda e, av=av: e.activation(out=av, in_=av, func=AF.Ln, bias=1.0), reads=['abc2'], writes=['abc2'])
                        P.op('dve', lambda e, av=av, nea=nea, ci=ci: e.tensor_tensor(out=GBt[:, ci, :, 0, :], in0=av, in1=nea, op=ALU.mult), reads=['abc2', 'abc'], writes=['GBt'])
                        bv = abt[:, 0:8].rearrange("p (d h) -> p d h", d=2)
                        P.op('dve', lambda e, pv=pv, bv=bv: e.tensor_copy(out=bv, in_=pv[:, :, 1, :]), reads=[pk], writes=['abt'])
                        P.op('act', lambda e, bv=bv, ci=ci: e.activation(out=GBt[:, ci, :, 1, :], in_=bv, func=AF.Sigmoid), reads=['abt'], writes=['GBt'])

                p1_gdn_group('ctx', 0)
                for g in range(NGL):
                    p1_gdn_group('lat', g)
            P.barrier()

        if stop_after >= 3:
            with contextlib.ExitStack() as ph:
                GW = sb('GW', [128, 2, 2, 4, 128], stack=ph)
                gbv = sb('gbv', [128, 16], stack=ph)
                nsp = sb('nsp', [128, 8], stack=ph)
                carry = sb('carry', [128, 4], stack=ph)
                xct_r = Ring(nc, ph, 'XCt', [128, 4, GRP], F32, 2)
                Rg = sb('Rg', [128, 4, GRP], stack=ph)
                Ig = sb('Ig', [128, 4, GRP], stack=ph)
                Ag = sb('Ag', [128, 4, GRP], stack=ph)
                Bg = sb('Bg', [128, 4, GRP], stack=ph)
                Hg = sb('Hg', [128, 4, GRP], stack=ph)
                hft_r = Ring(nc, ph, 'HFt', [128, 4, GRP], F32, 2)
                ggt_r = Ring(nc, ph, 'GGt', [128, 4, GRP], F32, 2)
                P.op('pool', lambda e: e.memset(GW[:].rearrange("p a b c f -> p (a b c f)"), 0.0), writes=['GW'])
                for d_ in range(2):
                    for gi in range(2):
                        for n in range(8):
                            c, hb = n // 2, n % 2
                            P.dma('sp', 'c%d' % (n % 2), lambda e, d_=d_, gi=gi, n=n, c=c, hb=hb: e.dma_start(
                                out=GW[hb * 64:(hb + 1) * 64, d_, gi, c, hb * 64:(hb + 1) * 64], in_=rg_gate_w[d_, gi, n]), writes=['GW'])
                load_cols(gbv[:, :], rg_gate_b.rearrange("a (c p) -> (a c) p", p=128), 'gbv')
                load_cols(nsp[:, :], rg_lambda.rearrange("a (c p) -> (a c) p", p=128), 'nsp')
                P.op('act', lambda e: e.activation(out=nsp[:], in_=nsp[:], func=AF.Exp, scale=-1.0), reads=['nsp'], writes=['nsp'])
                P.op('act', lambda e: e.activation(out=nsp[:], in_=nsp[:], func=AF.Ln, bias=1.0), reads=['nsp'], writes=['nsp'])
                P.op('dve', lambda e: e.tensor_scalar(out=nsp[:], in0=nsp[:], scalar1=-8.0, scalar2=None, op0=ALU.mult), reads=['nsp'], writes=['nsp'])
                hf_v = HF.rearrange("(c p) t -> p c t", p=128)
                ytr_v = YTR.rearrange("(c p) t -> p c t", p=128)

                def rg_group(d_, seq, g):
                    width = CTX if seq == 'ctx' else GRP
                    seqoff = 0 if seq == 'ctx' else CTX
                    t0 = g * GRP
                    XCt, xctk = xct_r.next()
                    P.dma('sp', 'l%d' % (xct_r.i % 2), lambda e: e.dma_start(out=XCt[:, :, 0:width], in_=xc_v[:, :, seqoff + t0:seqoff + t0 + width]), reads=['XC'], writes=[xctk])
                    if seq == 'lat' and d_ == 1:
                        HFt, hftk = hft_r.next()
                        GGt, ggtk = ggt_r.next()
                        P.dma('sp', 'l%d' % (2 + hft_r.i % 2), lambda e: e.dma_start(out=HFt[:, :, 0:width], in_=hf_v[:, :, t0:t0 + width]), reads=['HF'], writes=[hftk])
                        P.dma('sp', 'm%d' % (ggt_r.i % 2), lambda e: e.dma_start(out=GGt[:, :, 0:width], in_=gg_v[:, :, t0:t0 + width]), reads=['GG'], writes=[ggtk])
                    P.flush()
                    for gi, dstb, dk in ((0, Rg, 'Rg'), (1, Ig, 'Ig')):
                        for c in range(4):
                            pt, pk = psr.next()
                            P.op('pe', lambda e, gi=gi, c=c, pt=pt: e.matmul(pt[:, 0:width], lhsT=GW[:, d_, gi, c, :], rhs=XCt[:, c, 0:width], start=True, stop=True),
                                 reads=['GW', xctk], writes=[pk])
                            col = (d_ * 2 + gi) * 4 + c
                            P.op('act', lambda e, c=c, pt=pt, dstb=dstb, col=col: e.activation(out=dstb[:, c, 0:width], in_=pt[:, 0:width], func=AF.Sigmoid, bias=gbv[:, col:col + 1]),
                                 reads=[pk, 'gbv'], writes=[dk])
                    for c in range(4):
                        P.op('act', lambda e, c=c: e.activation(out=Ag[:, c, 0:width], in_=Rg[:, c, 0:width], func=AF.Exp, scale=nsp[:, d_ * 4 + c:d_ * 4 + c + 1]),
                             reads=['Rg', 'nsp'], writes=['Ag'])
                    av_ = Ag[:, :, 0:width]
                    P.op('pool', lambda e: e.tensor_tensor(out=Rg[:, :, 0:width], in0=av_, in1=av_, op=ALU.mult), reads=['Ag'], writes=['Rg'])
                    P.op('act', lambda e: e.activation(out=Rg[:, :, 0:width], in_=Rg[:, :, 0:width], func=AF.Sqrt, scale=-1.0, bias=1.0), reads=['Rg'], writes=['Rg'])
                    P.op('dve', lambda e: e.tensor_tensor(out=Bg[:, :, 0:width], in0=Ig[:, :, 0:width], in1=XCt[:, :, 0:width], op=ALU.mult), reads=['Ig', xctk], writes=['Bg'])
                    P.op('dve', lambda e: e.tensor_tensor(out=Bg[:, :, 0:width], in0=Bg[:, :, 0:width], in1=Rg[:, :, 0:width], op=ALU.mult), reads=['Bg', 'Rg'], writes=['Bg'])
                    for c in range(4):
                        if d_ == 0:
                            o_, a_, b_ = Hg[:, c, 0:width], Ag[:, c, 0:width], Bg[:, c, 0:width]
                        else:
                            o_, a_, b_ = Hg[:, c, width - 1::-1] if False else Hg[:, c, 0:width][:, ::-1], Ag[:, c, 0:width][:, ::-1], Bg[:, c, 0:width][:, ::-1]
                        P.op('dve', lambda e, c=c, o_=o_, a_=a_, b_=b_: e.tensor_tensor_scan(out=o_, data0=a_, data1=b_, initial=carry[:, c:c + 1], op0=ALU.mult, op1=ALU.add),
                             reads=['Ag', 'Bg', 'carry'], writes=['Hg'])
                    last = width - 1 if d_ == 0 else 0
                    P.op('dve', lambda e: e.tensor_copy(out=carry[:, :], in_=Hg[:, :, last]), reads=['Hg'], writes=['carry'])
                    if seq == 'lat' and d_ == 0:
                        P.dma('sp', 'st0', lambda e: e.dma_start(out=hf_v[:, :, t0:t0 + width], in_=Hg[:, :, 0:width]), reads=['Hg'], writes=['HF'], defer=True)
                    if seq == 'lat' and d_ == 1:
                        P.op('pool', lambda e: e.tensor_tensor(out=Hg[:, :, 0:width], in0=Hg[:, :, 0:width], in1=HFt[:, :, 0:width], op=ALU.add), reads=['Hg', hftk], writes=['Hg'])
                        P.op('dve', lambda e: e.tensor_tensor(out=Hg[:, :, 0:width], in0=Hg[:, :, 0:width], in1=GGt[:, :, 0:width], op=ALU.mult), reads=['Hg', ggtk], writes=['Hg'])
                        P.dma('sp', 'st1', lambda e: e.dma_start(out=ytr_v[:, :, t0:t0 + width], in_=Hg[:, :, 0:width]), reads=['Hg'], writes=['YTR'], defer=True)

                for d_ in range(2):
                    P.op('pool', lambda e: e.memset(carry[:], 0.0), writes=['carry'])
                    rg_group(d_, 'ctx', 0)
                    for g in (range(NGL) if d_ == 0 else range(NGL - 1, -1, -1)):
                        rg_group(d_, 'lat', g)
                P.barrier()

        if stop_after >= 4:
            with contextlib.ExitStack() as ph:
                qk_r = Ring(nc, ph, 'qkvt', [128, 12, 128], F32, 2)
                Sst = sb('Sst', [128, 4, 128], stack=ph)
                S1 = sb('S1', [128, 4, 128], stack=ph)
                SCg = sb('SCg', [128, 8], stack=ph)
                E1_r = Ring(nc, ph, 'E1', [128, 8], F32, 2)
                KDs = sb('KDs', [128, 4], stack=ph)
                NBt = sb('NBt', [128, 4], stack=ph)
                BGt = sb('BGt', [128, 4], stack=ph)
                GBC = sb('GBC', [128, 4, 128], stack=ph)
                Dm = sb('Dm', [128, 4, 128], stack=ph)
                NBN = sb('NBN', [128, 4, 128], stack=ph)
                T1 = sb('T1', [128, 4, 128], stack=ph)
                ATT = sb('ATT', [128, 4, 128], stack=ph)
                ATTT_r = Ring(nc, ph, 'ATTT', [128, 4, 128], F32, 2)
                XYr = Ring(nc, ph, 'XY', [128, 4, 256], F32, 2)
                TTr = Ring(nc, ph, 'TTm', [128, 4, 128], F32, 2)
                VB = sb('VB', [128, 4, 128], stack=ph)
                KBG = sb('KBG', [128, 4, 128], stack=ph)
                KDEC_r = Ring(nc, ph, 'KDEC', [128, 4, 128], F32, 2)
                WV_r = Ring(nc, ph, 'WV', [128, 4, 128], F32, 2)
                KCT_r = Ring(nc, ph, 'KCT', [128, 4, 128], F32, 2)
                VN = sb('VN', [128, 4, 128], stack=ph)
                O1 = sb('O1', [128, 4, 128], stack=ph)
                Ot = sb('Ot', [128, 4, 128], stack=ph)
                oft_r = Ring(nc, ph, 'OFt', [128, 4, 128], F32, 2)
                zst_r = Ring(nc, ph, 'ZSt', [128, 4, 128], F32, 2)
                NGb = sb('NGb', [128, 128], stack=ph)
                rsd = sb('rsd', [128, 4], stack=ph)
                P.dma('sp', 'c0', lambda e: e.dma_start(out=NGb[:], in_=gdn_norm_g[0:1, :].partition_broadcast(128)), writes=['NGb'])

                def bc_h(col4):
                    return col4[:, :, None].broadcast_to([128, 4, 128]) if False else col4.unsqueeze(2).broadcast_to([128, 4, 128])

                def bc_m(m):
                    return m.unsqueeze(1).broadcast_to([128, 4, 128])

                def mm4(lhs_fn, rhs_fn, reads, width=128):
                    pt, pk = psr.next()
                    for h in range(4):
                        P.op('pe', lambda e, h=h, pt=pt: e.matmul(pt[:, h * 128:(h + 1) * 128], lhsT=lhs_fn(h), rhs=rhs_fn(h), start=True, stop=True), reads=reads, writes=[pk])
                    return pt[:, :].rearrange("p (h f) -> p h f", h=4), pk

                def tr4(src_fn, reads):
                    pt, pk = psr.next()
                    for h in range(4):
                        P.op('pe', lambda e, h=h, pt=pt: e.transpose(out=pt[:, h * 128:(h + 1) * 128], in_=src_fn(h), identity=ident[:]), reads=reads + ['ident'], writes=[pk])
                    return pt[:, :].rearrange("p (h f) -> p h f", h=4), pk

                def gdn_chunk(d_, seq, cidx):
                    ci = cidx + (0 if seq == 'ctx' else NCC)
                    E1, e1k = E1_r.next()
                    ATTT, atttk = ATTT_r.next()
                    KDEC, kdeck = KDEC_r.next()
                    WV, wvk = WV_r.next()
                    KCT, kctk = KCT_r.next()
                    j0 = cidx * 128
                    seqoff = 0 if seq == 'ctx' else CTX
                    qt, qk_ = qk_r.next()
                    P.dma('sp', 'l%d' % (qk_r.i % 2), lambda e: e.dma_start(out=qt[:], in_=qkv_v[:, :, seqoff + j0:seqoff + j0 + 128]), reads=['QKV'], writes=[qk_])
                    if seq == 'lat' and d_ == 1:
                        OFt, oftk = oft_r.next()
                        ZSt, zstk = zst_r.next()
                        P.dma('sp', 'l%d' % (2 + oft_r.i % 2), lambda e: e.dma_start(out=OFt[:].rearrange("p h f -> p (h f)"), in_=OF[j0:j0 + 128, :]), reads=['OF'], writes=[oftk])
                        P.dma('sp', 'm%d' % (zst_r.i % 2), lambda e: e.dma_start(out=ZSt[:].rearrange("p h f -> p (h f)"), in_=ZS[j0:j0 + 128, :]), reads=['ZS'], writes=[zstk])
                    P.flush()
                    g_ = GBt[:, ci, d_, 0, :]
                    be_ = GBt[:, ci, d_, 1, :]
                    qT = lambda h: qt[:, h, :]
                    kT = lambda h: qt[:, 4 + h, :]
                    vT = lambda h: qt[:, 8 + h, :]
                    pt, pk = psr.next()
                    P.op('pe', lambda e: e.matmul(pt[:, 0:4], lhsT=Mm[d_][:], rhs=g_, start=True, stop=True), reads=[MK[d_], 'GBt'], writes=[pk])
                    P.op('pe', lambda e: e.matmul(pt[:, 4:8], lhsT=ones[:], rhs=g_, start=True, stop=True), reads=['ones', 'GBt'], writes=[pk])
                    P.op('dve', lambda e: e.tensor_copy(out=SCg[:], in_=pt[:, 0:8]), reads=[pk], writes=['SCg'])
                    P.op('act', lambda e: e.activation(out=E1[:], in_=SCg[:], func=AF.Exp), reads=['SCg'], writes=[e1k])
                    P.op('dve', lambda e: e.tensor_tensor(out=KDs[:], in0=SCg[:, 4:8], in1=SCg[:, 0:4], op=ALU.subtract), reads=['SCg'], writes=['KDs'])
                    P.op('act', lambda e: e.activation(out=KDs[:], in_=KDs[:], func=AF.Exp), reads=['KDs'], writes=['KDs'])
                    P.op('dve', lambda e: e.tensor_scalar(out=NBt[:], in0=be_, scalar1=-1.0, scalar2=None, op0=ALU.mult), reads=['GBt'], writes=['NBt'])
                    P.op('dve', lambda e: e.tensor_tensor(out=BGt[:], in0=be_, in1=E1[:, 0:4], op=ALU.mult), reads=['GBt', e1k], writes=['BGt'])
                    if gdn_cut <= 1:
                        return
                    P.op('dve', lambda e: e.tensor_copy(out=GBC[:], in_=bc_h(g_)), reads=['GBt'], writes=['GBC'])
                    pt, pk = psr.next()
                    for h in range(4):
                        P.op('pe', lambda e, h=h, pt=pt: e.matmul(pt[:, h * 128:(h + 1) * 128], lhsT=GBC[:, h, :], rhs=Mm[d_][:], start=True, stop=False), reads=['GBC', MK[d_]], writes=[pk])
                        P.op('pe', lambda e, h=h, pt=pt: e.matmul(pt[:, h * 128:(h + 1) * 128], lhsT=ident[:], rhs=BGm[d_][:], start=False, stop=True), reads=['ident', BK[d_]], writes=[pk])
                    for h in range(4):
                        P.op('act', lambda e, h=h, pt=pt: e.activation(out=Dm[:, h, :], in_=pt[:, h * 128:(h + 1) * 128], func=AF.Exp, scale=-1.0, bias=SCg[:, h:h + 1]),
                             reads=[pk, 'SCg'], writes=['Dm'])
                    if gdn_cut <= 2:
                        return
                    P.op(p3pool, lambda e: e.tensor_tensor(out=NBN[:], in0=bc_m(noti[:]), in1=bc_h(NBt[:]), op=ALU.mult), reads=['noti', 'NBt'], writes=['NBN'])
                    pkk, pkkk = mm4(kT, kT, [qk_])
                    P.op('dve', lambda e: e.tensor_tensor(out=T1[:], in0=pkk, in1=Dm[:], op=ALU.mult), reads=[pkkk, 'Dm'], writes=['T1'])
                    XY, xyk = XYr.next()
                    P.op('dve', lambda e: e.tensor_tensor(out=XY[:, :, 0:128], in0=T1[:], in1=NBN[:], op=ALU.mult), reads=['T1', 'NBN'], writes=[xyk])
                    if seq == 'lat':
                        pqk, pqkk = mm4(qT, kT, [qk_])
                        P.op('dve', lambda e: e.tensor_tensor(out=ATT[:], in0=pqk, in1=Dm[:], op=ALU.mult), reads=[pqkk, 'Dm'], writes=['ATT'])
                        pa, pak = tr4(lambda h: ATT[:, h, :], ['ATT'])
                        P.op('act', lambda e: e.copy(out=ATTT[:], in_=pa), reads=[pak], writes=[atttk])
                    if gdn_cut <= 3:
                        return
                    py, pyk = tr4(lambda h: XY[:, h, 0:128], [xyk])
                    if gdn_cut <= 3.1:
                        return
                    P.op('dve', lambda e: e.tensor_copy(out=XY[:, :, 128:256], in_=py), reads=[pyk], writes=[xyk])
                    if gdn_cut <= 3.2:
                        return
                    TTm, ttk = TTr.next()
                    P.op('dve', lambda e: e.tensor_tensor(out=TTm[:], in0=py, in1=bc_m(ident[:]), op=ALU.add), reads=[pyk, 'ident'], writes=[ttk])
                    if gdn_cut <= 3.4:
                        return
                    pkt, pktk = tr4(kT, [qk_])
                    P.op('dve', lambda e: e.tensor_tensor(out=KBG[:], in0=pkt, in1=bc_h(BGt[:]), op=ALU.mult), reads=[pktk, 'BGt'], writes=['KBG'])
                    P.op('dve', lambda e: e.tensor_tensor(out=KDEC[:], in0=pkt, in1=bc_h(KDs[:]), op=ALU.mult), reads=[pktk, 'KDs'], writes=[kdeck])
                    if gdn_cut <= 3.6:
                        return
                    pvt, pvtk = tr4(vT, [qk_])
                    P.op('dve', lambda e: e.tensor_tensor(out=VB[:], in0=pvt, in1=bc_h(be_), op=ALU.mult), reads=[pvtk, 'GBt'], writes=['VB'])
                    if gdn_cut <= 4:
                        return
                    for lvl in range(6):
                        XYn, xynk = XYr.next()
                        for half in range(2):
                            pt, pk = psr.next()
                            for hh in range(2):
                                h = half * 2 + hh
                                P.op('pe', lambda e, h=h, hh=hh, pt=pt: e.matmul(pt[:, hh * 256:hh * 256 + 128], lhsT=XY[:, h, 128:256], rhs=XY[:, h, 0:128], start=True, stop=True),
                                     reads=[xyk], writes=[pk])
                                P.op('pe', lambda e, h=h, hh=hh, pt=pt: e.matmul(pt[:, hh * 256 + 128:hh * 256 + 256], lhsT=XY[:, h, 0:128], rhs=XY[:, h, 128:256], start=True, stop=True),
                                     reads=[xyk], writes=[pk])
                            dstv = XYn[:, half * 2:half * 2 + 2, :]
                            srcv = pt[:, :].rearrange("p (h f) -> p h f", h=2)
                            if half == 0:
                                P.op('act', lambda e, dstv=dstv, srcv=srcv: e.copy(out=dstv, in_=srcv), reads=[pk], writes=[xynk])
                            else:
                                P.op('dve', lambda e, dstv=dstv, srcv=srcv: e.tensor_copy(out=dstv, in_=srcv), reads=[pk], writes=[xynk])
                        ptt, pttk = mm4(lambda h: XYn[:, h, 0:128], lambda h: TTm[:, h, :], [xynk, ttk])
                        TTn, ttnk = TTr.next()
                        P.op('dve', lambda e, TTn=TTn, TTm=TTm, ptt=ptt: e.tensor_tensor(out=TTn[:], in0=ptt, in1=TTm[:], op=ALU.add), reads=[pttk, ttk], writes=[ttnk])
                        XY, xyk, TTm, ttk = XYn, xynk, TTn, ttnk
                    if gdn_cut <= 5:
                        return
                    pw, pwk = mm4(lambda h: TTm[:, h, :], lambda h: VB[:, h, :], [ttk, 'VB'])
                    P.op('act', lambda e: e.copy(out=WV[:], in_=pw), reads=[pwk], writes=[wvk])
                    pc, pck = mm4(lambda h: KBG[:, h, :], lambda h: TTm[:, h, :], [ttk, 'KBG'])
                    P.op('dve', lambda e: e.tensor_copy(out=KCT[:], in_=pc), reads=[pck], writes=[kctk])
                    def rec():
                        pa_, pak_ = mm4(lambda h: KCT[:, h, :], lambda h: Sst[:, h, :], [kctk, 'Sst'])
                        P.op('dve', lambda e: e.tensor_tensor(out=VN[:], in0=WV[:], in1=pa_, op=ALU.subtract), reads=[wvk, pak_], writes=['VN'])
                        if seq == 'lat':
                            po1, po1k = mm4(qT, lambda h: Sst[:, h, :], [qk_, 'Sst'])
                            po2, po2k = mm4(lambda h: ATTT[:, h, :], lambda h: VN[:, h, :], [atttk, 'VN'])
                            P.op('dve', lambda e: e.tensor_tensor(out=O1[:], in0=po1, in1=bc_h(E1[:, 0:4]), op=ALU.mult), reads=[po1k, e1k], writes=['O1'])
                            P.op('dve', lambda e: e.tensor_tensor(out=Ot[:], in0=po2, in1=O1[:], op=ALU.add), reads=[po2k, 'O1'], writes=['Ot'])
                        ps_, psk_ = mm4(lambda h: KDEC[:, h, :], lambda h: VN[:, h, :], [kdeck, 'VN'])
                        P.op(p3pool, lambda e: e.tensor_tensor(out=S1[:], in0=Sst[:], in1=bc_h(E1[:, 4:8]), op=ALU.mult), reads=['Sst', e1k], writes=['S1'])
                        P.op('dve', lambda e: e.tensor_tensor(out=Sst[:], in0=ps_, in1=S1[:], op=ALU.add), reads=[psk_, 'S1'], writes=['Sst'])
                        if seq == 'lat' and d_ == 0:
                            P.dma('sp', 'st0', lambda e: e.dma_start(out=OF[j0:j0 + 128, :], in_=Ot[:].rearrange("p h f -> p (h f)")), reads=['Ot'], writes=['OF'], defer=True)
                        if seq == 'lat' and d_ == 1:
                            P.op(p3pool, lambda e: e.tensor_tensor(out=Ot[:], in0=Ot[:], in1=OFt[:], op=ALU.add), reads=['Ot', oftk], writes=['Ot'])
                            for h in range(4):
                                P.op('act', lambda e, h=h: e.activation(out=junk[:, 0:128], in_=Ot[:, h, :], func=AF.Square, accum_out=rsd[:, h:h + 1]), reads=['Ot'], writes=['junk', 'rsd'])
                            P.op('act', lambda e: e.activation(out=rsd[:], in_=rsd[:], func=AF.Sqrt, scale=1.0 / 128, bias=EPS), reads=['rsd'], writes=['rsd'])
                            P.op('dve', lambda e: e.reciprocal(out=rsd[:], in_=rsd[:]), reads=['rsd'], writes=['rsd'])
                            P.op('dve', lambda e: e.tensor_tensor(out=Ot[:], in0=Ot[:], in1=bc_h(rsd[:]), op=ALU.mult), reads=['Ot', 'rsd'], writes=['Ot'])
                            P.op(p3pool, lambda e: e.tensor_tensor(out=Ot[:], in0=Ot[:], in1=bc_m(NGb[:]), op=ALU.mult), reads=['Ot', 'NGb'], writes=['Ot'])
                            P.op('dve', lambda e: e.tensor_tensor(out=Ot[:], in0=Ot[:], in1=ZSt[:], op=ALU.mult), reads=['Ot', zstk], writes=['Ot'])
                            ncol = 128 // ROWS
                            for cc_ in range(ncol):
                                col = cidx * ncol + cc_
                                P.dma('sp', 'st%d' % (cc_ % 2), lambda e, cc_=cc_, col=col: e.dma_start(out=yg_cm[col], in_=Ot[cc_ * ROWS:(cc_ + 1) * ROWS].rearrange("p h f -> p (h f)")),
                                      reads=['Ot'], writes=['YG'], defer=True)
                    return rec

                for d_ in range(2):
                    P.op('pool', lambda e: e.memset(Sst[:].rearrange("p h f -> p (h f)"), 0.0), writes=['Sst'])
                    order = [('ctx', cidx) for cidx in (range(NCC) if d_ == 0 else range(NCC - 1, -1, -1))]
                    order += [('lat', cidx) for cidx in (range(NCL) if d_ == 0 else range(NCL - 1, -1, -1))]
                    prev = gdn_chunk(d_, order[0][0], order[0][1])
                    for k in range(1, len(order)):
                        nxt = gdn_chunk(d_, order[k][0], order[k][1])
                        if prev is not None:
                            prev()
                        prev = nxt
                    if prev is not None:
                        prev()
                P.barrier()

        st_a.close()
        NT = L // 128
        if stop_after >= 5:
            with contextlib.ExitStack() as ph:
                NXR = 2
                xr = Ring(nc, ph, 'xq', [128, D], F32, NXR)
                WO = sb('WO', [128, KD, D], stack=ph)
                WQ = sb('WQ', [128, KD, 2048], stack=ph)
                yrt_r = Ring(nc, ph, 'YRt', [128, 4, 128], F32, 2)
                ygt_r = Ring(nc, ph, 'YGt', [128, GDW], F32, 2)
                YGT = sb('YGT', [128, 4, 128], stack=ph)
                x1r = Ring(nc, ph, 'X1', [128, D], F32, 2)
                h2t_r = Ring(nc, ph, 'h2T', [128, KD, 128], F32, 2)
                qtr = Ring(nc, ph, 'QT', [128, 16, 128], F32, 2)
                stg_r = Ring(nc, ph, 'stg', [128, 1, D], F32, 2)
                stb_r = Ring(nc, ph, 'stb', [128, 1, D], BF16, 2)
                NCH = 256
                uv_v = UV16.rearrange("(p r) d -> p r d", p=128)
                tabs = [(peer_u.rearrange("(p r) d -> p r d", p=128), uv_v[:, :, 0:D]),
                        (peer_v.rearrange("(p r) d -> p r d", p=128), uv_v[:, :, D:2 * D])]

                def convert_chunk(q):
                    src, dst = tabs[q // 128]
                    j = q % 128
                    sg, sgk = stg_r.next()
                    P.dma('sp', 'cl%d' % (stg_r.i % 2), lambda e: e.dma_start(out=sg[:], in_=src[:, j:j + 1, :]), writes=[sgk])
                    P.flush()
                    sbt, sbk = stb_r.next()
                    P.op('pool', lambda e: e.tensor_copy(out=sbt[:], in_=sg[:]), reads=[sgk], writes=[sbk])
                    P.dma('sp', 'cs%d' % (stb_r.i % 2), lambda e: e.dma_start(out=dst[:, j:j + 1, :], in_=sbt[:]), reads=[sbk], writes=['T16'], defer=True)
                P.dma('sp', 'w0', lambda e: e.dma_start(out=WO[:], in_=w_out.rearrange("(k p) n -> p k n", p=128)), writes=['WO'])
                P.dma('sp', 'w1', lambda e: e.dma_start(out=WQ[:], in_=peer_wq.rearrange("(k p) n -> p k n", p=128)), writes=['WQ'])
                ytr_v2 = YTR.rearrange("(c p) t -> p c t", p=128)
                qts_v = QTS.rearrange("(a p) t -> p a t", p=128)
                def p4a_front(i):
                    t0 = i * 128
                    xt, xk = xr.next()
                    P.dma('sp', 'x%d' % (xr.i % NXR), lambda e: e.dma_start(out=xt[:], in_=x[t0:t0 + 128, :]), writes=[xk])
                    YRt, yrtk = yrt_r.next()
                    YGt, ygtk = ygt_r.next()
                    P.dma('sp', 'l%d' % (yrt_r.i % 2), lambda e: e.dma_start(out=YRt[:], in_=ytr_v2[:, :, t0:t0 + 128]), reads=['YTR'], writes=[yrtk])
                    P.dma('sp', 'l%d' % (2 + ygt_r.i % 2), lambda e: e.dma_start(out=YGt[:], in_=YG[t0:t0 + 128, :]), reads=['YG'], writes=[ygtk])
                    P.flush()
                    for q in range(i * NCH // NT, (i + 1) * NCH // NT):
                        convert_chunk(q)
                    pt, pk = psr.next()
                    for c in range(4):
                        P.op('pe', lambda e, c=c, pt=pt: e.transpose(out=pt[:, c * 128:(c + 1) * 128], in_=YGt[:, c * 128:(c + 1) * 128], identity=ident[:]), reads=[ygtk, 'ident'], writes=[pk])
                    P.op('act', lambda e, pt=pt: e.copy(out=YGT[:].rearrange("p c t -> p (c t)"), in_=pt[:]), reads=[pk], writes=['YGT'])
                    X1, x1k = x1r.next()
                    for n in range(2):
                        pt, pk = psr.next()
                        for kc in range(8):
                            lhs = YRt[:, kc, :] if kc < 4 else YGT[:, kc - 4, :]
                            P.op('pe', lambda e, kc=kc, n=n, pt=pt, lhs=lhs: e.matmul(pt[:], lhsT=lhs, rhs=WO[:, kc, n * 512:(n + 1) * 512], start=(kc == 0), stop=(kc == 7)),
                                 reads=[yrtk, 'YGT', 'WO'], writes=[pk])
                        P.op('dve', lambda e, n=n, pt=pt, X1=X1: e.tensor_tensor(out=X1[:, n * 512:(n + 1) * 512], in0=pt[:], in1=MOD['GT1'][:, n * 512:(n + 1) * 512], op=ALU.mult),
                             reads=[pk, 'mod_GT1'], writes=[x1k])
                    P.op('pool', lambda e, xt=xt, X1=X1: e.tensor_tensor(out=X1[:], in0=X1[:], in1=xt[:], op=ALU.add), reads=[x1k, xk], writes=[x1k])
                    P.dma('sp', 'st0', lambda e, X1=X1: e.dma_start(out=X1S[t0:t0 + 128, :], in_=X1[:]), reads=[x1k], writes=['X1S'], defer=True)
                    H2, h2k = norm_mod(X1, x1k, 'G2', 'SH2')
                    P.dma('sp', 'st1', lambda e, H2=H2: e.dma_start(out=H2S[t0:t0 + 128, :], in_=H2[:]), reads=[h2k], writes=['H2S'], defer=True)
                    h2T, h2tk = h2t_r.next()
                    transpose_to(H2, h2k, h2T, h2tk, 0)
                    def qproj():
                        QT, qtk = qtr.next()
                        for qb in range(4):
                            pt, pk = psr.next()
                            for j in range(4):
                                hh = qb * 4 + j
                                for kc in range(KD):
                                    P.op('pe', lambda e, j=j, hh=hh, kc=kc, pt=pt: e.matmul(pt[:, j * 128:(j + 1) * 128], lhsT=WQ[:, kc, hh * 128:(hh + 1) * 128], rhs=h2T[:, kc, :],
                                                                                           start=(kc == 0), stop=(kc == KD - 1)), reads=['WQ', h2tk], writes=[pk])
                            if qb % 2 == 0:
                                P.op('act', lambda e, qb=qb, pt=pt, QT=QT: e.copy(out=QT[:, qb * 4:qb * 4 + 4, :].rearrange("p a t -> p (a t)"), in_=pt[:]), reads=[pk], writes=[qtk])
                            else:
                                P.op('dve', lambda e, qb=qb, pt=pt, QT=QT: e.tensor_copy(out=QT[:, qb * 4:qb * 4 + 4, :].rearrange("p a t -> p (a t)"), in_=pt[:]), reads=[pk], writes=[qtk])
                        P.dma('sp', 'st2', lambda e, QT=QT: e.dma_start(out=qts_v[:, :, t0:t0 + 128], in_=QT[:]), reads=[qtk], writes=['QTS'], defer=True)
                    return qproj

                fq = p4a_front(0)
                for i in range(NT):
                    nfq = p4a_front(i + 1) if i + 1 < NT else None
                    fq()
                    fq = nfq
                P.barrier()

            st_b.close()
            with contextlib.ExitStack() as ph:
                NXR = 2
                xr = Ring(nc, ph, 'xw', [128, D], F32, NXR)
                accA, accB = psr.items[6][0], psr.items[7][0]
                psr.items = psr.items[:6]
                NUB = cfg.get('nub', 27)
                KEY = sb('KEY', [128, 2, 128], stack=ph)
                KEYT = sb('KEYT', [128, 2, 128], stack=ph)
                QT = sb('QTt', [128, 16, 128], stack=ph)
                SC = sb('SC', [128, 16, 128], stack=ph)
                SCB = sb('SCB', [128, 16, 128], stack=ph)
                CAND = SC[:].rearrange("p a t -> p (a t)").rearrange("p (h c) -> p h c", h=8)
                OH = QT[:].rearrange("p a t -> p (a t)").rearrange("p (k j) -> p k j", j=16)
                TOPV = sb('TOPV', [128, 16, 16], stack=ph)
                TOPI = sb('TOPI', [128, 16, 16], U32, stack=ph)
                TOPIF = sb('TOPIF', [128, 16, 16], stack=ph)
                CANDB = sb('CANDB', [128, 8, 256], stack=ph)
                TS = sb('TS', [128, 8, 16], stack=ph)
                POS = sb('POS', [128, 8, 16], U32, stack=ph)
                PAB = sb('PAB', [128, 2, 128], U32, stack=ph)
                PABF = sb('PABF', [128, 2, 128], stack=ph)
                IOT = sb('IOT', [128, 16], stack=ph)
                ISEL = sb('ISEL', [128, 2, 128], stack=ph)
                IDXF = sb('IDXF', [128, 128], stack=ph)
                idx_r = Ring(nc, ph, 'IDX', [128, 128], U32, 2)
                gate_r = Ring(nc, ph, 'GATE', [128, 8, 16], F32, 2)
                gsum = sb('gsum', [128, 8], stack=ph)
                DOT = sb('DOT', [128, 128], stack=ph)
                DOTG = sb('DOTG', [128, 128], stack=ph)
                jk_r = Ring(nc, ph, 'jk', [128, D], BF16, 4)
                WGT = sb('WGT', [128, 128], stack=ph)
                GTMP = sb('GTMP', [128, 1, 128], stack=ph)
                FIN = sb('FIN', [128, D], stack=ph)
                ub_r = Ring(nc, ph, 'UB', [128, 2 * D], BF16, NUB)
                GT_ = sb('GT_', [128, 128], stack=ph)
                dg_r = Ring(nc, ph, 'DG', [128, 128], BF16, 8)
                fin = sb('fin', [128, 2], stack=ph)
                P.dma('sp', 'c0', lambda e: e.dma_start(out=KEY[:], in_=peer_keys.rearrange("x k d -> k x d")), writes=['KEY'])
                pt, pk = psr.next()
                for x_ in range(2):
                    P.op('pe', lambda e, x_=x_, pt=pt: e.transpose(out=pt[:, x_ * 128:(x_ + 1) * 128], in_=KEY[:, x_, :], identity=ident[:]), reads=['KEY', 'ident'], writes=[pk])
                P.op('dve', lambda e: e.tensor_copy(out=KEYT[:].rearrange("p x k -> p (x k)"), in_=pt[:, 0:256]), reads=[pk], writes=['KEYT'])
                P.op('pool', lambda e: e.iota(IOT[:], pattern=[[1, 16]], base=0, channel_multiplier=0, allow_small_or_imprecise_dtypes=True), writes=['IOT'])
                qts_v = QTS.rearrange("(a p) t -> p a t", p=128)

                def top16_multi(n, vals_fn, vkey, scr_fn, skey, outv_fn, outi_fn, okeys):
                    for j in range(n):
                        P.op('dve', lambda e, j=j: e.max(out=outv_fn(j)[:, 0:8], in_=vals_fn(j)), reads=[vkey], writes=['%s%d' % (okeys[0], j)])
                    for j in range(n):
                        P.op('dve', lambda e, j=j: e.max_index(out=outi_fn(j)[:, 0:8], in_max=outv_fn(j)[:, 0:8], in_values=vals_fn(j)), reads=[vkey, '%s%d' % (okeys[0], j)], writes=['%s%d' % (okeys[1], j)])
                    for j in range(n):
                        P.op('dve', lambda e, j=j: e.match_replace(out=scr_fn(j), in_to_replace=outv_fn(j)[:, 0:8], in_values=vals_fn(j), imm_value=NEG), reads=[vkey, '%s%d' % (okeys[0], j)], writes=['%s%d' % (skey, j)])
                    for j in range(n):
                        P.op('dve', lambda e, j=j: e.max(out=outv_fn(j)[:, 8:16], in_=scr_fn(j)), reads=['%s%d' % (skey, j)], writes=['%s%d' % (okeys[0], j)])
                    for j in range(n):
                        P.op('dve', lambda e, j=j: e.max_index(out=outi_fn(j)[:, 8:16], in_max=outv_fn(j)[:, 8:16], in_values=scr_fn(j)), reads=['%s%d' % (skey, j), '%s%d' % (okeys[0], j)], writes=['%s%d' % (okeys[1], j)])

                def prep(i):
                    t0 = i * 128
                    X1t, x1k = xr.next()
                    P.dma('sp', 'x%d' % (xr.i % NXR), lambda e: e.dma_start(out=X1t[:], in_=X1S[t0:t0 + 128, :]), reads=['X1S'], writes=[x1k])
                    H2t, h2k = hr.next()
                    P.dma('sp', 'l%d' % (hr.i % 2), lambda e: e.dma_start(out=H2t[:], in_=H2S[t0:t0 + 128, :]), reads=['H2S'], writes=[h2k])
                    P.dma('sp', 'l2', lambda e: e.dma_start(out=QT[:], in_=qts_v[:, :, t0:t0 + 128]), reads=['QTS'], writes=['QT'])
                    P.flush()
                    for qb in range(4):
                        pt, pk = psr.next()
                        for j in range(4):
                            hh = qb * 4 + j
                            P.op('pe', lambda e, j=j, hh=hh, pt=pt: e.matmul(pt[:, j * 128:(j + 1) * 128], lhsT=QT[:, hh, :], rhs=KEYT[:, hh % 2, :], start=True, stop=True),
                                 reads=['QT', 'KEYT'], writes=[pk])
                        P.op('act', lambda e, qb=qb, pt=pt: e.copy(out=SC[:, qb * 4:qb * 4 + 4, :].rearrange("p a t -> p (a t)"), in_=pt[:]), reads=[pk], writes=['SC'])
                    top16_multi(16, lambda j: SC[:, j, :], 'SC', lambda j: SCB[:, j, :], 'SCB', lambda j: TOPV[:, j, :], lambda j: TOPI[:, j, :], ('TOPV', 'TOPI'))
                    tvk = ['TOPV%d' % j for j in range(16)]
                    tik = ['TOPI%d' % j for j in range(16)]
                    P.op('dve', lambda e: e.tensor_copy(out=TOPIF[:], in_=TOPI[:]), reads=tik, writes=['TOPIF'])
                    tv4 = TOPV[:].rearrange("p (h x) k -> p h x k", x=2)
                    ti4 = TOPIF[:].rearrange("p (h x) k -> p h x k", x=2)
                    P.op('dve', lambda e: e.tensor_tensor(out=CAND.rearrange("p h (a b) -> p h a b", a=16),
                                                          in0=tv4[:, :, 0, :].unsqueeze(3).broadcast_to([128, 8, 16, 16]),
                                                          in1=tv4[:, :, 1, :].unsqueeze(2).broadcast_to([128, 8, 16, 16]), op=ALU.add), reads=tvk + ['SCB%d' % j for j in range(16)], writes=['SC'])
                    top16_multi(8, lambda j: CAND[:, j, :], 'SC', lambda j: CANDB[:, j, :], 'CANDB', lambda j: TS[:, j, :], lambda j: POS[:, j, :], ('TS', 'POS'))
                    tsk = ['TS%d' % j for j in range(8)]
                    posk = ['POS%d' % j for j in range(8)]
                    posf = POS[:].rearrange("p h k -> p (h k)")
                    P.op('dve', lambda e: e.tensor_single_scalar(out=PAB[:, 0, :], in_=posf, scalar=4, op=ALU.arith_shift_right), reads=posk, writes=['PAB'])
                    P.op('dve', lambda e: e.tensor_single_scalar(out=PAB[:, 1, :], in_=posf, scalar=15, op=ALU.bitwise_and), reads=posk, writes=['PAB'])
                    P.op('dve', lambda e: e.tensor_copy(out=PABF[:], in_=PAB[:]), reads=['PAB'], writes=['PABF'])
                    for x_ in range(2):
                        P.op('dve', lambda e, x_=x_: e.tensor_tensor(out=OH, in0=PABF[:, x_, :].unsqueeze(2).broadcast_to([128, 128, 16]),
                                                                      in1=IOT[:].unsqueeze(1).broadcast_to([128, 128, 16]), op=ALU.is_equal), reads=['PABF', 'IOT'], writes=['QT'])
                        oh4 = OH.rearrange("p (h k) j -> p h k j", h=8)
                        P.op('dve', lambda e, x_=x_, oh4=oh4: e.tensor_tensor(out=oh4, in0=oh4, in1=ti4[:, :, x_, :].unsqueeze(2).broadcast_to([128, 8, 16, 16]), op=ALU.mult),
                             reads=['QT', 'TOPIF'], writes=['QT'])
                        P.op('dve', lambda e, x_=x_: e.tensor_reduce(out=ISEL[:, x_, :], in_=OH, axis=AX.X, op=ALU.add), reads=['QT'], writes=['ISEL'])
                    IDX, idxk = idx_r.next()
                    GATE, gatek = gate_r.next()
                    P.op('dve', lambda e: e.scalar_tensor_tensor(out=IDXF[:], in0=ISEL[:, 0, :], scalar=128.0, in1=ISEL[:, 1, :], op0=ALU.mult, op1=ALU.add), reads=['ISEL'], writes=['IDXF'])
                    P.op('dve', lambda e: e.tensor_copy(out=IDX[:], in_=IDXF[:]), reads=['IDXF'], writes=[idxk])
                    P.op('dve', lambda e: e.tensor_tensor(out=GATE[:], in0=TS[:], in1=TS[:, :, 0:1].broadcast_to([128, 8, 16]), op=ALU.subtract), reads=tsk, writes=[gatek])
                    P.op('act', lambda e: e.activation(out=GATE[:], in_=GATE[:], func=AF.Exp), reads=[gatek], writes=[gatek])
                    P.op('dve', lambda e: e.tensor_reduce(out=gsum[:], in_=GATE[:], axis=AX.X, op=ALU.add), reads=[gatek], writes=['gsum'])
                    P.op('dve', lambda e: e.reciprocal(out=gsum[:], in_=gsum[:]), reads=['gsum'], writes=['gsum'])
                    P.op('dve', lambda e: e.tensor_tensor(out=GATE[:], in0=GATE[:], in1=gsum[:].unsqueeze(2).broadcast_to([128, 8, 16]), op=ALU.mult), reads=[gatek, 'gsum'], writes=[gatek])
                    return dict(X1t=X1t, x1k=x1k, H2t=H2t, h2k=h2k, IDX=IDX, idxk=idxk, GATE=GATE, gatek=gatek, t0=t0)

                def gather(tbl, c, hk):
                    ub, ubk = ub_r.next()
                    P.dma('pool', 'g%d' % (ub_r.i % NUB), lambda e: e.indirect_dma_start(
                        out=ub[:], out_offset=None, in_=tbl[:, :], in_offset=bass.IndirectOffsetOnAxis(ap=c['IDX'][:, hk:hk + 1], axis=0)),
                        reads=[c['idxk'], 'T16'], writes=[ubk])
                    return ub, ubk

                def uv_phase(c, mid_hook=None):
                    gflat = c['GATE'][:].rearrange("p h k -> p (h k)")
                    for g in range(16):
                        ubs = []
                        for j in range(8):
                            hk = g * 8 + j
                            ub, ubk = gather(UV16, c, hk)
                            jk, jkk = jk_r.next()
                            P.op('dve', lambda e, hk=hk, ub=ub, jk=jk: e.scalar_tensor_tensor(out=jk[:], in0=ub[:, 0:D], scalar=1.0, in1=c['H2t'][:], op0=ALU.mult, op1=ALU.mult,
                                                                                             accum_out=DOT[:, hk:hk + 1]), reads=[ubk, c['h2k']], writes=[jkk, 'DOT%d' % hk])
                            ubs.append((ub, ubk))
                        c0_, c1_ = g * 8, (g + 1) * 8
                        dks = ['DOT%d' % hk for hk in range(c0_, c1_)]
                        gk = 'GT%d' % g
                        t = GT_[:, c0_:c1_]
                        v = DOT[:, c0_:c1_]
                        P.op('dve', lambda e, t=t, v=v: e.tensor_tensor(out=t, in0=v, in1=v, op=ALU.mult), reads=dks, writes=[gk])
                        P.op('dve', lambda e, t=t: e.tensor_scalar(out=t, in0=t, scalar1=0.044715, scalar2=1.0, op0=ALU.mult, op1=ALU.add), reads=[gk], writes=[gk])
                        P.op('dve', lambda e, t=t, v=v: e.tensor_tensor(out=t, in0=t, in1=v, op=ALU.mult), reads=[gk] + dks, writes=[gk])
                        P.op('act', lambda e, t=t: e.activation(out=t, in_=t, func=AF.Sigmoid, scale=1.5957691216), reads=[gk], writes=[gk])
                        P.op('dve', lambda e, t=t, v=v: e.tensor_tensor(out=t, in0=t, in1=v, op=ALU.mult), reads=[gk] + dks, writes=[gk])
                        wk = 'WG%d' % g
                        P.op('dve', lambda e, t=t, c0_=c0_, c1_=c1_: e.tensor_tensor(out=WGT[:, c0_:c1_], in0=t, in1=gflat[:, c0_:c1_], op=ALU.mult), reads=[gk, c['gatek']], writes=[wk])
                        for j, (ub, ubk) in enumerate(ubs):
                            hk = g * 8 + j
                            dg, dgk = dg_r.next()
                            P.op('act', lambda e, hk=hk, dg=dg: e.activation(out=dg[:], in_=ident[:], func=AF.Copy, scale=WGT[:, hk:hk + 1]), reads=['ident', wk], writes=[dgk])
                            for half, (acc, ak) in enumerate(((accA, 'accA'), (accB, 'accB'))):
                                P.op('pe', lambda e, hk=hk, dg=dg, ub=ub, half=half, acc=acc: e.matmul(acc[:], lhsT=dg[:], rhs=ub[:, D + half * 512:D + (half + 1) * 512],
                                                                                                     start=(hk == 0), stop=(hk == 127)), reads=[dgk, ubk], writes=[ak])
                        if g == 3 and mid_hook is not None:
                            mid_hook()

                def fin_phase(c):
                    for half, (acc, ak) in enumerate(((accA, 'accA'), (accB, 'accB'))):
                        P.op('dve', lambda e, half=half, acc=acc: e.tensor_tensor(out=FIN[:, half * 512:(half + 1) * 512], in0=acc[:], in1=MOD['GT2'][:, half * 512:(half + 1) * 512], op=ALU.mult),
                             reads=[ak, 'mod_GT2'], writes=['FIN'])
                    P.op('dve', lambda e: e.tensor_tensor(out=FIN[:], in0=FIN[:], in1=c['X1t'][:], op=ALU.add), reads=['FIN', c['x1k']], writes=['FIN'])
                    rms_rstd(FIN[:], 'FIN', D, fin[:, 0:1], 'fin')
                    P.op('dve', lambda e: e.scalar_tensor_tensor(out=FIN[:], in0=FIN[:], scalar=fin[:, 0:1], in1=FG[:], op0=ALU.mult, op1=ALU.mult), reads=['FIN', 'fin', 'FG'], writes=['FIN'])
                    t0 = c['t0']
                    P.dma('sp', 'o0', lambda e: e.dma_start(out=out[t0:t0 + 128, :], in_=FIN[:]), reads=['FIN'], writes=['OUT'], defer=True)

                cur = prep(0)
                for i in range(NT):
                    box = {}

                    def hook(i=i, box=box):
                        box['n'] = prep(i + 1) if i + 1 < NT else None
                    uv_phase(cur, hook)
                    fin_phase(cur)
                    cur = box.get('n')
                P.barrier()
        if stop_after < 5:
            st_b.close()
        P.finish()
        nc._prog_nins = P.nins
    return nc


FULL_CFG = dict(rows=64, ctx=256, grp=512)
_NC_CACHE = {}


def make_in_maps(inputs, nb):
    f = lambda a: np.ascontiguousarray(np.asarray(a, dtype=np.float32))
    maps = []
    for b in range(nb):
        m = {
            "x": f(inputs['x'][b]), "ctxx": f(inputs['ctx'][b]),
            "cc": f(np.stack([np.asarray(inputs['c'][b]), np.asarray(inputs['c_ctx'])], 0)),
            "w_mod": f(inputs['w_mod'][0]), "b_mod": f(inputs['b_mod'][0][None, :]),
            "norm1_g": f(inputs['norm1_g'][0][None, :]), "norm2_g": f(inputs['norm2_g'][0][None, :]),
            "w_in": f(inputs['w_in'][0]), "rg_conv_w": f(inputs['rg_conv_w'][0]), "rg_conv_b": f(inputs['rg_conv_b'][0][None, :]),
            "rg_gate_w": f(inputs['rg_gate_w'][0]), "rg_gate_b": f(np.asarray(inputs['rg_gate_b'][0]).reshape(4, RGW)),
            "rg_lambda": f(inputs['rg_lambda'][0]), "gdn_conv_w": f(inputs['gdn_conv_w'][0]),
            "gdn_a_log": f(np.asarray(inputs['gdn_a_log'][0]).reshape(1, 8)), "gdn_dt_bias": f(np.asarray(inputs['gdn_dt_bias'][0]).reshape(1, 8)),
            "gdn_norm_g": f(inputs['gdn_norm_g'][0][None, :]), "w_out": f(inputs['w_out'][0]), "peer_wq": f(inputs['peer_wq'][0]),
            "peer_keys": f(inputs['peer_keys'][0]), "peer_u": f(inputs['peer_u'][0]), "peer_v": f(inputs['peer_v'][0]),
            "final_g": f(np.asarray(inputs['final_g'])[None, :]),
        }
        maps.append(m)
    return maps


def kernel(**inputs):
    nb = 8
    if 'full' not in _NC_CACHE:
        _NC_CACHE['full'] = build_nc(FULL_CFG)
    nc = _NC_CACHE['full']
    in_maps = make_in_maps(inputs, nb)
    res = run_bass_kernel_spmd(nc, in_maps, core_ids=list(range(nb)))
    return np.stack([np.asarray(r["out"], dtype=np.float32) for r in res.results], axis=0)
```

```python
import contextlib
import numpy as np
import concourse.bass as bass
import concourse.mybir as mybir
from concourse.bass_utils import run_bass_kernel_spmd

F32 = mybir.dt.float32
U32 = mybir.dt.uint32
BF16 = mybir.dt.bfloat16
AF = mybir.ActivationFunctionType
ALU = mybir.AluOpType
AX = mybir.AxisListType

D = 1024
KD = 8
EPS = 1e-6
RGW = 512
GDW = 512
INC = 3088
NEXP = 16384
BIG = 30000.0
NEG = -1.0e30


class _Rec:
    def dma_start(self, **kw):
        self.kw = kw
        return self


class Prog:
    def __init__(self, nc, st):
        self.nc = nc
        self.st = st
        self.eng = {'pe': nc.tensor, 'act': nc.scalar, 'dve': nc.vector, 'pool': nc.gpsimd, 'sp': nc.sync}
        self.cnt = {e: 0 for e in self.eng}
        self.sem = {}
        for e in ('pe', 'act', 'dve', 'pool'):
            self.sem[e] = st.enter_context(nc.semaphore('sem_' + e))
        self.dcnt = {}
        self.seen = {e: {} for e in self.eng}
        self.lastw = {}
        self.readers = {}
        self.nins = 0
        self.pending = []

    def flush(self):
        p, self.pending = self.pending, []
        for a in p:
            self.dma(*a)

    def _deps(self, eng, reads, writes):
        if self.pending:
            rs, ws = set(reads), set(writes)
            for (_, _, _, pr, pw) in self.pending:
                if (set(pr) & ws) or (set(pw) & (rs | ws)):
                    self.flush()
                    break
        need = {}

        def add(t):
            if t is None:
                return
            k, v = t
            if need.get(k, 0) < v:
                need[k] = v
        for k in reads:
            add(self.lastw.get(k))
        for k in writes:
            add(self.lastw.get(k))
            for t in self.readers.get(k, {}).items():
                add(t)
        e = self.eng[eng]
        for k, v in need.items():
            if k == 'pe' and eng == 'pe':
                continue
            if self.seen[eng].get(k, 0) >= v:
                continue
            self.seen[eng][k] = v
            e.wait_ge(self.sem[k], v)

    def _commit(self, t, reads, writes):
        for k in reads:
            r = self.readers.setdefault(k, {})
            if r.get(t[0], 0) < t[1]:
                r[t[0]] = t[1]
        for k in writes:
            self.lastw[k] = t
            self.readers[k] = {}

    def op(self, eng, fn, reads=(), writes=()):
        self._deps(eng, reads, writes)
        ins = fn(self.eng[eng])
        self.cnt[eng] += 1
        ins.then_inc(self.sem[eng], 1)
        self._commit((eng, self.cnt[eng]), reads, writes)
        self.nins += 1

    def dma(self, q, slot, fn, reads=(), writes=(), defer=False):
        if defer:
            rec = _Rec()
            fn(rec)
            kw = rec.kw
            self.pending.append((q, slot, (lambda e, kw=kw: e.dma_start(**kw)), tuple(reads), tuple(writes)))
            return
        key = 'd_' + slot
        if key not in self.sem:
            self.sem[key] = self.st.enter_context(self.nc.semaphore(key))
            self.dcnt[key] = 0
        self._deps(q, reads, writes)
        e = self.eng[q]
        if self.dcnt[key] > 0 and self.seen[q].get(key, 0) < self.dcnt[key]:
            self.seen[q][key] = self.dcnt[key]
            e.wait_ge(self.sem[key], self.dcnt[key])
        ins = fn(e)
        self.dcnt[key] += 16
        ins.then_inc(self.sem[key], 16)
        self._commit((key, self.dcnt[key]), reads, writes)
        self.nins += 1

    def barrier(self):
        self.flush()
        for e in ('pe', 'act', 'dve', 'pool', 'sp'):
            eo = self.eng[e]
            for k in ('pe', 'act', 'dve', 'pool'):
                if k != e and self.cnt[k] > self.seen[e].get(k, 0):
                    self.seen[e][k] = self.cnt[k]
                    eo.wait_ge(self.sem[k], self.cnt[k])
            for k, v in self.dcnt.items():
                if v > self.seen[e].get(k, 0):
                    self.seen[e][k] = v
                    eo.wait_ge(self.sem[k], v)

    def finish(self):
        self.flush()
        eo = self.eng['sp']
        for k, v in self.dcnt.items():
            if v > self.seen['sp'].get(k, 0):
                self.seen['sp'][k] = v
                eo.wait_ge(self.sem[k], v)


class Ring:
    def __init__(self, nc, st, name, shape, dtype, n, psum=False):
        self.items = []
        for i in range(n):
            nm = '%s%d' % (name, i)
            t = st.enter_context((nc.psum_tensor if psum else nc.sbuf_tensor)(nm, shape, dtype))
            self.items.append((t, nm))
        self.i = 0

    def next(self):
        it = self.items[self.i % len(self.items)]
        self.i += 1
        return it


def build_nc(cfg):
    ROWS = cfg['rows']
    L = ROWS * 64
    CTX = cfg['ctx']
    GRP = cfg['grp']
    stop_after = cfg.get('stop_after', 99)
    gdn_cut = cfg.get('gdn_cut', 99)
    p3pool = cfg.get('p3pool', 'pool')
    TT = CTX + L
    NGL = L // GRP
    NTG = GRP // 128
    NCC = CTX // 128
    NCL = L // 128
    CPT = 128 // ROWS if ROWS < 128 else 1
    assert ROWS <= 128 and 128 % ROWS == 0 and GRP % 128 == 0 and L % GRP == 0 and CTX % 128 == 0 and CTX <= GRP

    nc = bass.Bass("TRN2", target_bir_lowering=False)

    def din(name, shape):
        return nc.dram_tensor(name, shape, F32, kind="ExternalInput").ap()
    x = din("x", [L, D])
    ctxx = din("ctxx", [CTX, D])
    cc = din("cc", [2, D])
    w_mod = din("w_mod", [D, 6 * D])
    b_mod = din("b_mod", [1, 6 * D])
    norm1_g = din("norm1_g", [1, D])
    norm2_g = din("norm2_g", [1, D])
    w_in = din("w_in", [D, INC])
    rg_conv_w = din("rg_conv_w", [4, RGW])
    rg_conv_b = din("rg_conv_b", [1, RGW])
    rg_gate_w = din("rg_gate_w", [2, 2, 8, 64, 64])
    rg_gate_b = din("rg_gate_b", [4, RGW])
    rg_lambda = din("rg_lambda", [2, RGW])
    gdn_conv_w = din("gdn_conv_w", [4, 3 * GDW])
    gdn_a_log = din("gdn_a_log", [1, 8])
    gdn_dt_bias = din("gdn_dt_bias", [1, 8])
    gdn_norm_g = din("gdn_norm_g", [1, 128])
    w_out = din("w_out", [D, D])
    peer_wq = din("peer_wq", [D, 2048])
    peer_keys = din("peer_keys", [2, 128, 128])
    peer_u = din("peer_u", [NEXP, D])
    peer_v = din("peer_v", [NEXP, D])
    final_g = din("final_g", [1, D])
    out = nc.dram_tensor("out", [L, D], F32, kind="ExternalOutput").ap()

    def dscr(name, shape):
        return nc.dram_tensor(name, shape, F32).ap()
    XC = dscr("s_xc", [RGW, TT])
    GG = dscr("s_gg", [RGW, L])
    HF = dscr("s_hf", [RGW, L])
    YTR = dscr("s_ytr", [RGW, L])
    QKV = dscr("s_qkv", [3 * GDW, TT])
    ZS = dscr("s_zs", [L, GDW])
    OF = dscr("s_of", [L, GDW])
    YG = dscr("s_yg", [L, GDW])
    X1S = dscr("s_x1", [L, D])
    H2S = dscr("s_h2", [L, D])
    QTS = dscr("s_qt", [2048, L])
    UV16 = nc.dram_tensor("s_uv16", [NEXP, 2 * D], BF16).ap()

    x_cm = x.rearrange("(r c) f -> c r f", c=64)
    yg_cm = YG.rearrange("(r c) f -> c r f", c=64)

    with contextlib.ExitStack() as st:
        P = Prog(nc, st)

        def sb(name, shape, dtype=F32, stack=None):
            return (stack or st).enter_context(nc.sbuf_tensor(name, shape, dtype))
        psr = Ring(nc, st, 'psb', [128, 512], F32, 8, psum=True)

        ident = sb('ident', [128, 128])
        ones = sb('ones', [128, 128])
        noti = sb('noti', [128, 128])
        Mm = [sb('Mf', [128, 128]), sb('Mb', [128, 128])]
        BGm = [sb('BGf', [128, 128]), sb('BGb', [128, 128])]
        bigs = sb('bigs', [128, 128])
        P.op('pool', lambda e: e.memset(ones[:], 1.0), writes=['ones'])
        P.op('pool', lambda e: e.memset(bigs[:], BIG), writes=['bigs'])
        P.op('pool', lambda e: e.memset(ident[:], 0.0), writes=['ident'])
        P.op('pool', lambda e: e.affine_select(out=ident[:], in_=ident[:], pattern=[[-1, 128]], compare_op=ALU.not_equal,
                                               fill=1.0, base=0, channel_multiplier=1), reads=['ident'], writes=['ident'])
        P.op('pool', lambda e: e.affine_select(out=noti[:], in_=ones[:], pattern=[[-1, 128]], compare_op=ALU.not_equal,
                                               fill=0.0, base=0, channel_multiplier=1), reads=['ones'], writes=['noti'])
        P.op('pool', lambda e: e.affine_select(out=Mm[0][:], in_=ones[:], pattern=[[1, 128]], compare_op=ALU.is_ge,
                                               fill=0.0, base=0, channel_multiplier=-1), reads=['ones'], writes=['Mf'])
        P.op('pool', lambda e: e.affine_select(out=Mm[1][:], in_=ones[:], pattern=[[-1, 128]], compare_op=ALU.is_ge,
                                               fill=0.0, base=0, channel_multiplier=1), reads=['ones'], writes=['Mb'])
        P.op('pool', lambda e: e.affine_select(out=BGm[0][:], in_=bigs[:], pattern=[[1, 128]], compare_op=ALU.is_gt,
                                               fill=0.0, base=0, channel_multiplier=-1), reads=['bigs'], writes=['BGf'])
        P.op('pool', lambda e: e.affine_select(out=BGm[1][:], in_=bigs[:], pattern=[[-1, 128]], compare_op=ALU.is_gt,
                                               fill=0.0, base=0, channel_multiplier=1), reads=['bigs'], writes=['BGb'])
        MK = ['Mf', 'Mb']
        BK = ['BGf', 'BGb']

        junk = sb('junk', [128, D])
        ssr = Ring(nc, st, 'ss', [128, 2], F32, 2)
        hr = Ring(nc, st, 'hh', [128, D], F32, 2)
        MOD = {n: sb('mod_' + n, [128, D]) for n in ['GT2']}
        FG = sb('FG', [128, D])
        st_b = contextlib.ExitStack()
        st_b.__enter__()
        for n in ['GT1', 'SH2', 'G2']:
            MOD[n] = sb('mod_' + n, [128, D], stack=st_b)
        st_a = contextlib.ExitStack()
        st_a.__enter__()
        NXR = 4
        xr = Ring(nc, st_a, 'xt', [128, D], F32, NXR)
        for n in ['SH1', 'G1', 'CSH1', 'CG1']:
            MOD[n] = sb('mod_' + n, [128, D], stack=st_a)
        GBt = sb('GBt', [128, (CTX + L) // 128, 2, 2, 4], stack=st_a)
        P.dma('sp', 'c0', lambda e: e.dma_start(out=FG[:], in_=final_g[0:1, :].partition_broadcast(128)), writes=['FG'])

        with contextlib.ExitStack() as ph:
            cc2 = sb('cc2', [2, D], stack=ph)
            sc2 = sb('sc2', [2, D], stack=ph)
            scT = sb('scT', [128, KD, 2], stack=ph)
            scB = sb('scB', [128, KD, 2, 128], stack=ph)
            bmod = sb('bmod', [1, 6 * D], stack=ph)
            ng = sb('ng', [128, D], stack=ph)
            wmr = Ring(nc, ph, 'wm', [128, KD, 512], F32, 2)
            P.dma('sp', 'c0', lambda e: e.dma_start(out=cc2[:], in_=cc[:, :]), writes=['cc2'])
            P.dma('sp', 'c1', lambda e: e.dma_start(out=bmod[:], in_=b_mod[:, :]), writes=['bmod'])
            P.op('act', lambda e: e.activation(out=sc2[:], in_=cc2[:], func=AF.Silu), reads=['cc2'], writes=['sc2'])
            pt, pk = psr.next()
            for kc in range(KD):
                P.op('pe', lambda e, kc=kc: e.transpose(out=pt[:, kc * 2:kc * 2 + 2], in_=sc2[0:2, kc * 128:(kc + 1) * 128],
                                                         identity=ident[0:2, 0:2]), reads=['sc2', 'ident'], writes=[pk])
            P.op('dve', lambda e: e.tensor_copy(out=scT[:].rearrange("p k r -> p (k r)"), in_=pt[:, 0:2 * KD]), reads=[pk], writes=['scT'])
            for r in range(2):
                P.op('dve', lambda e, r=r: e.tensor_copy(out=scB[:, :, r, :], in_=scT[:, :, r:r + 1].broadcast_to([128, KD, 128])),
                     reads=['scT'], writes=['scB'])
            wmv = w_mod.rearrange("(k p) n -> p k n", p=128)
            order = [('SH1', 'CSH1'), ('G1', 'CG1'), ('GT1', None), ('SH2', None), ('G2', None), ('GT2', None)]
            for n in range(12):
                wt, wk = wmr.next()
                P.dma('sp', 'wm%d' % (n % 2), lambda e, n=n, wt=wt: e.dma_start(out=wt[:], in_=wmv[:, :, n * 512:(n + 1) * 512]), writes=[wk])
                for r in range(2):
                    dst = order[n // 2][r]
                    if dst is None:
                        continue
                    pt, pk = psr.next()
                    for kc in range(KD):
                        P.op('pe', lambda e, kc=kc, r=r, wt=wt, pt=pt: e.matmul(pt[:], lhsT=scB[:, kc, r, :], rhs=wt[:, kc, :], start=(kc == 0), stop=False),
                             reads=['scB', wk], writes=[pk])
                    P.op('pe', lambda e, n=n, pt=pt: e.matmul(pt[:], lhsT=ones[0:1, :], rhs=bmod[0:1, n * 512:(n + 1) * 512], start=False, stop=True),
                         reads=['ones', 'bmod'], writes=[pk])
                    P.op('act', lambda e, dst=dst, n=n, pt=pt: e.copy(out=MOD[dst][:, (n % 2) * 512:(n % 2 + 1) * 512], in_=pt[:]),
                         reads=[pk], writes=['mod_' + dst])
            for gsrc, names in ((norm1_g, ('G1', 'CG1')), (norm2_g, ('G2',))):
                P.dma('sp', 'c0', lambda e, gsrc=gsrc: e.dma_start(out=ng[:], in_=gsrc[0:1, :].partition_broadcast(128)), writes=['ng'])
                for nm in names:
                    P.op('dve', lambda e, nm=nm: e.scalar_tensor_tensor(out=MOD[nm][:], in0=MOD[nm][:], scalar=1.0, in1=ng[:], op0=ALU.add, op1=ALU.mult),
                         reads=['mod_' + nm, 'ng'], writes=['mod_' + nm])
            P.barrier()


        def rms_rstd(src, skey, width, dstcol, dkey, np_=128):
            P.op('act', lambda e: e.activation(out=junk[0:np_, 0:width], in_=src, func=AF.Square, accum_out=dstcol),
                 reads=[skey], writes=['junk', dkey])
            P.op('act', lambda e: e.activation(out=dstcol, in_=dstcol, func=AF.Sqrt, scale=1.0 / width, bias=EPS),
                 reads=[dkey], writes=[dkey])
            P.op('dve', lambda e: e.reciprocal(out=dstcol, in_=dstcol), reads=[dkey], writes=[dkey])

        def norm_mod(xt, xk, gname, shname, np_=128):
            s_, sk = ssr.next()
            rms_rstd(xt[0:np_, :], xk, D, s_[0:np_, 0:1], sk, np_=np_)
            h, hk = hr.next()
            P.op('dve', lambda e: e.scalar_tensor_tensor(out=h[0:np_, :], in0=xt[0:np_, :], scalar=s_[0:np_, 0:1], in1=MOD[gname][0:np_, :],
                                                         op0=ALU.mult, op1=ALU.mult), reads=[xk, sk, 'mod_' + gname], writes=[hk])
            P.op('pool', lambda e: e.tensor_tensor(out=h[0:np_, :], in0=h[0:np_, :], in1=MOD[shname][0:np_, :], op=ALU.add),
                 reads=[hk, 'mod_' + shname], writes=[hk])
            return h, hk

        def transpose_to(h, hk, hT, hTk, tok0, np_=128, flip=0):
            for half in range(2):
                pt, pk = psr.next()
                for j in range(4):
                    kc = half * 4 + j
                    P.op('pe', lambda e, j=j, kc=kc, pt=pt: e.transpose(out=pt[:, j * np_:(j + 1) * np_], in_=h[0:np_, kc * 128:(kc + 1) * 128],
                                                                       identity=ident[0:np_, 0:np_]), reads=[hk, 'ident'], writes=[pk])
                eng = 'act' if (half + flip) % 2 == 0 else 'dve'
                src = pt[:, 0:4 * np_].rearrange("p (k t) -> p k t", k=4)
                dst = hT[:, half * 4:half * 4 + 4, tok0:tok0 + np_]
                if eng == 'act':
                    P.op('act', lambda e, src=src, dst=dst: e.copy(out=dst, in_=src), reads=[pk], writes=[hTk])
                else:
                    P.op('dve', lambda e, src=src, dst=dst: e.tensor_copy(out=dst, in_=src), reads=[pk], writes=[hTk])

        def gelu_inplace_g(buf, key, nchk, width, tmp, tkey, sq_eng='pool'):
            v = buf[:, 0:nchk, 0:width]
            t = tmp[:, 0:nchk, 0:width]
            P.op(sq_eng, lambda e: e.tensor_tensor(out=t, in0=v, in1=v, op=ALU.mult), reads=[key], writes=[tkey])
            P.op('dve', lambda e: e.tensor_scalar(out=t, in0=t, scalar1=0.044715, scalar2=1.0, op0=ALU.mult, op1=ALU.add), reads=[tkey], writes=[tkey])
            P.op('dve', lambda e: e.tensor_tensor(out=t, in0=t, in1=v, op=ALU.mult), reads=[tkey, key], writes=[tkey])
            P.op('act', lambda e: e.activation(out=t, in_=t, func=AF.Sigmoid, scale=1.5957691216), reads=[tkey], writes=[tkey])
            P.op('dve', lambda e: e.tensor_tensor(out=v, in0=v, in1=t, op=ALU.mult), reads=[tkey, key], writes=[key])


        def load_cols(dst, src2d, key, slot='c0'):
            P.dma('sp', slot, lambda e: e.dma_start(out=dst, in_=src2d.rearrange("n p -> p n"), allow_slow_non_contiguous=True), writes=[key])

        def seq_tile_loads(seq, order, i, xt, xk):
            if seq == 'ctx':
                P.dma('sp', 'x%d' % (xr.i % NXR), lambda e: e.dma_start(out=xt[:], in_=ctxx[i * 128:(i + 1) * 128, :]), writes=[xk])
            elif order == 'raster':
                P.dma('sp', 'x%d' % (xr.i % NXR), lambda e: e.dma_start(out=xt[:], in_=x[i * 128:(i + 1) * 128, :]), writes=[xk])
            else:
                ncol = 128 // ROWS
                for ci in range(ncol):
                    col = i * ncol + ci
                    P.dma('sp', 'x%d' % (xr.i % NXR), lambda e, ci=ci, col=col: e.dma_start(out=xt[ci * ROWS:(ci + 1) * ROWS, :], in_=x_cm[col]), writes=[xk])

        with contextlib.ExitStack() as ph:
            WIN = sb('WIN', [128, KD, 2064], stack=ph)
            hT = sb('hT', [128, KD, GRP], stack=ph)
            hTb = sb('hTb', [128, KD, 16], stack=ph)
            PTb = sb('PTb', [128, 12, 16], stack=ph)
            PT = sb('PT', [128, 4, GRP + 3], stack=ph)
            CV = sb('CV', [128, 4, GRP], stack=ph)
            SQ = sb('SQ', [128, 4, GRP], stack=ph)
            RS = sb('RS', [128, GRP], stack=ph)
            GEL = sb('GEL', [128, 4, GRP], stack=ph)
            HALO = sb('HALO', [128, 12, 2], stack=ph)
            cw = sb('cw', [128, 12, 4], stack=ph)
            cb = sb('cb', [128, 4], stack=ph)
            zt = sb('zt', [128, GDW], stack=ph)
            abc = sb('abc', [128, 3, 8], stack=ph)
            abt = sb('abt', [128, 16], stack=ph)
            xb = sb('xb', [16, D], stack=ph)
            w_in_v = w_in.rearrange("(k p) n -> p k n", p=128)

            def inproj_fm(c0col, nch, width, rhsT, rhsk, dst_fn, dkey):
                for c in range(nch):
                    pt, pk = psr.next()
                    for kc in range(KD):
                        P.op('pe', lambda e, c=c, kc=kc, pt=pt: e.matmul(pt[:, 0:width], lhsT=WIN[:, kc, c0col + c * 128:c0col + (c + 1) * 128],
                                                                           rhs=rhsT[:, kc, 0:width], start=(kc == 0), stop=(kc == KD - 1)),
                             reads=['WIN', rhsk], writes=[pk])
                    if c % 2 == 0:
                        P.op('act', lambda e, c=c, pt=pt: e.copy(out=dst_fn(c), in_=pt[:, 0:width]), reads=[pk], writes=[dkey])
                    else:
                        P.op('dve', lambda e, c=c, pt=pt: e.tensor_copy(out=dst_fn(c), in_=pt[:, 0:width]), reads=[pk], writes=[dkey])

            def boundary(tokens, c0col, nch):
                nb = len(tokens)
                if nb == 0:
                    return
                for i_, t in enumerate(tokens):
                    P.dma('sp', 'c0', lambda e, i_=i_, t=t: e.dma_start(out=xb[i_:i_ + 1, :], in_=x[t:t + 1, :]), writes=['xb'])
                h, hk = norm_mod(xb, 'xb', 'G1', 'SH1', np_=nb)
                transpose_to(h, hk, hTb, 'hTb', 0, np_=nb)
                inproj_fm(c0col, nch, nb, hTb, 'hTb', lambda c: PTb[:, c, 0:nb], 'PTb')

            def conv4(nchk, cwoff, bias, width):
                for c in range(nchk):
                    eng = 'dve' if c % 2 == 0 else 'pool'
                    if bias:
                        P.op(eng, lambda e, c=c: e.tensor_scalar(out=CV[:, c, 0:width], in0=PT[:, c, 0:width], scalar1=cw[:, cwoff + c, 0:1], scalar2=cb[:, c:c + 1],
                                                                 op0=ALU.mult, op1=ALU.add), reads=['PT', 'cw', 'cb'], writes=['CV%d' % c])
                    else:
                        P.op(eng, lambda e, c=c: e.tensor_scalar(out=CV[:, c, 0:width], in0=PT[:, c, 0:width], scalar1=cw[:, cwoff + c, 0:1], scalar2=None,
                                                                 op0=ALU.mult), reads=['PT', 'cw'], writes=['CV%d' % c])
                    for k in range(1, 4):
                        P.op('dve', lambda e, c=c, k=k: e.scalar_tensor_tensor(out=CV[:, c, 0:width], in0=PT[:, c, k:k + width], scalar=cw[:, cwoff + c, k:k + 1],
                                                                               in1=CV[:, c, 0:width], op0=ALU.mult, op1=ALU.add),
                             reads=['PT', 'cw', 'CV%d' % c], writes=['CV%d' % c])

            def gelu_inplace(buf, key, nchk, width, tmp, tkey):
                v = buf[:, 0:nchk, 0:width]
                t = tmp[:, 0:nchk, 0:width]
                P.op('pool', lambda e: e.tensor_tensor(out=t, in0=v, in1=v, op=ALU.mult), reads=[key], writes=[tkey])
                P.op('dve', lambda e: e.tensor_scalar(out=t, in0=t, scalar1=0.044715, scalar2=1.0, op0=ALU.mult, op1=ALU.add), reads=[tkey], writes=[tkey])
                P.op('dve', lambda e: e.tensor_tensor(out=t, in0=t, in1=v, op=ALU.mult), reads=[tkey, key], writes=[tkey])
                P.op('act', lambda e: e.activation(out=t, in_=t, func=AF.Sigmoid, scale=1.5957691216), reads=[tkey], writes=[tkey])
                P.op('dve', lambda e: e.tensor_tensor(out=v, in0=v, in1=t, op=ALU.mult), reads=[tkey, key], writes=[key])

            P.dma('sp', 'w0', lambda e: e.dma_start(out=WIN[:, :, 0:1024], in_=w_in_v[:, :, 0:1024]), writes=['WIN'])
            for k in range(4):
                load_cols(cw[:, 0:4, k], rg_conv_w[k:k + 1, :].rearrange("o (c p) -> (o c) p", p=128), 'cw')
            load_cols(cb[:, 0:4], rg_conv_b[0:1, :].rearrange("o (c p) -> (o c) p", p=128), 'cb')
            rg_bt = [g * GRP for g in range(1, NGL)]
            boundary(rg_bt, 0, 4)
            xc_v = XC.rearrange("(c p) t -> p c t", p=128)
            gg_v = GG.rearrange("(c p) t -> p c t", p=128)

            def p1_rg_group(seq, g):
                width = CTX if seq == 'ctx' else GRP
                ntile = width // 128
                seqoff = 0 if seq == 'ctx' else CTX
                t0 = g * GRP
                tl = []
                for i in range(ntile):
                    xt, xk = xr.next()
                    seq_tile_loads(seq, 'raster', g * NTG + i, xt, xk)
                    tl.append((xt, xk))
                P.flush()
                for i in range(ntile):
                    xt, xk = tl[i]
                    h, hk = norm_mod(xt, xk, 'CG1' if seq == 'ctx' else 'G1', 'CSH1' if seq == 'ctx' else 'SH1')
                    transpose_to(h, hk, hT, 'hT', i * 128, flip=i)
                if g == 0:
                    P.op('pool', lambda e: e.memset(PT[:, :, 0:2], 0.0), writes=['PT'])
                else:
                    P.op('pool', lambda e: e.tensor_copy(out=PT[:, :, 0:2], in_=PT[:, :, GRP:GRP + 2]), reads=['PT'], writes=['PT'])
                inproj_fm(0, 4, width, hT, 'hT', lambda c: PT[:, c, 2:2 + width], 'PT')
                if seq == 'ctx' or g == NGL - 1:
                    P.op('pool', lambda e: e.memset(PT[:, :, 2 + width:3 + width], 0.0), writes=['PT'])
                else:
                    P.op('pool', lambda e: e.tensor_copy(out=PT[:, :, 2 + width:3 + width], in_=PTb[:, 0:4, g:g + 1]), reads=['PTb'], writes=['PT'])
                conv4(4, 0, True, width)
                P.dma('sp', 'st0', lambda e: e.dma_start(out=xc_v[:, :, seqoff + t0:seqoff + t0 + width], in_=CV[:, :, 0:width]),
                      reads=['CV0', 'CV1', 'CV2', 'CV3'], writes=['XC'], defer=True)
                if seq == 'lat':
                    inproj_fm(512, 4, width, hT, 'hT', lambda c: SQ[:, c, 0:width], 'SQ')
                    gelu_inplace(SQ, 'SQ', 4, width, GEL, 'GEL')
                    P.dma('sp', 'st1', lambda e: e.dma_start(out=gg_v[:, :, t0:t0 + width], in_=SQ[:, :, 0:width]), reads=['SQ'], writes=['GG'], defer=True)

            def touch_cv():
                pass

            p1_rg_group('ctx', 0)
            for g in range(NGL):
                p1_rg_group('lat', g)

            if stop_after >= 2:
                P.dma('sp', 'w0', lambda e: e.dma_start(out=WIN[:, :, 0:2064], in_=w_in_v[:, :, 1024:3088]), reads=['WIN'], writes=['WIN'])
                for k in range(4):
                    load_cols(cw[:, 0:12, k], gdn_conv_w[k:k + 1, :].rearrange("o (c p) -> (o c) p", p=128), 'cw')
                P.dma('sp', 'c0', lambda e: e.dma_start(out=abc[:, 0, :], in_=gdn_dt_bias[0:1, :].partition_broadcast(128)), writes=['abc'])
                P.dma('sp', 'c0', lambda e: e.dma_start(out=abc[:, 1, :], in_=gdn_a_log[0:1, :].partition_broadcast(128)), writes=['abc'])
                P.op('act', lambda e: e.activation(out=abc[:, 1, :], in_=abc[:, 1, :], func=AF.Exp), reads=['abc'], writes=['abc'])
                P.op('dve', lambda e: e.tensor_scalar(out=abc[:, 1, :], in0=abc[:, 1, :], scalar1=-1.0, scalar2=None, op0=ALU.mult), reads=['abc'], writes=['abc'])
                gd_bt = [(g * GRP) // ROWS for g in range(1, NGL)]
                boundary(gd_bt, 0, 12)
                qkv_v = QKV.rearrange("(c p) t -> p c t", p=128)

                def p1_gdn_group(seq, g):
                    width = CTX if seq == 'ctx' else GRP
                    ntile = width // 128
                    seqoff = 0 if seq == 'ctx' else CTX
                    j0 = g * GRP
                    tl = []
                    for i in range(ntile):
                        xt, xk = xr.next()
                        seq_tile_loads(seq, 'cm', g * NTG + i, xt, xk)
                        tl.append((xt, xk))
                    P.flush()
                    for i in range(ntile):
                        xt, xk = tl[i]
                        h, hk = norm_mod(xt, xk, 'CG1' if seq == 'ctx' else 'G1', 'CSH1' if seq == 'ctx' else 'SH1')
                        transpose_to(h, hk, hT, 'hT', i * 128, flip=i)
                    for part in range(3):
                        if g == 0:
                            P.op('pool', lambda e: e.memset(PT[:, :, 0:2], 0.0), writes=['PT'])
                        else:
                            P.op('pool', lambda e, part=part: e.tensor_copy(out=PT[:, :, 0:2], in_=HALO[:, part * 4:part * 4 + 4, :]), reads=['HALO'], writes=['PT'])
                        inproj_fm(part * 512, 4, width, hT, 'hT', lambda c: PT[:, c, 2:2 + width], 'PT')
                        P.op('pool', lambda e, part=part: e.tensor_copy(out=HALO[:, part * 4:part * 4 + 4, :], in_=PT[:, :, width:width + 2]), reads=['PT'], writes=['HALO'])
                        if seq == 'ctx' or g == NGL - 1:
                            P.op('pool', lambda e: e.memset(PT[:, :, 2 + width:3 + width], 0.0), writes=['PT'])
                        else:
                            P.op('pool', lambda e, part=part: e.tensor_copy(out=PT[:, :, 2 + width:3 + width], in_=PTb[:, part * 4:part * 4 + 4, g:g + 1]), reads=['PTb'], writes=['PT'])
                        conv4(4, part * 4, False, width)
                        cvk = ['CV0', 'CV1', 'CV2', 'CV3']
                        P.op('act', lambda e: e.activation(out=CV[:, :, 0:width], in_=CV[:, :, 0:width], func=AF.Silu), reads=cvk, writes=cvk)
                        if part < 2:
                            P.op('pool', lambda e: e.tensor_tensor(out=SQ[:, :, 0:width], in0=CV[:, :, 0:width], in1=CV[:, :, 0:width], op=ALU.mult), reads=cvk, writes=['SQ'])
                            for c in range(4):
                                pt, pk = psr.next()
                                P.op('pe', lambda e, c=c, pt=pt: e.matmul(pt[:, 0:width], lhsT=ones[:], rhs=SQ[:, c, 0:width], start=True, stop=True), reads=['ones', 'SQ'], writes=[pk])
                                P.op('act', lambda e, pt=pt: e.activation(out=RS[:, 0:width], in_=pt[:, 0:width], func=AF.Sqrt, bias=EPS), reads=[pk], writes=['RS'])
                                P.op('dve', lambda e: e.reciprocal(out=RS[:, 0:width], in_=RS[:, 0:width]), reads=['RS'], writes=['RS'])
                                sc_ = (128.0 ** -0.5) if part == 0 else 1.0
                                P.op('dve', lambda e, c=c, sc_=sc_: e.scalar_tensor_tensor(out=CV[:, c, 0:width], in0=CV[:, c, 0:width], scalar=sc_, in1=RS[:, 0:width], op0=ALU.mult, op1=ALU.mult),
                                     reads=['CV%d' % c, 'RS'], writes=['CV%d' % c])
                        P.dma('sp', 'st0', lambda e, part=part: e.dma_start(out=qkv_v[:, part * 4:part * 4 + 4, seqoff + j0:seqoff + j0 + width], in_=CV[:, :, 0:width]),
                              reads=cvk, writes=['QKV'], defer=True)
                    for i in range(ntile):
                        ci = (g * NTG + i) + (0 if seq == 'ctx' else NCC)
                        if seq == 'lat':
                            pt, pk = psr.next()
                            for kc in range(KD):
                                P.op('pe', lambda e, kc=kc, i=i, pt=pt: e.matmul(pt[:, 0:GDW], lhsT=hT[:, kc, i * 128:(i + 1) * 128], rhs=WIN[:, kc, 1536:2048],
                                                                                 start=(kc == 0), stop=(kc == KD - 1)), reads=['hT', 'WIN'], writes=[pk])
                            P.op('act', lambda e, pt=pt: e.activation(out=zt[:], in_=pt[:, 0:GDW], func=AF.Silu), reads=[pk], writes=['zt'])
                            P.dma('sp', 'st1', lambda e, i=i: e.dma_start(out=ZS[j0 + i * 128:j0 + (i + 1) * 128, :], in_=zt[:]), reads=['zt'], writes=['ZS'], defer=True)
                        pt, pk = psr.next()
                        for kc in range(KD):
                            P.op('pe', lambda e, kc=kc, i=i, pt=pt: e.matmul(pt[:, 0:16], lhsT=hT[:, kc, i * 128:(i + 1) * 128], rhs=WIN[:, kc, 2048:2064],
                                                                             start=(kc == 0), stop=(kc == KD - 1)), reads=['hT', 'WIN'], writes=[pk])
                        pv = pt[:, 0:16].rearrange("p (d a h) -> p d a h", d=2, a=2)
                        av = abc[:, 2, :].rearrange("p (d h) -> p d h", d=2)
                        dtb = abc[:, 0, :].rearrange("p (d h) -> p d h", d=2)
                        nea = abc[:, 1, :].rearrange("p (d h) -> p d h", d=2)
                        P.op('dve', lambda e, pv=pv, av=av, dtb=dtb: e.tensor_tensor(out=av, in0=pv[:, :, 0, :], in1=dtb, op=ALU.add), reads=[pk, 'abc'], writes=['abc2'])
                        P.op('act', lambda e, av=av: e.activation(out=av, in_=av, func=AF.Exp), reads=['abc2'], writes=['abc2'])
                        P.op('act', lambda e, av=av: e.activation(out=av, in_=av, func=AF.Ln, bias=1.0), reads=['abc2'], writes=['abc2'])
                        P.op('dve', lambda e, av=av, nea=nea, ci=ci: e.tensor_tensor(out=GBt[:, ci, :, 0, :], in0=av, in1=nea, op=ALU.mult), reads=['abc2', 'abc'], writes=['GBt'])
                        bv = abt[:, 0:8].rearrange("p (d h) -> p d h", d=2)
                        P.op('dve', lambda e, pv=pv, bv=bv: e.tensor_copy(out=bv, in_=pv[:, :, 1, :]), reads=[pk], writes=['abt'])
                        P.op('act', lambda e, bv=bv, ci=ci: e.activation(out=GBt[:, ci, :, 1, :], in_=bv, func=AF.Sigmoid), reads=['abt'], writes=['GBt'])

                p1_gdn_group('ctx', 0)
                for g in range(NGL):
                    p1_gdn_group('lat', g)
            P.barrier()

        if stop_after >= 3:
            with contextlib.ExitStack() as ph:
                GW = sb('GW', [128, 2, 2, 4, 128], stack=ph)
                gbv = sb('gbv', [128, 16], stack=ph)
                nsp = sb('nsp', [128, 8], stack=ph)
                carry = sb('carry', [128, 4], stack=ph)
                xct_r = Ring(nc, ph, 'XCt', [128, 4, GRP], F32, 2)
                Rg = sb('Rg', [128, 4, GRP], stack=ph)
                Ig = sb('Ig', [128, 4, GRP], stack=ph)
                Ag = sb('Ag', [128, 4, GRP], stack=ph)
                Bg = sb('Bg', [128, 4, GRP], stack=ph)
                Hg = sb('Hg', [128, 4, GRP], stack=ph)
                hft_r = Ring(nc, ph, 'HFt', [128, 4, GRP], F32, 2)
                ggt_r = Ring(nc, ph, 'GGt', [128, 4, GRP], F32, 2)
                P.op('pool', lambda e: e.memset(GW[:].rearrange("p a b c f -> p (a b c f)"), 0.0), writes=['GW'])
                for d_ in range(2):
                    for gi in range(2):
                        for n in range(8):
                            c, hb = n // 2, n % 2
                            P.dma('sp', 'c%d' % (n % 2), lambda e, d_=d_, gi=gi, n=n, c=c, hb=hb: e.dma_start(
                                out=GW[hb * 64:(hb + 1) * 64, d_, gi, c, hb * 64:(hb + 1) * 64], in_=rg_gate_w[d_, gi, n]), writes=['GW'])
                load_cols(gbv[:, :], rg_gate_b.rearrange("a (c p) -> (a c) p", p=128), 'gbv')
                load_cols(nsp[:, :], rg_lambda.rearrange("a (c p) -> (a c) p", p=128), 'nsp')
                P.op('act', lambda e: e.activation(out=nsp[:], in_=nsp[:], func=AF.Exp, scale=-1.0), reads=['nsp'], writes=['nsp'])
                P.op('act', lambda e: e.activation(out=nsp[:], in_=nsp[:], func=AF.Ln, bias=1.0), reads=['nsp'], writes=['nsp'])
                P.op('dve', lambda e: e.tensor_scalar(out=nsp[:], in0=nsp[:], scalar1=-8.0, scalar2=None, op0=ALU.mult), reads=['nsp'], writes=['nsp'])
                hf_v = HF.rearrange("(c p) t -> p c t", p=128)
                ytr_v = YTR.rearrange("(c p) t -> p c t", p=128)

                def rg_group(d_, seq, g):
                    width = CTX if seq == 'ctx' else GRP
                    seqoff = 0 if seq == 'ctx' else CTX
                    t0 = g * GRP
                    XCt, xctk = xct_r.next()
                    P.dma('sp', 'l%d' % (xct_r.i % 2), lambda e: e.dma_start(out=XCt[:, :, 0:width], in_=xc_v[:, :, seqoff + t0:seqoff + t0 + width]), reads=['XC'], writes=[xctk])
                    if seq == 'lat' and d_ == 1:
                        HFt, hftk = hft_r.next()
                        GGt, ggtk = ggt_r.next()
                        P.dma('sp', 'l%d' % (2 + hft_r.i % 2), lambda e: e.dma_start(out=HFt[:, :, 0:width], in_=hf_v[:, :, t0:t0 + width]), reads=['HF'], writes=[hftk])
                        P.dma('sp', 'm%d' % (ggt_r.i % 2), lambda e: e.dma_start(out=GGt[:, :, 0:width], in_=gg_v[:, :, t0:t0 + width]), reads=['GG'], writes=[ggtk])
                    P.flush()
                    for gi, dstb, dk in ((0, Rg, 'Rg'), (1, Ig, 'Ig')):
                        for c in range(4):
                            pt, pk = psr.next()
                            P.op('pe', lambda e, gi=gi, c=c, pt=pt: e.matmul(pt[:, 0:width], lhsT=GW[:, d_, gi, c, :], rhs=XCt[:, c, 0:width], start=True, stop=True),
                                 reads=['GW', xctk], writes=[pk])
                            col = (d_ * 2 + gi) * 4 + c
                            P.op('act', lambda e, c=c, pt=pt, dstb=dstb, col=col: e.activation(out=dstb[:, c, 0:width], in_=pt[:, 0:width], func=AF.Sigmoid, bias=gbv[:, col:col + 1]),
                                 reads=[pk, 'gbv'], writes=[dk])
                    for c in range(4):
                        P.op('act', lambda e, c=c: e.activation(out=Ag[:, c, 0:width], in_=Rg[:, c, 0:width], func=AF.Exp, scale=nsp[:, d_ * 4 + c:d_ * 4 + c + 1]),
                             reads=['Rg', 'nsp'], writes=['Ag'])
                    av_ = Ag[:, :, 0:width]
                    P.op('pool', lambda e: e.tensor_tensor(out=Rg[:, :, 0:width], in0=av_, in1=av_, op=ALU.mult), reads=['Ag'], writes=['Rg'])
                    P.op('act', lambda e: e.activation(out=Rg[:, :, 0:width], in_=Rg[:, :, 0:width], func=AF.Sqrt, scale=-1.0, bias=1.0), reads=['Rg'], writes=['Rg'])
                    P.op('dve', lambda e: e.tensor_tensor(out=Bg[:, :, 0:width], in0=Ig[:, :, 0:width], in1=XCt[:, :, 0:width], op=ALU.mult), reads=['Ig', xctk], writes=['Bg'])
                    P.op('dve', lambda e: e.tensor_tensor(out=Bg[:, :, 0:width], in0=Bg[:, :, 0:width], in1=Rg[:, :, 0:width], op=ALU.mult), reads=['Bg', 'Rg'], writes=['Bg'])
                    for c in range(4):
                        if d_ == 0:
                            o_, a_, b_ = Hg[:, c, 0:width], Ag[:, c, 0:width], Bg[:, c, 0:width]
                        else:
                            o_, a_, b_ = Hg[:, c, width - 1::-1] if False else Hg[:, c, 0:width][:, ::-1], Ag[:, c, 0:width][:, ::-1], Bg[:, c, 0:width][:, ::-1]
                        P.op('dve', lambda e, c=c, o_=o_, a_=a_, b_=b_: e.tensor_tensor_scan(out=o_, data0=a_, data1=b_, initial=carry[:, c:c + 1], op0=ALU.mult, op1=ALU.add),
                             reads=['Ag', 'Bg', 'carry'], writes=['Hg'])
                    last = width - 1 if d_ == 0 else 0
                    P.op('dve', lambda e: e.tensor_copy(out=carry[:, :], in_=Hg[:, :, last]), reads=['Hg'], writes=['carry'])
                    if seq == 'lat' and d_ == 0:
                        P.dma('sp', 'st0', lambda e: e.dma_start(out=hf_v[:, :, t0:t0 + width], in_=Hg[:, :, 0:width]), reads=['Hg'], writes=['HF'], defer=True)
                    if seq == 'lat' and d_ == 1:
                        P.op('pool', lambda e: e.tensor_tensor(out=Hg[:, :, 0:width], in0=Hg[:, :, 0:width], in1=HFt[:, :, 0:width], op=ALU.add), reads=['Hg', hftk], writes=['Hg'])
                        P.op('dve', lambda e: e.tensor_tensor(out=Hg[:, :, 0:width], in0=Hg[:, :, 0:width], in1=GGt[:, :, 0:width], op=ALU.mult), reads=['Hg', ggtk], writes=['Hg'])
                        P.dma('sp', 'st1', lambda e: e.dma_start(out=ytr_v[:, :, t0:t0 + width], in_=Hg[:, :, 0:width]), reads=['Hg'], writes=['YTR'], defer=True)

                for d_ in range(2):
                    P.op('pool', lambda e: e.memset(carry[:], 0.0), writes=['carry'])
                    rg_group(d_, 'ctx', 0)
                    for g in (range(NGL) if d_ == 0 else range(NGL - 1, -1, -1)):
                        rg_group(d_, 'lat', g)
                P.barrier()

        if stop_after >= 4:
            with contextlib.ExitStack() as ph:
                qk_r = Ring(nc, ph, 'qkvt', [128, 12, 128], F32, 2)
                Sst = sb('Sst', [128, 4, 128], stack=ph)
                S1 = sb('S1', [128, 4, 128], stack=ph)
                SCg = sb('SCg', [128, 8], stack=ph)
                E1_r = Ring(nc, ph, 'E1', [128, 8], F32, 2)
                KDs = sb('KDs', [128, 4], stack=ph)
                NBt = sb('NBt', [128, 4], stack=ph)
                BGt = sb('BGt', [128, 4], stack=ph)
                GBC = sb('GBC', [128, 4, 128], stack=ph)
                Dm = sb('Dm', [128, 4, 128], stack=ph)
                NBN = sb('NBN', [128, 4, 128], stack=ph)
                T1 = sb('T1', [128, 4, 128], stack=ph)
                ATT = sb('ATT', [128, 4, 128], stack=ph)
                ATTT_r = Ring(nc, ph, 'ATTT', [128, 4, 128], F32, 2)
                XYr = Ring(nc, ph, 'XY', [128, 4, 256], F32, 2)
                TTr = Ring(nc, ph, 'TTm', [128, 4, 128], F32, 2)
                VB = sb('VB', [128, 4, 128], stack=ph)
                KBG = sb('KBG', [128, 4, 128], stack=ph)
                KDEC_r = Ring(nc, ph, 'KDEC', [128, 4, 128], F32, 2)
                WV_r = Ring(nc, ph, 'WV', [128, 4, 128], F32, 2)
                KCT_r = Ring(nc, ph, 'KCT', [128, 4, 128], F32, 2)
                VN = sb('VN', [128, 4, 128], stack=ph)
                O1 = sb('O1', [128, 4, 128], stack=ph)
                Ot = sb('Ot', [128, 4, 128], stack=ph)
                oft_r = Ring(nc, ph, 'OFt', [128, 4, 128], F32, 2)
                zst_r = Ring(nc, ph, 'ZSt', [128, 4, 128], F32, 2)
                NGb = sb('NGb', [128, 128], stack=ph)
                rsd = sb('rsd', [128, 4], stack=ph)
                P.dma('sp', 'c0', lambda e: e.dma_start(out=NGb[:], in_=gdn_norm_g[0:1, :].partition_broadcast(128)), writes=['NGb'])

                def bc_h(col4):
                    return col4[:, :, None].broadcast_to([128, 4, 128]) if False else col4.unsqueeze(2).broadcast_to([128, 4, 128])

                def bc_m(m):
                    return m.unsqueeze(1).broadcast_to([128, 4, 128])

                def mm4(lhs_fn, rhs_fn, reads, width=128):
                    pt, pk = psr.next()
                    for h in range(4):
                        P.op('pe', lambda e, h=h, pt=pt: e.matmul(pt[:, h * 128:(h + 1) * 128], lhsT=lhs_fn(h), rhs=rhs_fn(h), start=True, stop=True), reads=reads, writes=[pk])
                    return pt[:, :].rearrange("p (h f) -> p h f", h=4), pk

                def tr4(src_fn, reads):
                    pt, pk = psr.next()
                    for h in range(4):
                        P.op('pe', lambda e, h=h, pt=pt: e.transpose(out=pt[:, h * 128:(h + 1) * 128], in_=src_fn(h), identity=ident[:]), reads=reads + ['ident'], writes=[pk])
                    return pt[:, :].rearrange("p (h f) -> p h f", h=4), pk

                def gdn_chunk(d_, seq, cidx):
                    ci = cidx + (0 if seq == 'ctx' else NCC)
                    E1, e1k = E1_r.next()
                    ATTT, atttk = ATTT_r.next()
                    KDEC, kdeck = KDEC_r.next()
                    WV, wvk = WV_r.next()
                    KCT, kctk = KCT_r.next()
                    j0 = cidx * 128
                    seqoff = 0 if seq == 'ctx' else CTX
                    qt, qk_ = qk_r.next()
                    P.dma('sp', 'l%d' % (qk_r.i % 2), lambda e: e.dma_start(out=qt[:], in_=qkv_v[:, :, seqoff + j0:seqoff + j0 + 128]), reads=['QKV'], writes=[qk_])
                    if seq == 'lat' and d_ == 1:
                        OFt, oftk = oft_r.next()
                        ZSt, zstk = zst_r.next()
                        P.dma('sp', 'l%d' % (2 + oft_r.i % 2), lambda e: e.dma_start(out=OFt[:].rearrange("p h f -> p (h f)"), in_=OF[j0:j0 + 128, :]), reads=['OF'], writes=[oftk])
                        P.dma('sp', 'm%d' % (zst_r.i % 2), lambda e: e.dma_start(out=ZSt[:].rearrange("p h f -> p (h f)"), in_=ZS[j0:j0 + 128, :]), reads=['ZS'], writes=[zstk])
                    P.flush()
                    g_ = GBt[:, ci, d_, 0, :]
                    be_ = GBt[:, ci, d_, 1, :]
                    qT = lambda h: qt[:, h, :]
                    kT = lambda h: qt[:, 4 + h, :]
                    vT = lambda h: qt[:, 8 + h, :]
                    pt, pk = psr.next()
                    P.op('pe', lambda e: e.matmul(pt[:, 0:4], lhsT=Mm[d_][:], rhs=g_, start=True, stop=True), reads=[MK[d_], 'GBt'], writes=[pk])
                    P.op('pe', lambda e: e.matmul(pt[:, 4:8], lhsT=ones[:], rhs=g_, start=True, stop=True), reads=['ones', 'GBt'], writes=[pk])
                    P.op('dve', lambda e: e.tensor_copy(out=SCg[:], in_=pt[:, 0:8]), reads=[pk], writes=['SCg'])
                    P.op('act', lambda e: e.activation(out=E1[:], in_=SCg[:], func=AF.Exp), reads=['SCg'], writes=[e1k])
                    P.op('dve', lambda e: e.tensor_tensor(out=KDs[:], in0=SCg[:, 4:8], in1=SCg[:, 0:4], op=ALU.subtract), reads=['SCg'], writes=['KDs'])
                    P.op('act', lambda e: e.activation(out=KDs[:], in_=KDs[:], func=AF.Exp), reads=['KDs'], writes=['KDs'])
                    P.op('dve', lambda e: e.tensor_scalar(out=NBt[:], in0=be_, scalar1=-1.0, scalar2=None, op0=ALU.mult), reads=['GBt'], writes=['NBt'])
                    P.op('dve', lambda e: e.tensor_tensor(out=BGt[:], in0=be_, in1=E1[:, 0:4], op=ALU.mult), reads=['GBt', e1k], writes=['BGt'])
                    if gdn_cut <= 1:
                        return
                    P.op('dve', lambda e: e.tensor_copy(out=GBC[:], in_=bc_h(g_)), reads=['GBt'], writes=['GBC'])
                    pt, pk = psr.next()
                    for h in range(4):
                        P.op('pe', lambda e, h=h, pt=pt: e.matmul(pt[:, h * 128:(h + 1) * 128], lhsT=GBC[:, h, :], rhs=Mm[d_][:], start=True, stop=False), reads=['GBC', MK[d_]], writes=[pk])
                        P.op('pe', lambda e, h=h, pt=pt: e.matmul(pt[:, h * 128:(h + 1) * 128], lhsT=ident[:], rhs=BGm[d_][:], start=False, stop=True), reads=['ident', BK[d_]], writes=[pk])
                    for h in range(4):
                        P.op('act', lambda e, h=h, pt=pt: e.activation(out=Dm[:, h, :], in_=pt[:, h * 128:(h + 1) * 128], func=AF.Exp, scale=-1.0, bias=SCg[:, h:h + 1]),
                             reads=[pk, 'SCg'], writes=['Dm'])
                    if gdn_cut <= 2:
                        return
                    P.op(p3pool, lambda e: e.tensor_tensor(out=NBN[:], in0=bc_m(noti[:]), in1=bc_h(NBt[:]), op=ALU.mult), reads=['noti', 'NBt'], writes=['NBN'])
                    pkk, pkkk = mm4(kT, kT, [qk_])
                    P.op('dve', lambda e: e.tensor_tensor(out=T1[:], in0=pkk, in1=Dm[:], op=ALU.mult), reads=[pkkk, 'Dm'], writes=['T1'])
                    XY, xyk = XYr.next()
                    P.op('dve', lambda e: e.tensor_tensor(out=XY[:, :, 0:128], in0=T1[:], in1=NBN[:], op=ALU.mult), reads=['T1', 'NBN'], writes=[xyk])
                    if seq == 'lat':
                        pqk, pqkk = mm4(qT, kT, [qk_])
                        P.op('dve', lambda e: e.tensor_tensor(out=ATT[:], in0=pqk, in1=Dm[:], op=ALU.mult), reads=[pqkk, 'Dm'], writes=['ATT'])
                        pa, pak = tr4(lambda h: ATT[:, h, :], ['ATT'])
                        P.op('act', lambda e: e.copy(out=ATTT[:], in_=pa), reads=[pak], writes=[atttk])
                    if gdn_cut <= 3:
                        return
                    py, pyk = tr4(lambda h: XY[:, h, 0:128], [xyk])
                    if gdn_cut <= 3.1:
                        return
                    P.op('dve', lambda e: e.tensor_copy(out=XY[:, :, 128:256], in_=py), reads=[pyk], writes=[xyk])
                    if gdn_cut <= 3.2:
                        return
                    TTm, ttk = TTr.next()
                    P.op('dve', lambda e: e.tensor_tensor(out=TTm[:], in0=py, in1=bc_m(ident[:]), op=ALU.add), reads=[pyk, 'ident'], writes=[ttk])
                    if gdn_cut <= 3.4:
                        return
                    pkt, pktk = tr4(kT, [qk_])
                    P.op('dve', lambda e: e.tensor_tensor(out=KBG[:], in0=pkt, in1=bc_h(BGt[:]), op=ALU.mult), reads=[pktk, 'BGt'], writes=['KBG'])
                    P.op('dve', lambda e: e.tensor_tensor(out=KDEC[:], in0=pkt, in1=bc_h(KDs[:]), op=ALU.mult), reads=[pktk, 'KDs'], writes=[kdeck])
                    if gdn_cut <= 3.6:
                        return
                    pvt, pvtk = tr4(vT, [qk_])
                    P.op('dve', lambda e: e.tensor_tensor(out=VB[:], in0=pvt, in1=bc_h(be_), op=ALU.mult), reads=[pvtk, 'GBt'], writes=['VB'])
                    if gdn_cut <= 4:
                        return
                    for lvl in range(6):
                        XYn, xynk = XYr.next()
                        for half in range(2):
                            pt, pk = psr.next()
                            for hh in range(2):
                                h = half * 2 + hh
                                P.op('pe', lambda e, h=h, hh=hh, pt=pt: e.matmul(pt[:, hh * 256:hh * 256 + 128], lhsT=XY[:, h, 128:256], rhs=XY[:, h, 0:128], start=True, stop=True),
                                     reads=[xyk], writes=[pk])
                                P.op('pe', lambda e, h=h, hh=hh, pt=pt: e.matmul(pt[:, hh * 256 + 128:hh * 256 + 256], lhsT=XY[:, h, 0:128], rhs=XY[:, h, 128:256], start=True, stop=True),
                                     reads=[xyk], writes=[pk])
                            dstv = XYn[:, half * 2:half * 2 + 2, :]
                            srcv = pt[:, :].rearrange("p (h f) -> p h f", h=2)
                            if half == 0:
                                P.op('act', lambda e, dstv=dstv, srcv=srcv: e.copy(out=dstv, in_=srcv), reads=[pk], writes=[xynk])
                            else:
                                P.op('dve', lambda e, dstv=dstv, srcv=srcv: e.tensor_copy(out=dstv, in_=srcv), reads=[pk], writes=[xynk])
                        ptt, pttk = mm4(lambda h: XYn[:, h, 0:128], lambda h: TTm[:, h, :], [xynk, ttk])
                        TTn, ttnk = TTr.next()
                        P.op('dve', lambda e, TTn=TTn, TTm=TTm, ptt=ptt: e.tensor_tensor(out=TTn[:], in0=ptt, in1=TTm[:], op=ALU.add), reads=[pttk, ttk], writes=[ttnk])
                        XY, xyk, TTm, ttk = XYn, xynk, TTn, ttnk
                    if gdn_cut <= 5:
                        return
                    pw, pwk = mm4(lambda h: TTm[:, h, :], lambda h: VB[:, h, :], [ttk, 'VB'])
                    P.op('act', lambda e: e.copy(out=WV[:], in_=pw), reads=[pwk], writes=[wvk])
                    pc, pck = mm4(lambda h: KBG[:, h, :], lambda h: TTm[:, h, :], [ttk, 'KBG'])
                    P.op('dve', lambda e: e.tensor_copy(out=KCT[:], in_=pc), reads=[pck], writes=[kctk])
                    def rec():
                        pa_, pak_ = mm4(lambda h: KCT[:, h, :], lambda h: Sst[:, h, :], [kctk, 'Sst'])
                        P.op('dve', lambda e: e.tensor_tensor(out=VN[:], in0=WV[:], in1=pa_, op=ALU.subtract), reads=[wvk, pak_], writes=['VN'])
                        if seq == 'lat':
                            po1, po1k = mm4(qT, lambda h: Sst[:, h, :], [qk_, 'Sst'])
                            po2, po2k = mm4(lambda h: ATTT[:, h, :], lambda h: VN[:, h, :], [atttk, 'VN'])
                            P.op('dve', lambda e: e.tensor_tensor(out=O1[:], in0=po1, in1=bc_h(E1[:, 0:4]), op=ALU.mult), reads=[po1k, e1k], writes=['O1'])
                            P.op('dve', lambda e: e.tensor_tensor(out=Ot[:], in0=po2, in1=O1[:], op=ALU.add), reads=[po2k, 'O1'], writes=['Ot'])
                        ps_, psk_ = mm4(lambda h: KDEC[:, h, :], lambda h: VN[:, h, :], [kdeck, 'VN'])
                        P.op(p3pool, lambda e: e.tensor_tensor(out=S1[:], in0=Sst[:], in1=bc_h(E1[:, 4:8]), op=ALU.mult), reads=['Sst', e1k], writes=['S1'])
                        P.op('dve', lambda e: e.tensor_tensor(out=Sst[:], in0=ps_, in1=S1[:], op=ALU.add), reads=[psk_, 'S1'], writes=['Sst'])
                        if seq == 'lat' and d_ == 0:
                            P.dma('sp', 'st0', lambda e: e.dma_start(out=OF[j0:j0 + 128, :], in_=Ot[:].rearrange("p h f -> p (h f)")), reads=['Ot'], writes=['OF'], defer=True)
                        if seq == 'lat' and d_ == 1:
                            P.op(p3pool, lambda e: e.tensor_tensor(out=Ot[:], in0=Ot[:], in1=OFt[:], op=ALU.add), reads=['Ot', oftk], writes=['Ot'])
                            for h in range(4):
                                P.op('act', lambda e, h=h: e.activation(out=junk[:, 0:128], in_=Ot[:, h, :], func=AF.Square, accum_out=rsd[:, h:h + 1]), reads=['Ot'], writes=['junk', 'rsd'])
                            P.op('act', lambda e: e.activation(out=rsd[:], in_=rsd[:], func=AF.Sqrt, scale=1.0 / 128, bias=EPS), reads=['rsd'], writes=['rsd'])
                            P.op('dve', lambda e: e.reciprocal(out=rsd[:], in_=rsd[:]), reads=['rsd'], writes=['rsd'])
                            P.op('dve', lambda e: e.tensor_tensor(out=Ot[:], in0=Ot[:], in1=bc_h(rsd[:]), op=ALU.mult), reads=['Ot', 'rsd'], writes=['Ot'])
                            P.op(p3pool, lambda e: e.tensor_tensor(out=Ot[:], in0=Ot[:], in1=bc_m(NGb[:]), op=ALU.mult), reads=['Ot', 'NGb'], writes=['Ot'])
                            P.op('dve', lambda e: e.tensor_tensor(out=Ot[:], in0=Ot[:], in1=ZSt[:], op=ALU.mult), reads=['Ot', zstk], writes=['Ot'])
                            ncol = 128 // ROWS
                            for cc_ in range(ncol):
                                col = cidx * ncol + cc_
                                P.dma('sp', 'st%d' % (cc_ % 2), lambda e, cc_=cc_, col=col: e.dma_start(out=yg_cm[col], in_=Ot[cc_ * ROWS:(cc_ + 1) * ROWS].rearrange("p h f -> p (h f)")),
                                      reads=['Ot'], writes=['YG'], defer=True)
                    return rec

                for d_ in range(2):
                    P.op('pool', lambda e: e.memset(Sst[:].rearrange("p h f -> p (h f)"), 0.0), writes=['Sst'])
                    order = [('ctx', cidx) for cidx in (range(NCC) if d_ == 0 else range(NCC - 1, -1, -1))]
                    order += [('lat', cidx) for cidx in (range(NCL) if d_ == 0 else range(NCL - 1, -1, -1))]
                    prev = gdn_chunk(d_, order[0][0], order[0][1])
                    for k in range(1, len(order)):
                        nxt = gdn_chunk(d_, order[k][0], order[k][1])
                        if prev is not None:
                            prev()
                        prev = nxt
                    if prev is not None:
                        prev()
                P.barrier()

        st_a.close()
        NT = L // 128
        if stop_after >= 5:
            with contextlib.ExitStack() as ph:
                NXR = 2
                xr = Ring(nc, ph, 'xq', [128, D], F32, NXR)
                WO = sb('WO', [128, KD, D], stack=ph)
                WQ = sb('WQ', [128, KD, 2048], stack=ph)
                yrt_r = Ring(nc, ph, 'YRt', [128, 4, 128], F32, 2)
                ygt_r = Ring(nc, ph, 'YGt', [128, GDW], F32, 2)
                YGT = sb('YGT', [128, 4, 128], stack=ph)
                x1r = Ring(nc, ph, 'X1', [128, D], F32, 2)
                h2t_r = Ring(nc, ph, 'h2T', [128, KD, 128], F32, 2)
                qtr = Ring(nc, ph, 'QT', [128, 16, 128], F32, 2)
                stg_r = Ring(nc, ph, 'stg', [128, 1, D], F32, 2)
                stb_r = Ring(nc, ph, 'stb', [128, 1, D], BF16, 2)
                NCH = 256
                uv_v = UV16.rearrange("(p r) d -> p r d", p=128)
                tabs = [(peer_u.rearrange("(p r) d -> p r d", p=128), uv_v[:, :, 0:D]),
                        (peer_v.rearrange("(p r) d -> p r d", p=128), uv_v[:, :, D:2 * D])]

                def convert_chunk(q):
                    src, dst = tabs[q // 128]
                    j = q % 128
                    sg, sgk = stg_r.next()
                    P.dma('sp', 'cl%d' % (stg_r.i % 2), lambda e: e.dma_start(out=sg[:], in_=src[:, j:j + 1, :]), writes=[sgk])
                    P.flush()
                    sbt, sbk = stb_r.next()
                    P.op('pool', lambda e: e.tensor_copy(out=sbt[:], in_=sg[:]), reads=[sgk], writes=[sbk])
                    P.dma('sp', 'cs%d' % (stb_r.i % 2), lambda e: e.dma_start(out=dst[:, j:j + 1, :], in_=sbt[:]), reads=[sbk], writes=['T16'], defer=True)
                P.dma('sp', 'w0', lambda e: e.dma_start(out=WO[:], in_=w_out.rearrange("(k p) n -> p k n", p=128)), writes=['WO'])
                P.dma('sp', 'w1', lambda e: e.dma_start(out=WQ[:], in_=peer_wq.rearrange("(k p) n -> p k n", p=128)), writes=['WQ'])
                ytr_v2 = YTR.rearrange("(c p) t -> p c t", p=128)
                qts_v = QTS.rearrange("(a p) t -> p a t", p=128)
                def p4a_front(i):
                    t0 = i * 128
                    xt, xk = xr.next()
                    P.dma('sp', 'x%d' % (xr.i % NXR), lambda e: e.dma_start(out=xt[:], in_=x[t0:t0 + 128, :]), writes=[xk])
                    YRt, yrtk = yrt_r.next()
                    YGt, ygtk = ygt_r.next()
                    P.dma('sp', 'l%d' % (yrt_r.i % 2), lambda e: e.dma_start(out=YRt[:], in_=ytr_v2[:, :, t0:t0 + 128]), reads=['YTR'], writes=[yrtk])
                    P.dma('sp', 'l%d' % (2 + ygt_r.i % 2), lambda e: e.dma_start(out=YGt[:], in_=YG[t0:t0 + 128, :]), reads=['YG'], writes=[ygtk])
                    P.flush()
                    for q in range(i * NCH // NT, (i + 1) * NCH // NT):
                        convert_chunk(q)
                    pt, pk = psr.next()
                    for c in range(4):
                        P.op('pe', lambda e, c=c, pt=pt: e.transpose(out=pt[:, c * 128:(c + 1) * 128], in_=YGt[:, c * 128:(c + 1) * 128], identity=ident[:]), reads=[ygtk, 'ident'], writes=[pk])
                    P.op('act', lambda e, pt=pt: e.copy(out=YGT[:].rearrange("p c t -> p (c t)"), in_=pt[:]), reads=[pk], writes=['YGT'])
                    X1, x1k = x1r.next()
                    for n in range(2):
                        pt, pk = psr.next()
                        for kc in range(8):
                            lhs = YRt[:, kc, :] if kc < 4 else YGT[:, kc - 4, :]
                            P.op('pe', lambda e, kc=kc, n=n, pt=pt, lhs=lhs: e.matmul(pt[:], lhsT=lhs, rhs=WO[:, kc, n * 512:(n + 1) * 512], start=(kc == 0), stop=(kc == 7)),
                                 reads=[yrtk, 'YGT', 'WO'], writes=[pk])
                        P.op('dve', lambda e, n=n, pt=pt, X1=X1: e.tensor_tensor(out=X1[:, n * 512:(n + 1) * 512], in0=pt[:], in1=MOD['GT1'][:, n * 512:(n + 1) * 512], op=ALU.mult),
                             reads=[pk, 'mod_GT1'], writes=[x1k])
                    P.op('pool', lambda e, xt=xt, X1=X1: e.tensor_tensor(out=X1[:], in0=X1[:], in1=xt[:], op=ALU.add), reads=[x1k, xk], writes=[x1k])
                    P.dma('sp', 'st0', lambda e, X1=X1: e.dma_start(out=X1S[t0:t0 + 128, :], in_=X1[:]), reads=[x1k], writes=['X1S'], defer=True)
                    H2, h2k = norm_mod(X1, x1k, 'G2', 'SH2')
                    P.dma('sp', 'st1', lambda e, H2=H2: e.dma_start(out=H2S[t0:t0 + 128, :], in_=H2[:]), reads=[h2k], writes=['H2S'], defer=True)
                    h2T, h2tk = h2t_r.next()
                    transpose_to(H2, h2k, h2T, h2tk, 0)
                    def qproj():
                        QT, qtk = qtr.next()
                        for qb in range(4):
                            pt, pk = psr.next()
                            for j in range(4):
                                hh = qb * 4 + j
                                for kc in range(KD):
                                    P.op('pe', lambda e, j=j, hh=hh, kc=kc, pt=pt: e.matmul(pt[:, j * 128:(j + 1) * 128], lhsT=WQ[:, kc, hh * 128:(hh + 1) * 128], rhs=h2T[:, kc, :],
                                                                                           start=(kc == 0), stop=(kc == KD - 1)), reads=['WQ', h2tk], writes=[pk])
                            if qb % 2 == 0:
                                P.op('act', lambda e, qb=qb, pt=pt, QT=QT: e.copy(out=QT[:, qb * 4:qb * 4 + 4, :].rearrange("p a t -> p (a t)"), in_=pt[:]), reads=[pk], writes=[qtk])
                            else:
                                P.op('dve', lambda e, qb=qb, pt=pt, QT=QT: e.tensor_copy(out=QT[:, qb * 4:qb * 4 + 4, :].rearrange("p a t -> p (a t)"), in_=pt[:]), reads=[pk], writes=[qtk])
                        P.dma('sp', 'st2', lambda e, QT=QT: e.dma_start(out=qts_v[:, :, t0:t0 + 128], in_=QT[:]), reads=[qtk], writes=['QTS'], defer=True)
                    return qproj

                fq = p4a_front(0)
                for i in range(NT):
                    nfq = p4a_front(i + 1) if i + 1 < NT else None
                    fq()
                    fq = nfq
                P.barrier()

            st_b.close()
            with contextlib.ExitStack() as ph:
                NXR = 2
                xr = Ring(nc, ph, 'xw', [128, D], F32, NXR)
                accA, accB = psr.items[6][0], psr.items[7][0]
                psr.items = psr.items[:6]
                NUB = cfg.get('nub', 27)
                KEY = sb('KEY', [128, 2, 128], stack=ph)
                KEYT = sb('KEYT', [128, 2, 128], stack=ph)
                QT = sb('QTt', [128, 16, 128], stack=ph)
                SC = sb('SC', [128, 16, 128], stack=ph)
                SCB = sb('SCB', [128, 16, 128], stack=ph)
                CAND = SC[:].rearrange("p a t -> p (a t)").rearrange("p (h c) -> p h c", h=8)
                OH = QT[:].rearrange("p a t -> p (a t)").rearrange("p (k j) -> p k j", j=16)
                TOPV = sb('TOPV', [128, 16, 16], stack=ph)
                TOPI = sb('TOPI', [128, 16, 16], U32, stack=ph)
                TOPIF = sb('TOPIF', [128, 16, 16], stack=ph)
                CANDB = sb('CANDB', [128, 8, 256], stack=ph)
                TS = sb('TS', [128, 8, 16], stack=ph)
                POS = sb('POS', [128, 8, 16], U32, stack=ph)
                PAB = sb('PAB', [128, 2, 128], U32, stack=ph)
                PABF = sb('PABF', [128, 2, 128], stack=ph)
                IOT = sb('IOT', [128, 16], stack=ph)
                ISEL = sb('ISEL', [128, 2, 128], stack=ph)
                IDXF = sb('IDXF', [128, 128], stack=ph)
                idx_r = Ring(nc, ph, 'IDX', [128, 128], U32, 2)
                gate_r = Ring(nc, ph, 'GATE', [128, 8, 16], F32, 2)
                gsum = sb('gsum', [128, 8], stack=ph)
                DOT = sb('DOT', [128, 128], stack=ph)
                DOTG = sb('DOTG', [128, 128], stack=ph)
                jk_r = Ring(nc, ph, 'jk', [128, D], BF16, 4)
                WGT = sb('WGT', [128, 128], stack=ph)
                GTMP = sb('GTMP', [128, 1, 128], stack=ph)
                FIN = sb('FIN', [128, D], stack=ph)
                ub_r = Ring(nc, ph, 'UB', [128, 2 * D], BF16, NUB)
                GT_ = sb('GT_', [128, 128], stack=ph)
                dg_r = Ring(nc, ph, 'DG', [128, 128], BF16, 8)
                fin = sb('fin', [128, 2], stack=ph)
                P.dma('sp', 'c0', lambda e: e.dma_start(out=KEY[:], in_=peer_keys.rearrange("x k d -> k x d")), writes=['KEY'])
                pt, pk = psr.next()
                for x_ in range(2):
                    P.op('pe', lambda e, x_=x_, pt=pt: e.transpose(out=pt[:, x_ * 128:(x_ + 1) * 128], in_=KEY[:, x_, :], identity=ident[:]), reads=['KEY', 'ident'], writes=[pk])
                P.op('dve', lambda e: e.tensor_copy(out=KEYT[:].rearrange("p x k -> p (x k)"), in_=pt[:, 0:256]), reads=[pk], writes=['KEYT'])
                P.op('pool', lambda e: e.iota(IOT[:], pattern=[[1, 16]], base=0, channel_multiplier=0, allow_small_or_imprecise_dtypes=True), writes=['IOT'])
                qts_v = QTS.rearrange("(a p) t -> p a t", p=128)

                def top16_multi(n, vals_fn, vkey, scr_fn, skey, outv_fn, outi_fn, okeys):
                    for j in range(n):
                        P.op('dve', lambda e, j=j: e.max(out=outv_fn(j)[:, 0:8], in_=vals_fn(j)), reads=[vkey], writes=['%s%d' % (okeys[0], j)])
                    for j in range(n):
                        P.op('dve', lambda e, j=j: e.max_index(out=outi_fn(j)[:, 0:8], in_max=outv_fn(j)[:, 0:8], in_values=vals_fn(j)), reads=[vkey, '%s%d' % (okeys[0], j)], writes=['%s%d' % (okeys[1], j)])
                    for j in range(n):
                        P.op('dve', lambda e, j=j: e.match_replace(out=scr_fn(j), in_to_replace=outv_fn(j)[:, 0:8], in_values=vals_fn(j), imm_value=NEG), reads=[vkey, '%s%d' % (okeys[0], j)], writes=['%s%d' % (skey, j)])
                    for j in range(n):
                        P.op('dve', lambda e, j=j: e.max(out=outv_fn(j)[:, 8:16], in_=scr_fn(j)), reads=['%s%d' % (skey, j)], writes=['%s%d' % (okeys[0], j)])
                    for j in range(n):
                        P.op('dve', lambda e, j=j: e.max_index(out=outi_fn(j)[:, 8:16], in_max=outv_fn(j)[:, 8:16], in_values=scr_fn(j)), reads=['%s%d' % (skey, j), '%s%d' % (okeys[0], j)], writes=['%s%d' % (okeys[1], j)])

                def prep(i):
                    t0 = i * 128
                    X1t, x1k = xr.next()
                    P.dma('sp', 'x%d' % (xr.i % NXR), lambda e: e.dma_start(out=X1t[:], in_=X1S[t0:t0 + 128, :]), reads=['X1S'], writes=[x1k])
                    H2t, h2k = hr.next()
                    P.dma('sp', 'l%d' % (hr.i % 2), lambda e: e.dma_start(out=H2t[:], in_=H2S[t0:t0 + 128, :]), reads=['H2S'], writes=[h2k])
                    P.dma('sp', 'l2', lambda e: e.dma_start(out=QT[:], in_=qts_v[:, :, t0:t0 + 128]), reads=['QTS'], writes=['QT'])
                    P.flush()
                    for qb in range(4):
                        pt, pk = psr.next()
                        for j in range(4):
                            hh = qb * 4 + j
                            P.op('pe', lambda e, j=j, hh=hh, pt=pt: e.matmul(pt[:, j * 128:(j + 1) * 128], lhsT=QT[:, hh, :], rhs=KEYT[:, hh % 2, :], start=True, stop=True),
                                 reads=['QT', 'KEYT'], writes=[pk])
                        P.op('act', lambda e, qb=qb, pt=pt: e.copy(out=SC[:, qb * 4:qb * 4 + 4, :].rearrange("p a t -> p (a t)"), in_=pt[:]), reads=[pk], writes=['SC'])
                    top16_multi(16, lambda j: SC[:, j, :], 'SC', lambda j: SCB[:, j, :], 'SCB', lambda j: TOPV[:, j, :], lambda j: TOPI[:, j, :], ('TOPV', 'TOPI'))
                    tvk = ['TOPV%d' % j for j in range(16)]
                    tik = ['TOPI%d' % j for j in range(16)]
                    P.op('dve', lambda e: e.tensor_copy(out=TOPIF[:], in_=TOPI[:]), reads=tik, writes=['TOPIF'])
                    tv4 = TOPV[:].rearrange("p (h x) k -> p h x k", x=2)
                    ti4 = TOPIF[:].rearrange("p (h x) k -> p h x k", x=2)
                    P.op('dve', lambda e: e.tensor_tensor(out=CAND.rearrange("p h (a b) -> p h a b", a=16),
                                                          in0=tv4[:, :, 0, :].unsqueeze(3).broadcast_to([128, 8, 16, 16]),
                                                          in1=tv4[:, :, 1, :].unsqueeze(2).broadcast_to([128, 8, 16, 16]), op=ALU.add), reads=tvk + ['SCB%d' % j for j in range(16)], writes=['SC'])
                    top16_multi(8, lambda j: CAND[:, j, :], 'SC', lambda j: CANDB[:, j, :], 'CANDB', lambda j: TS[:, j, :], lambda j: POS[:, j, :], ('TS', 'POS'))
                    tsk = ['TS%d' % j for j in range(8)]
                    posk = ['POS%d' % j for j in range(8)]
                    posf = POS[:].rearrange("p h k -> p (h k)")
                    P.op('dve', lambda e: e.tensor_single_scalar(out=PAB[:, 0, :], in_=posf, scalar=4, op=ALU.arith_shift_right), reads=posk, writes=['PAB'])
                    P.op('dve', lambda e: e.tensor_single_scalar(out=PAB[:, 1, :], in_=posf, scalar=15, op=ALU.bitwise_and), reads=posk, writes=['PAB'])
                    P.op('dve', lambda e: e.tensor_copy(out=PABF[:], in_=PAB[:]), reads=['PAB'], writes=['PABF'])
                    for x_ in range(2):
                        P.op('dve', lambda e, x_=x_: e.tensor_tensor(out=OH, in0=PABF[:, x_, :].unsqueeze(2).broadcast_to([128, 128, 16]),
                                                                      in1=IOT[:].unsqueeze(1).broadcast_to([128, 128, 16]), op=ALU.is_equal), reads=['PABF', 'IOT'], writes=['QT'])
                        oh4 = OH.rearrange("p (h k) j -> p h k j", h=8)
                        P.op('dve', lambda e, x_=x_, oh4=oh4: e.tensor_tensor(out=oh4, in0=oh4, in1=ti4[:, :, x_, :].unsqueeze(2).broadcast_to([128, 8, 16, 16]), op=ALU.mult),
                             reads=['QT', 'TOPIF'], writes=['QT'])
                        P.op('dve', lambda e, x_=x_: e.tensor_reduce(out=ISEL[:, x_, :], in_=OH, axis=AX.X, op=ALU.add), reads=['QT'], writes=['ISEL'])
                    IDX, idxk = idx_r.next()
                    GATE, gatek = gate_r.next()
                    P.op('dve', lambda e: e.scalar_tensor_tensor(out=IDXF[:], in0=ISEL[:, 0, :], scalar=128.0, in1=ISEL[:, 1, :], op0=ALU.mult, op1=ALU.add), reads=['ISEL'], writes=['IDXF'])
                    P.op('dve', lambda e: e.tensor_copy(out=IDX[:], in_=IDXF[:]), reads=['IDXF'], writes=[idxk])
                    P.op('dve', lambda e: e.tensor_tensor(out=GATE[:], in0=TS[:], in1=TS[:, :, 0:1].broadcast_to([128, 8, 16]), op=ALU.subtract), reads=tsk, writes=[gatek])
                    P.op('act', lambda e: e.activation(out=GATE[:], in_=GATE[:], func=AF.Exp), reads=[gatek], writes=[gatek])
                    P.op('dve', lambda e: e.tensor_reduce(out=gsum[:], in_=GATE[:], axis=AX.X, op=ALU.add), reads=[gatek], writes=['gsum'])
                    P.op('dve', lambda e: e.reciprocal(out=gsum[:], in_=gsum[:]), reads=['gsum'], writes=['gsum'])
                    P.op('dve', lambda e: e.tensor_tensor(out=GATE[:], in0=GATE[:], in1=gsum[:].unsqueeze(2).broadcast_to([128, 8, 16]), op=ALU.mult), reads=[gatek, 'gsum'], writes=[gatek])
                    return dict(X1t=X1t, x1k=x1k, H2t=H2t, h2k=h2k, IDX=IDX, idxk=idxk, GATE=GATE, gatek=gatek, t0=t0)

                def gather(tbl, c, hk):
                    ub, ubk = ub_r.next()
                    P.dma('pool', 'g%d' % (ub_r.i % NUB), lambda e: e.indirect_dma_start(
                        out=ub[:], out_offset=None, in_=tbl[:, :], in_offset=bass.IndirectOffsetOnAxis(ap=c['IDX'][:, hk:hk + 1], axis=0)),
                        reads=[c['idxk'], 'T16'], writes=[ubk])
                    return ub, ubk

                def uv_phase(c, mid_hook=None):
                    gflat = c['GATE'][:].rearrange("p h k -> p (h k)")
                    for g in range(16):
                        ubs = []
                        for j in range(8):
                            hk = g * 8 + j
                            ub, ubk = gather(UV16, c, hk)
                            jk, jkk = jk_r.next()
                            P.op('dve', lambda e, hk=hk, ub=ub, jk=jk: e.scalar_tensor_tensor(out=jk[:], in0=ub[:, 0:D], scalar=1.0, in1=c['H2t'][:], op0=ALU.mult, op1=ALU.mult,
                                                                                             accum_out=DOT[:, hk:hk + 1]), reads=[ubk, c['h2k']], writes=[jkk, 'DOT%d' % hk])
                            ubs.append((ub, ubk))
                        c0_, c1_ = g * 8, (g + 1) * 8
                        dks = ['DOT%d' % hk for hk in range(c0_, c1_)]
                        gk = 'GT%d' % g
                        t = GT_[:, c0_:c1_]
                        v = DOT[:, c0_:c1_]
                        P.op('act', lambda e, t=t, v=v: e.activation(out=t, in_=v, func=AF.Gelu_apprx_tanh), reads=dks, writes=[gk])
                        wk = 'WG%d' % g
                        P.op('dve', lambda e, t=t, c0_=c0_, c1_=c1_: e.tensor_tensor(out=WGT[:, c0_:c1_], in0=t, in1=gflat[:, c0_:c1_], op=ALU.mult), reads=[gk, c['gatek']], writes=[wk])
                        for j, (ub, ubk) in enumerate(ubs):
                            hk = g * 8 + j
                            dg, dgk = dg_r.next()
                            P.op('act', lambda e, hk=hk, dg=dg: e.activation(out=dg[:], in_=ident[:], func=AF.Copy, scale=WGT[:, hk:hk + 1]), reads=['ident', wk], writes=[dgk])
                            for half, (acc, ak) in enumerate(((accA, 'accA'), (accB, 'accB'))):
                                P.op('pe', lambda e, hk=hk, dg=dg, ub=ub, half=half, acc=acc: e.matmul(acc[:], lhsT=dg[:], rhs=ub[:, D + half * 512:D + (half + 1) * 512],
                                                                                                     start=(hk == 0), stop=(hk == 127)), reads=[dgk, ubk], writes=[ak])
                        if g == 3 and mid_hook is not None:
                            mid_hook()

                def fin_phase(c):
                    for half, (acc, ak) in enumerate(((accA, 'accA'), (accB, 'accB'))):
                        P.op('dve', lambda e, half=half, acc=acc: e.tensor_tensor(out=FIN[:, half * 512:(half + 1) * 512], in0=acc[:], in1=MOD['GT2'][:, half * 512:(half + 1) * 512], op=ALU.mult),
                             reads=[ak, 'mod_GT2'], writes=['FIN'])
                    P.op('dve', lambda e: e.tensor_tensor(out=FIN[:], in0=FIN[:], in1=c['X1t'][:], op=ALU.add), reads=['FIN', c['x1k']], writes=['FIN'])
                    rms_rstd(FIN[:], 'FIN', D, fin[:, 0:1], 'fin')
                    P.op('dve', lambda e: e.scalar_tensor_tensor(out=FIN[:], in0=FIN[:], scalar=fin[:, 0:1], in1=FG[:], op0=ALU.mult, op1=ALU.mult), reads=['FIN', 'fin', 'FG'], writes=['FIN'])
                    t0 = c['t0']
                    P.dma('sp', 'o0', lambda e: e.dma_start(out=out[t0:t0 + 128, :], in_=FIN[:]), reads=['FIN'], writes=['OUT'], defer=True)

                cur = prep(0)
                for i in range(NT):
                    box = {}

                    def hook(i=i, box=box):
                        box['n'] = prep(i + 1) if i + 1 < NT else None
                    uv_phase(cur, hook)
                    fin_phase(cur)
                    cur = box.get('n')
                P.barrier()
        if stop_after < 5:
            st_b.close()
        P.finish()
        nc._prog_nins = P.nins
    return nc


FULL_CFG = dict(rows=64, ctx=256, grp=512)
_NC_CACHE = {}


def make_in_maps(inputs, nb):
    f = lambda a: np.ascontiguousarray(np.asarray(a, dtype=np.float32))
    maps = []
    for b in range(nb):
        m = {
            "x": f(inputs['x'][b]), "ctxx": f(inputs['ctx'][b]),
            "cc": f(np.stack([np.asarray(inputs['c'][b]), np.asarray(inputs['c_ctx'])], 0)),
            "w_mod": f(inputs['w_mod'][0]), "b_mod": f(inputs['b_mod'][0][None, :]),
            "norm1_g": f(inputs['norm1_g'][0][None, :]), "norm2_g": f(inputs['norm2_g'][0][None, :]),
            "w_in": f(inputs['w_in'][0]), "rg_conv_w": f(inputs['rg_conv_w'][0]), "rg_conv_b": f(inputs['rg_conv_b'][0][None, :]),
            "rg_gate_w": f(inputs['rg_gate_w'][0]), "rg_gate_b": f(np.asarray(inputs['rg_gate_b'][0]).reshape(4, RGW)),
            "rg_lambda": f(inputs['rg_lambda'][0]), "gdn_conv_w": f(inputs['gdn_conv_w'][0]),
            "gdn_a_log": f(np.asarray(inputs['gdn_a_log'][0]).reshape(1, 8)), "gdn_dt_bias": f(np.asarray(inputs['gdn_dt_bias'][0]).reshape(1, 8)),
            "gdn_norm_g": f(inputs['gdn_norm_g'][0][None, :]), "w_out": f(inputs['w_out'][0]), "peer_wq": f(inputs['peer_wq'][0]),
            "peer_keys": f(inputs['peer_keys'][0]), "peer_u": f(inputs['peer_u'][0]), "peer_v": f(inputs['peer_v'][0]),
            "final_g": f(np.asarray(inputs['final_g'])[None, :]),
        }
        maps.append(m)
    return maps


def kernel(**inputs):
    nb = 8
    if 'full' not in _NC_CACHE:
        _NC_CACHE['full'] = build_nc(FULL_CFG)
    nc = _NC_CACHE['full']
    in_maps = make_in_maps(inputs, nb)
    res = run_bass_kernel_spmd(nc, in_maps, core_ids=list(range(nb)))
    return np.stack([np.asarray(r["out"], dtype=np.float32) for r in res.results], axis=0)
```
